# Optimizing a Trainium2 kernel written in Bass

```python
import jax, jax.numpy as jnp
from jax import lax
import numpy as np

D_MODEL = 2048
BATCH = 4
SEQ = 4096
DEPTH = 1
DEC_BATCH = 8
DEC_SEQ = 64
PAST_LEN = 2048

CHUNK = 64
N_PAST_CHUNKS = 8
BAND_PAST = N_PAST_CHUNKS * CHUNK
BAND_LEN = BAND_PAST + CHUNK
BAND_HEADS = 8
BAND_HEAD_DIM = 128
BAND_WIDTH = BAND_HEADS * BAND_HEAD_DIM
REL_MAX = 256
REL_SIZE = (CHUNK - 1) + REL_MAX + 1
BAND_SCALE = BAND_HEAD_DIM ** -0.5
MLA_HEADS = 8
MLA_NOPE = 128
MLA_ROPE = 64
MLA_QK = MLA_NOPE + MLA_ROPE
MLA_V = 128
MLA_WIDTH = MLA_HEADS * MLA_V
MLA_KV_RANK = 512
MLA_SCALE = MLA_QK ** -0.5
ROPE_THETA = 10000.0
D_FF = 4 * D_MODEL
Q_BLOCK = 128
EPS = 1e-6
_IN_WIDTHS = (BAND_WIDTH, BAND_WIDTH, BAND_WIDTH, MLA_HEADS * MLA_QK, MLA_KV_RANK, MLA_ROPE, D_MODEL, D_MODEL)
D_IN_PROJ = sum(_IN_WIDTHS)
SPLIT_POINTS = tuple(sum(_IN_WIDTHS[:i + 1]) for i in range(len(_IN_WIDTHS) - 1))

kernel_name = 'streaming_band_mla_hybrid_step'


def rms_norm(x, g):
    xf = x.astype(jnp.float32)
    y = xf * lax.rsqrt(jnp.mean(xf * xf, axis=-1, keepdims=True) + EPS)
    return (y * g.astype(jnp.float32)).astype(x.dtype)


def rope(x, pos):
    half = x.shape[-1] // 2
    freqs = ROPE_THETA ** (-(jnp.arange(half, dtype=jnp.float32) / half))
    ang = pos.astype(jnp.float32)[:, None] * freqs[None, :]
    shape = (1, x.shape[1]) + (1,) * (x.ndim - 3) + (half,)
    cos = jnp.cos(ang).reshape(shape)
    sin = jnp.sin(ang).reshape(shape)
    xf = x.astype(jnp.float32)
    x1, x2 = xf[..., :half], xf[..., half:]
    return jnp.concatenate([x1 * cos - x2 * sin, x2 * cos + x1 * sin], axis=-1).astype(x.dtype)


def band_bias(n_q, n_past, n_k, table):
    dist = n_past + jnp.arange(n_q)[:, None] - jnp.arange(n_k)[None, :]
    idx = jnp.clip(dist, -(CHUNK - 1), REL_MAX) + (CHUNK - 1)
    return table[:, idx].astype(jnp.float32)


def mixer_projections(x, pos, norm_g, w_in, g_aq, g_ak, g_kv, g_kr, g_qn, g_qr):
    b, s = x.shape[0], x.shape[1]
    xn = rms_norm(x, norm_g)
    h = xn @ w_in
    aq, ak, av, bq, ckv, kr, ga, gb = jnp.split(h, SPLIT_POINTS, axis=-1)
    aq = rms_norm(aq.reshape(b, s, BAND_HEADS, BAND_HEAD_DIM), g_aq)
    ak = rms_norm(ak.reshape(b, s, BAND_HEADS, BAND_HEAD_DIM), g_ak)
    av = av.reshape(b, s, BAND_HEADS, BAND_HEAD_DIM)
    bq = bq.reshape(b, s, MLA_HEADS, MLA_QK)
    qn = rms_norm(bq[..., :MLA_NOPE], g_qn)
    qr = rope(rms_norm(bq[..., MLA_NOPE:], g_qr), pos)
    ckv = rms_norm(ckv, g_kv)
    kr = rope(rms_norm(kr, g_kr), pos)
    return aq, ak, av, qn, qr, ckv, kr, ga, gb


def band_attention_prompt(q, k, v, table):
    b, s = q.shape[0], q.shape[1]
    nc = s // CHUNK
    pad = jnp.zeros((b, BAND_PAST, BAND_HEADS, BAND_HEAD_DIM), k.dtype)
    kp = jnp.concatenate([pad, k], axis=1)
    vp = jnp.concatenate([pad, v], axis=1)
    valid = jnp.arange(BAND_PAST + s) >= BAND_PAST
    bias = band_bias(CHUNK, BAND_PAST, BAND_LEN, table)
    qc = q.reshape(b, nc, CHUNK, BAND_HEADS, BAND_HEAD_DIM).swapaxes(0, 1)

    def step(args):
        c, qb = args
        start = c * CHUNK
        kb = lax.dynamic_slice_in_dim(kp, start, BAND_LEN, axis=1)
        vb = lax.dynamic_slice_in_dim(vp, start, BAND_LEN, axis=1)
        mb = lax.dynamic_slice_in_dim(valid, start, BAND_LEN, axis=0)
        sc = jnp.einsum('bqhd,bkhd->bhqk', qb, kb).astype(jnp.float32) * BAND_SCALE + bias[None]
        sc = jnp.where(mb[None, None, None, :], sc, -jnp.inf)
        p = jax.nn.softmax(sc, axis=-1).astype(vb.dtype)
        return jnp.einsum('bhqk,bkhd->bqhd', p, vb)

    out = lax.map(step, (jnp.arange(nc), qc))
    return out.swapaxes(0, 1).reshape(b, s, BAND_WIDTH)


def band_attention_sample(q, k_new, v_new, k_cache, v_cache, table):
    b, t = q.shape[0], q.shape[1]
    n_past = k_cache.shape[1]
    kb = jnp.concatenate([k_cache.astype(k_new.dtype), k_new], axis=1)
    vb = jnp.concatenate([v_cache.astype(v_new.dtype), v_new], axis=1)
    bias = band_bias(t, n_past, n_past + t, table)
    sc = jnp.einsum('bqhd,bkhd->bhqk', q, kb).astype(jnp.float32) * BAND_SCALE + bias[None]
    p = jax.nn.softmax(sc, axis=-1).astype(vb.dtype)
    return jnp.einsum('bhqk,bkhd->bqhd', p, vb).reshape(b, t, BAND_WIDTH)


def mla_expand(ckv, w_kv_b, g_kn):
    b, l = ckv.shape[0], ckv.shape[1]
    kv = (ckv @ w_kv_b).reshape(b, l, MLA_HEADS, MLA_NOPE + MLA_V)
    kn = rms_norm(kv[..., :MLA_NOPE], g_kn)
    return kn, kv[..., MLA_NOPE:]


def mla_block(qn, qr, qpos, kn, kr, v, kpos):
    sc = (jnp.einsum('bqhd,bkhd->bhqk', qn, kn) + jnp.einsum('bqhr,bkr->bhqk', qr, kr)).astype(jnp.float32) * MLA_SCALE
    mask = (kpos[None, :] // CHUNK) <= (qpos[:, None] // CHUNK)
    sc = jnp.where(mask[None, None], sc, -jnp.inf)
    p = jax.nn.softmax(sc, axis=-1).astype(v.dtype)
    return jnp.einsum('bhqk,bkhd->bqhd', p, v)


def mla_attention_prompt(qn, qr, kn, kr, v, pos):
    b, s = qn.shape[0], qn.shape[1]
    nb = s // Q_BLOCK

    def to_blocks(t):
        return t.reshape((b, nb, Q_BLOCK) + t.shape[2:]).swapaxes(0, 1)

    out = lax.map(lambda a: mla_block(a[0], a[1], a[2], kn, kr, v, pos),
                  (to_blocks(qn), to_blocks(qr), pos.reshape(nb, Q_BLOCK)))
    return out.swapaxes(0, 1).reshape(b, s, MLA_WIDTH)


def merge_and_ffn(x, oa, ob, ga, gb, w_pa, w_pb, w_out, norm_ffn_g, w_up, w_down):
    mix = jax.nn.sigmoid(ga) * (oa @ w_pa) + jax.nn.sigmoid(gb) * (ob @ w_pb)
    h = x + mix @ w_out
    u = jnp.square(jax.nn.relu(rms_norm(h, norm_ffn_g) @ w_up))
    return h + u @ w_down


def setup_inputs(seed: int = 0) -> dict:
    key = jax.random.key(seed)
    ks = jax.random.split(key, 24)
    f32 = jnp.float32

    def nrm(k, shape, scale):
        return jax.random.normal(k, shape, f32) * scale

    def gain(k, n):
        return 1.0 + 0.01 * jax.random.normal(k, (DEPTH, n), f32)

    a_cache_len = min(BAND_PAST, PAST_LEN)
    return {
        'x_prompt': nrm(ks[0], (BATCH, SEQ, D_MODEL), 1.0),
        'x_sample': nrm(ks[1], (DEC_BATCH, DEC_SEQ, D_MODEL), 1.0),
        'cache_a_k': nrm(ks[2], (DEPTH, DEC_BATCH, a_cache_len, BAND_HEADS, BAND_HEAD_DIM), 1.0),
        'cache_a_v': nrm(ks[3], (DEPTH, DEC_BATCH, a_cache_len, BAND_HEADS, BAND_HEAD_DIM), 1.0),
        'cache_mla_ckv': nrm(ks[4], (DEPTH, DEC_BATCH, PAST_LEN, MLA_KV_RANK), 1.0),
        'cache_mla_krope': nrm(ks[5], (DEPTH, DEC_BATCH, PAST_LEN, MLA_ROPE), 1.0),
        'norm_mix_g': gain(ks[6], D_MODEL),
        'w_in': nrm(ks[7], (DEPTH, D_MODEL, D_IN_PROJ), D_MODEL ** -0.5),
        'g_aq': gain(ks[8], BAND_HEAD_DIM),
        'g_ak': gain(ks[9], BAND_HEAD_DIM),
        'rel_bias': nrm(ks[10], (DEPTH, BAND_HEADS, REL_SIZE), 0.5),
        'g_kv': gain(ks[11], MLA_KV_RANK),
        'g_kr': gain(ks[12], MLA_ROPE),
        'g_qn': gain(ks[13], MLA_NOPE),
        'g_qr': gain(ks[14], MLA_ROPE),
        'g_kn': gain(ks[15], MLA_NOPE),
        'w_kv_b': nrm(ks[16], (DEPTH, MLA_KV_RANK, MLA_HEADS * (MLA_NOPE + MLA_V)), MLA_KV_RANK ** -0.5),
        'w_pa': nrm(ks[17], (DEPTH, BAND_WIDTH, D_MODEL), BAND_WIDTH ** -0.5),
        'w_pb': nrm(ks[18], (DEPTH, MLA_WIDTH, D_MODEL), MLA_WIDTH ** -0.5),
        'w_out': nrm(ks[19], (DEPTH, D_MODEL, D_MODEL), D_MODEL ** -0.5),
        'norm_ffn_g': gain(ks[20], D_MODEL),
        'w_up': nrm(ks[21], (DEPTH, D_MODEL, D_FF), D_MODEL ** -0.5),
        'w_down': nrm(ks[22], (DEPTH, D_FF, D_MODEL), D_FF ** -0.5),
    }


def reference(x_prompt, x_sample, cache_a_k, cache_a_v, cache_mla_ckv, cache_mla_krope,
              norm_mix_g, w_in, g_aq, g_ak, rel_bias, g_kv, g_kr, g_qn, g_qr, g_kn,
              w_kv_b, w_pa, w_pb, w_out, norm_ffn_g, w_up, w_down):
    s = x_prompt.shape[1]
    t = x_sample.shape[1]
    past = cache_mla_ckv.shape[2]
    keep = min(BAND_PAST, s)
    pos_p = jnp.arange(s)
    pos_s = past + jnp.arange(t)
    kpos_s = jnp.arange(past + t)
    yp, ys = x_prompt, x_sample
    akp, avp, ckp, krp, aks, avs, cks, krs = [], [], [], [], [], [], [], []
    for l in range(DEPTH):
        aq, ak, av, qn, qr, ckv, kr, ga, gb = mixer_projections(
            yp, pos_p, norm_mix_g[l], w_in[l], g_aq[l], g_ak[l], g_kv[l], g_kr[l], g_qn[l], g_qr[l])
        oa = band_attention_prompt(aq, ak, av, rel_bias[l])
        kn, vb = mla_expand(ckv, w_kv_b[l], g_kn[l])
        ob = mla_attention_prompt(qn, qr, kn, kr, vb, pos_p)
        akp.append(ak[:, s - keep:])
        avp.append(av[:, s - keep:])
        ckp.append(ckv)
        krp.append(kr)
        yp = merge_and_ffn(yp, oa, ob, ga, gb, w_pa[l], w_pb[l], w_out[l], norm_ffn_g[l], w_up[l], w_down[l])
        aq, ak, av, qn, qr, ckv, kr, ga, gb = mixer_projections(
            ys, pos_s, norm_mix_g[l], w_in[l], g_aq[l], g_ak[l], g_kv[l], g_kr[l], g_qn[l], g_qr[l])
        oa = band_attention_sample(aq, ak, av, cache_a_k[l], cache_a_v[l], rel_bias[l])
        ckv_all = jnp.concatenate([cache_mla_ckv[l].astype(ckv.dtype), ckv], axis=1)
        kr_all = jnp.concatenate([cache_mla_krope[l].astype(kr.dtype), kr], axis=1)
        kn, vb = mla_expand(ckv_all, w_kv_b[l], g_kn[l])
        ob = mla_block(qn, qr, pos_s, kn, kr_all, vb, kpos_s).reshape(ys.shape[0], t, MLA_WIDTH)
        aks.append(ak)
        avs.append(av)
        cks.append(ckv)
        krs.append(kr)
        ys = merge_and_ffn(ys, oa, ob, ga, gb, w_pa[l], w_pb[l], w_out[l], norm_ffn_g[l], w_up[l], w_down[l])
    return (yp, ys, jnp.stack(akp), jnp.stack(avp), jnp.stack(ckp), jnp.stack(krp),
            jnp.stack(aks), jnp.stack(avs), jnp.stack(cks), jnp.stack(krs))
```

```python
import contextlib
import numpy as np
import concourse.bass as bass
import concourse.mybir as mybir
from concourse.bass_utils import run_bass_kernel_spmd

F32 = mybir.dt.float32
BF16 = mybir.dt.bfloat16
AF = mybir.ActivationFunctionType
ALU = mybir.AluOpType
AX = mybir.AxisListType

D = 2048
DIN = 9280
NGO = 17
TOWN = NGO * 128
NPRE = 2048
EPS = 1e-6
BAND_SCALE = 128 ** -0.5
MLA_SCALE = 192 ** -0.5
NEG = -30000.0
NKEY_M = 6272
NKEY_B = 3200


class Buf:
    def __init__(self, name, t=None):
        self.name = name
        self.t = t
        self.w = {}
        self.r = {}
        self.dkey = None
        self.dcnt = 0

    def __getitem__(self, idx):
        return self.t[idx]


class KB:
    LIMIT = 20000

    def __init__(self, nc, es):
        self.nc = nc
        self.es = es
        self.eng = dict(pe=nc.tensor, act=nc.scalar, dve=nc.vector, pool=nc.gpsimd, sp=nc.sync)
        self.sems = {}
        self.final = {}
        self.ekey = {}
        self.ecnt = {}
        self.nsem = 0
        for e in self.eng:
            self._roll(e)
        self.waited = {e: {} for e in self.eng}
        self.dbufs = []
        self.pe_pending = False

    def _newsem(self, name):
        h = self.es.enter_context(self.nc.semaphore(name))
        self.sems[name] = h
        self.nsem += 1
        return name

    def _roll(self, e):
        if e in self.ekey:
            self.final[self.ekey[e]] = self.ecnt[e]
        self.ekey[e] = self._newsem(f"e_{e}_{self.nsem}")
        self.ecnt[e] = 0

    def _wait(self, e, need):
        wd = self.waited[e]
        for key, val in need.items():
            if val <= 0:
                continue
            if key == self.ekey['pe'] and val > self.ecnt['pe']:
                raise RuntimeError("wait on unmarked PE op")
            if wd.get(key, 0) < val:
                self.eng[e].wait_ge(self.sems[key], val)
                wd[key] = val

    @staticmethod
    def _need(R, W, A):
        need = {}

        def upd(d):
            for k, v in d.items():
                if need.get(k, 0) < v:
                    need[k] = v
        for b in R:
            upd(b.w)
        for b in W:
            upd(b.w)
            upd(b.r)
        for b in A:
            upd(b.r)
        return need

    def _record(self, key, val, R, W, A):
        for b in R:
            if b.r.get(key, 0) < val:
                b.r[key] = val
        for b in W:
            b.w = {key: val}
            b.r = {}
        for b in A:
            if b.w.get(key, 0) < val:
                b.w[key] = val
            b.r = {}

    def op(self, e, fn, R=(), W=(), A=(), mark=True):
        if not (e == 'pe' and self.pe_pending):
            if self.ecnt[e] >= self.LIMIT:
                self._roll(e)
        need = self._need(R, W, A)
        if e == 'pe':
            need = {k: v for k, v in need.items() if not k.startswith('e_pe_')}
        self._wait(e, need)
        ins = fn()
        key = self.ekey[e]
        if mark:
            self.ecnt[e] += 1
            ins.then_inc(self.sems[key], 1)
            val = self.ecnt[e]
            if e == 'pe':
                self.pe_pending = False
        else:
            val = self.ecnt[e] + 1
            if e == 'pe':
                self.pe_pending = True
        self._record(key, val, R, W, A)
        return ins

    def dma(self, e, out, in_, R=(), W=(), A=(), sb=None):
        need = self._need(R, W, A)
        self._wait(e, need)
        if sb.dkey is None:
            sb.dkey = {}
            sb.dcnt = {}
            self.dbufs.append(sb)
        if e not in sb.dkey:
            sb.dkey[e] = self._newsem("d_" + e + "_" + sb.name)
            sb.dcnt[e] = 0
        ins = self.eng[e].dma_start(out=out, in_=in_)
        sb.dcnt[e] += 16
        ins.then_inc(self.sems[sb.dkey[e]], 16)
        self._record(sb.dkey[e], sb.dcnt[e], R, W, A)
        return ins

    def barrier(self, engines=None):
        assert not self.pe_pending
        need = dict(self.final)
        for e in self.eng:
            need[self.ekey[e]] = self.ecnt[e]
        for b in self.dbufs:
            for e_ in b.dkey:
                need[b.dkey[e_]] = b.dcnt[e_]
        for e in (engines or self.eng):
            self._wait(e, dict(need))


def _runs(groups, mapfn):
    runs = []
    for li, g in enumerate(groups):
        d = mapfn(g)
        if d is None:
            continue
        if runs and runs[-1][0] + runs[-1][1] == li and runs[-1][2] + runs[-1][1] * 128 == d:
            runs[-1][1] += 1
        else:
            runs.append([li, 1, d])
    return runs


def build_program(stage=99):
    nc = bass.Bass("TRN2", target_bir_lowering=False)

    def din(name, shape, dt=F32):
        return nc.dram_tensor(name, list(shape), dt, kind="ExternalInput").ap()

    def dout(name, shape):
        return nc.dram_tensor(name, list(shape), F32, kind="ExternalOutput").ap()

    def dscr(name, shape, dt=BF16):
        return nc.dram_tensor(name, list(shape), dt, kind="Internal").ap()

    xo = din("xo", [TOWN, D])
    xp = din("xp", [NPRE, D])
    cak = din("cak", [512, 1024])
    cav = din("cav", [512, 1024])
    cckv = din("cckv", [2048, 512])
    ckr = din("ckr", [2048, 64])
    w_in = din("w_in", [D, DIN])
    w_kv_b = din("w_kv_b", [512, 2048])
    w_pa = din("w_pa", [1024, D])
    w_pb = din("w_pb", [1024, D])
    w_out = din("w_out", [D, D])
    w_up = din("w_up", [D, 4 * D])
    w_down = din("w_down", [4 * D, D])
    g_mix = din("g_mix", [1, D])
    g_ffn = din("g_ffn", [1, D])
    g_aq = din("g_aq", [1, 128])
    g_ak = din("g_ak", [1, 128])
    g_kv = din("g_kv", [1, 512])
    g_kr = din("g_kr", [1, 64])
    g_qn = din("g_qn", [1, 128])
    g_qr = din("g_qr", [1, 64])
    g_kn = din("g_kn", [128, 1])
    tabext = din("tabext", [8, 768])
    cos_o = din("cos_o", [TOWN, 32])
    sin_o = din("sin_o", [TOWN, 32])
    cos_p = din("cos_p", [NPRE, 32])
    sin_p = din("sin_p", [NPRE, 32])
    pmask_d = din("pmask", [128, 1])
    identb_d = din("identb", [128, 128])
    identf_d = din("identf", [128, 128])

    y_d = dout("y", [TOWN, D])
    oak_d = dout("o_ak", [640, 1024])
    oav_d = dout("o_av", [640, 1024])
    ockv_d = dout("o_ckv", [TOWN, 512])
    okr_d = dout("o_kr", [TOWN, 64])

    aqT_s = dscr("aqT_s", [8, 128, TOWN])
    qnT_s = dscr("qnT_s", [8, 128, TOWN])
    qrT_s = dscr("qrT_s", [4, 128, TOWN])
    akT_s = dscr("akT_s", [8, 128, NKEY_B])
    av_s = dscr("av_s", [NKEY_B, 1024])
    ckvT_s = dscr("ckvT_s", [4, 128, NKEY_M])
    krT_s = dscr("krT_s", [128, NKEY_M])
    oaT_s = dscr("oaT_s", [8, 128, TOWN])
    MB_s = dscr("MB_s", [8, 128, 640], F32)
    obT_s = dscr("obT_s", [8, 128, TOWN])

    with contextlib.ExitStack() as es:
        kb = KB(nc, es)
        op, dma = kb.op, kb.dma

        def sb(name, shape, dt, stack=None):
            t = (stack or es).enter_context(nc.sbuf_tensor("sb_" + name, list(shape), dt))
            return Buf(name, t)

        def ps(name, shape, dt, stack=None):
            t = (stack or es).enter_context(nc.psum_tensor("ps_" + name, list(shape), dt))
            return Buf(name, t)

        B_aqT, B_qnT, B_qrT, B_akT, B_av = Buf("s_aqT"), Buf("s_qnT"), Buf("s_qrT"), Buf("s_akT"), Buf("s_av")
        B_ckvT, B_krT, B_oaT, B_obT = Buf("s_ckvT"), Buf("s_krT"), Buf("s_oaT"), Buf("s_obT")
        B_out = Buf("outs")

        identb = sb("identb", [128, 128], BF16)
        identf = sb("identf", [128, 128], F32)
        onesb = sb("onesb", [128, 128], BF16)
        onesf = sb("onesf", [128, 128], F32)
        epsb = sb("epsb", [128, 1], F32)
        zerob = sb("zerob", [128, 1], F32)
        pmask = sb("pmaskb", [128, 1], F32)
        Gbig = sb("Gbig", [128, D], F32)
        gkn = sb("gknb", [128, 1], F32)
        wring = [sb(f"wring{i}", [128, 16, 512], BF16) for i in range(3)]

        dma('pool', identb[:], identb_d[:, :], W=[identb], sb=identb)
        dma('sp', identf[:], identf_d[:, :], W=[identf], sb=identf)
        dma('sp', pmask[:], pmask_d[:, :], W=[pmask], sb=pmask)
        dma('sp', gkn[:], g_kn[:, :], W=[gkn], sb=gkn)
        op('dve', lambda: nc.vector.memset(onesb[:], 1.0), W=[onesb])
        op('dve', lambda: nc.vector.memset(onesf[:], 1.0), W=[onesf])
        op('dve', lambda: nc.vector.memset(epsb[:], EPS), W=[epsb])
        op('dve', lambda: nc.vector.memset(zerob[:], 0.0), W=[zerob])

        B_MB = Buf("s_MB")
        mbsem = Buf("mbsem")
        for kk in range(128):
            dma('sp', MB_s[:, kk, :], tabext[:, 127 - kk:127 - kk + 640], A=[B_MB], sb=mbsem)

        def load_gain(dst, g, n, rep):
            src = bass.AP(g.tensor, 0, [[0, 128], [0, rep], [1, n]])
            dma('sp', dst[:].rearrange("p (r n) -> p r n", r=rep), src, W=[dst], sb=dst)

        def load_Gbig(g):
            dma('sp', Gbig[:], g.partition_broadcast(128)[:, 0, :], W=[Gbig], sb=Gbig)

        wstate = {'n': 0}

        def wload(src_ap, nk, ncols, view=None):
            if wstate.get('force') is not None:
                slot = wstate['force']
            else:
                slot = wring[wstate['n'] % 3]
                wstate['n'] += 1
            if isinstance(src_ap, list):
                for i, (c0, n, sp_) in enumerate(src_ap):
                    if i == 0:
                        dma('pool', slot[:, 0:nk, c0:c0 + n], sp_, W=[slot], sb=slot)
                    else:
                        dma('pool', slot[:, 0:nk, c0:c0 + n], sp_, A=[slot], sb=slot)
                return slot
            dst = slot[:, 0:nk, 0:ncols] if view is None else view(slot)
            dma('pool', dst, src_ap, W=[slot], sb=slot)
            return slot

        def wsrc(w, r0, nk, c0, ncols):
            return w[r0:r0 + nk * 128, c0:c0 + ncols].rearrange("(k p) n -> p k n", p=128)

        def frontend(fs, src_buf, src_ap, xnT, loc, T):
            xnb = fs['xnb'][fs['i'] % 2]
            ssb = fs['ss'][fs['i'] % 2]
            fs['i'] += 1
            op('act', lambda: nc.scalar.activation(out=xnb[:], in_=src_ap, func=AF.Square,
                                                   accum_out=ssb[:, 0:1]), R=[src_buf], W=[xnb, ssb])
            op('act', lambda: nc.scalar.activation(out=ssb[:, 1:2], in_=ssb[:, 0:1], func=AF.Sqrt,
                                                   bias=epsb[:], scale=1.0 / D), R=[ssb, epsb], A=[ssb])
            op('dve', lambda: nc.vector.reciprocal(out=ssb[:, 2:3], in_=ssb[:, 1:2]), R=[ssb], A=[ssb])
            op('dve', lambda: nc.vector.scalar_tensor_tensor(out=xnb[:], in0=src_ap, scalar=ssb[:, 2:3],
                                                             in1=Gbig[:], op0=ALU.mult, op1=ALU.mult),
               R=[src_buf, ssb, Gbig], W=[xnb])
            for half in range(2):
                pt = fs['pst'][fs['j'] % 2]
                fs['j'] += 1

                def mm(pt=pt, half=half):
                    ins = None
                    for j in range(8):
                        k = half * 8 + j
                        ins = nc.tensor.transpose(out=pt[:, j * 128:(j + 1) * 128],
                                                  in_=xnb[:, k * 128:(k + 1) * 128], identity=identb[:])
                    return ins
                op('pe', mm, R=[xnb, identb], W=[pt])
                dst = xnT[:, half * 8:half * 8 + 8, loc * 128:(loc + 1) * 128]
                src = pt[:].rearrange("p (k t) -> p k t", k=8)
                if half == 0:
                    op('act', lambda: nc.scalar.activation(out=dst, in_=src, func=AF.Copy), R=[pt], A=[xnT])
                else:
                    op('dve', lambda: nc.vector.tensor_copy(out=dst, in_=src), R=[pt], A=[xnT])

        with contextlib.ExitStack() as pa:
            TA = 768
            gt_aq = sb("gt_aq", [128, 512], F32, pa)
            gt_ak = sb("gt_ak", [128, 512], F32, pa)
            gt_qn = sb("gt_qn", [128, 512], F32, pa)
            gt_qr = sb("gt_qr", [128, 512], F32, pa)
            gt_kv = sb("gt_kv", [128, 512], F32, pa)
            gt_kr = sb("gt_kr", [128, 64], F32, pa)
            load_gain(gt_aq, g_aq, 128, 4)
            load_gain(gt_ak, g_ak, 128, 4)
            load_gain(gt_qn, g_qn, 128, 4)
            load_gain(gt_qr, g_qr, 64, 8)
            load_gain(gt_kv, g_kv, 512, 1)
            load_gain(gt_kr, g_kr, 64, 1)
            xst = [sb(f"xst{i}", [128, D], F32, pa) for i in range(2)]
            fs = dict(i=0, j=0,
                      xnb=[sb(f"xnb{i}", [128, D], BF16, pa) for i in range(2)],
                      ss=[sb(f"fss{i}", [128, 4], F32, pa) for i in range(2)],
                      pst=[ps(f"pst{i}", [128, 1024], BF16, pa) for i in range(2)])
            xnTs = [sb(f"xnT{i}", [128, 16, TA], BF16, pa) for i in range(2)]
            psm = [ps(f"psm{i}", [128, 512], F32, pa) for i in range(4)]
            ptr = [ps(f"ptr{i}", [128, 512], BF16, pa) for i in range(2)]
            sqr = [sb(f"sqj{i}", [128, 512], F32, pa) for i in range(2)]
            ssr = [sb(f"ssr{i}", [128, 24], F32, pa) for i in range(4)]
            nfr = [sb(f"nf{i}", [128, 512], F32, pa) for i in range(4)]
            nbr = [sb(f"nb{i}", [128, 512], BF16, pa) for i in range(4)]
            rtmp = sb("rtmp", [128, 4, 256], F32, pa)
            tst = [sb(f"tst{i}", [128, 4, TA], BF16, pa) for i in range(2)]
            csts = [sb(f"cst{i}", [128, 8, 2, 32], F32, pa) for i in range(2)]
            cnt = dict(ps=0, tr=0, r=0, tst=0)

            load_Gbig(g_mix)

            def u_plain(c0, n):
                return lambda: (wsrc(w_in, 0, 16, c0, n), 16, n, None)

            def u_qn(h0):
                def f():
                    src = [(j * 128, 128, bass.AP(w_in.tensor, 3072 + 192 * (h0 + j), [[DIN, 128], [128 * DIN, 16], [1, 128]]))
                           for j in range(4)]
                    return (src, 16, 512, None)
                return f

            def u_qr():
                def f():
                    src = [(j * 64, 64, bass.AP(w_in.tensor, 3072 + 128 + 192 * j, [[DIN, 128], [128 * DIN, 16], [1, 64]]))
                           for j in range(8)]
                    return (src, 16, 512, None)
                return f

            U = {
                'aq0': ('aq', u_plain(0, 512), 0), 'aq1': ('aq', u_plain(512, 512), 4),
                'ak0': ('ak', u_plain(1024, 512), 0), 'ak1': ('ak', u_plain(1536, 512), 4),
                'av0': ('av', u_plain(2048, 512), 0), 'av1': ('av', u_plain(2560, 512), 4),
                'qn0': ('qn', u_qn(0), 0), 'qn1': ('qn', u_qn(4), 4),
                'qr': ('qr', u_qr(), 0),
                'ckv': ('ckv', u_plain(4608, 512), 0),
                'kr': ('kr', u_plain(5120, 64), 0),
            }
            own_units = ['aq0', 'aq1', 'ak0', 'ak1', 'av0', 'av1', 'qn0', 'qn1', 'qr', 'ckv', 'kr']
            tiles = [
                ('p', list(range(0, 6)), ['ckv', 'kr']),
                ('p', list(range(6, 12)), ['ckv', 'kr']),
                ('p', list(range(12, 16)), ['ckv', 'kr', 'ak0', 'ak1', 'av0', 'av1']),
                ('o', list(range(0, 6)), own_units),
                ('o', list(range(6, 12)), own_units),
                ('o', list(range(12, 17)), own_units),
            ]

            def map_band(src, g):
                if src == 'p':
                    return (g - 12) * 128 if g >= 12 else None
                return 512 + g * 128 if g < 16 else 3072

            def map_mla(src, g):
                if src == 'p':
                    return g * 128
                return 2048 + g * 128 if g < 16 else 6144

            def map_own(src, g):
                return g * 128

            sched = [(ti, un) for ti, t in enumerate(tiles) for un in t[2]]
            loaded = {}

            def issue(idx):
                if idx < len(sched):
                    ti, un = sched[idx]
                    src, nk, ncols, view = U[un][1]()
                    loaded[idx] = wload(src, nk, ncols, view)
            sidx = 0
            if stage < 0.1:
                tiles = []
            elif stage < 0.4:
                tiles = tiles[:1]
            elif stage < 0.45:
                tiles = tiles[:2]
            elif stage < 0.55:
                tiles = tiles[:3]
                tiles[2] = (tiles[2][0], tiles[2][1], ['ckv', 'kr'])
            elif stage < 0.61:
                tiles = tiles[:3]
                tiles[2] = (tiles[2][0], tiles[2][1], ['ckv', 'kr', 'ak0'])
            elif stage < 0.63:
                tiles = tiles[:3]
                tiles[2] = (tiles[2][0], tiles[2][1], ['ckv', 'kr', 'ak0', 'ak1'])
            elif stage < 0.65:
                tiles = tiles[:3]
                tiles[2] = (tiles[2][0], tiles[2][1], ['ckv', 'kr', 'av0'])
            elif stage < 0.7:
                tiles = tiles[:3]
            import os as _os
            if _os.environ.get("KTILES"):
                tiles = tiles[:int(_os.environ["KTILES"])]
            if _os.environ.get("KOWN"):
                ou = _os.environ["KOWN"].split(",")
                tiles = [(a, b, (ou if a == 'o' else c)) for (a, b, c) in tiles]
            sched = [(ti, un) for ti, t in enumerate(tiles) for un in t[2]]
            issue(0)
            issue(1)
            def tile_front(ti):
                src, groups, units = tiles[ti]
                xsrc = xp if src == 'p' else xo
                G = len(groups)
                T = G * 128
                cst_ = csts[ti % 2]
                xnT_ = xnTs[ti % 2]
                csrc_c = (cos_p if src == 'p' else cos_o)
                csrc_s = (sin_p if src == 'p' else sin_o)
                g0 = groups[0]
                dma('sp', cst_[:, 0:G, 0, :], csrc_c[g0 * 128:(g0 + G) * 128, :].rearrange("(g p) c -> p g c", p=128),
                    W=[cst_], sb=cst_)
                dma('sp', cst_[:, 0:G, 1, :], csrc_s[g0 * 128:(g0 + G) * 128, :].rearrange("(g p) c -> p g c", p=128),
                    A=[cst_], sb=cst_)
                for li, g in enumerate(groups):
                    xs = xst[(fs['i']) % 2]
                    dma('sp', xs[:], xsrc[g * 128:(g + 1) * 128, :], W=[xs], sb=xs)
                    frontend(fs, xs, xs[:], xnT_, li, T)

            pend = []

            def do_transposes(nb, li, nblk, ts_, final_cb):
                pt = ptr[cnt['tr'] % 2]
                cnt['tr'] += 1

                def mm():
                    ins = None
                    for j in range(nblk):
                        ins = nc.tensor.transpose(out=pt[:, j * 128:(j + 1) * 128],
                                                  in_=nb[:, j * 128:(j + 1) * 128], identity=identb[:])
                    return ins
                op('pe', mm, R=[nb, identb], W=[pt])
                dst = ts_[:, 0:nblk, li * 128:(li + 1) * 128]
                s_ = pt[:, 0:nblk * 128].rearrange("p (k t) -> p k t", k=nblk)
                op('act', lambda: nc.scalar.activation(out=dst, in_=s_, func=AF.Copy), R=[pt], A=[ts_])
                if final_cb is not None:
                    final_cb()

            def make_final(kind, src, groups, lgs, ts_, h0):
                def fin():
                    if kind in ('aq', 'qn', 'qr'):
                        dstT, Bd, mp = {'aq': (aqT_s, B_aqT, map_own), 'qn': (qnT_s, B_qnT, map_own),
                                        'qr': (qrT_s, B_qrT, map_own)}[kind]
                    elif kind == 'ak':
                        dstT, Bd, mp = akT_s, B_akT, map_band
                    elif kind == 'ckv':
                        dstT, Bd, mp = ckvT_s, B_ckvT, map_mla
                    else:
                        dstT, Bd, mp = krT_s, B_krT, map_mla
                    sub = [groups[li] for li in lgs]
                    for (ls, n, d0) in _runs(sub, lambda g: mp(src, g)):
                        l0 = lgs[ls]
                        if kind == 'kr':
                            dma('sp', dstT[:, d0:d0 + n * 128], ts_[:, 0, l0 * 128:(l0 + n) * 128],
                                R=[ts_], A=[Bd], sb=ts_)
                        else:
                            hh = h0 if kind in ('aq', 'ak', 'qn') else 0
                            dma('sp', dstT[hh:hh + 4, :, d0:d0 + n * 128].rearrange("h p t -> p h t"),
                                ts_[:, 0:4, l0 * 128:(l0 + n) * 128], R=[ts_], A=[Bd], sb=ts_)
                return fin

            if tiles:
                tile_front(0)
            for ti, (src, groups, units) in enumerate(tiles):
                G = len(groups)
                T = G * 128
                cst = csts[ti % 2]
                xnT = xnTs[ti % 2]
                for ui, un in enumerate(units):
                    if ui == min(1, len(units) - 1) and ti + 1 < len(tiles):
                        tile_front(ti + 1)
                    kind, _, h0 = U[un]
                    slot = loaded.pop(sidx)
                    issue(sidx + 2)
                    sidx += 1
                    ncols = 64 if kind == 'kr' else 512
                    if kind in ('ak', 'av') and src == 'p':
                        lgs = [li for li, g in enumerate(groups) if g >= 12]
                    else:
                        lgs = list(range(G))
                    transposed = kind != 'av'
                    if not transposed:
                        while pend:
                            do_transposes(*pend.pop(0))
                    if transposed:
                        ts_ = tst[cnt['tst'] % 2]
                        cnt['tst'] += 1
                    for li in lgs:
                        g = groups[li]
                        pm = psm[cnt['ps'] % 4]
                        cnt['ps'] += 1

                        def mm(pm=pm, li=li):
                            ins = None
                            for k in range(16):
                                ins = nc.tensor.matmul(pm[:, 0:ncols], lhsT=xnT[:, k, li * 128:(li + 1) * 128],
                                                       rhs=slot[:, k, 0:ncols], start=(k == 0), stop=(k == 15))
                            return ins
                        op('pe', mm, R=[xnT, slot], W=[pm])
                        if len(pend) >= 3:
                            do_transposes(*pend.pop(0))
                        ri = cnt['r'] % 4
                        cnt['r'] += 1
                        nf, nb, ssb = nfr[ri], nbr[ri], ssr[ri]
                        own = (src == 'o')
                        if kind == 'av':
                            need_out = own and g >= 12
                            if need_out:
                                op('dve', lambda: nc.vector.tensor_copy(out=nf[:], in_=pm[:]), R=[pm], W=[nf])
                                orow = (g - 12) * 128
                                dma('sp', oav_d[orow:orow + 128, h0 * 128:h0 * 128 + 512], nf[:], R=[nf], A=[B_out], sb=nf)
                                op('act', lambda: nc.scalar.activation(out=nb[:], in_=nf[:], func=AF.Copy), R=[nf], W=[nb])
                            else:
                                op('dve', lambda: nc.vector.tensor_copy(out=nb[:], in_=pm[:]), R=[pm], W=[nb])
                            krow = map_band(src, g)
                            dma('sp', av_s[krow:krow + 128, h0 * 128:h0 * 128 + 512], nb[:], R=[nb], A=[B_av], sb=nb)
                            continue
                        hd, nh, gt = {'aq': (128, 4, gt_aq), 'ak': (128, 4, gt_ak), 'qn': (128, 4, gt_qn),
                                      'qr': (64, 8, gt_qr), 'ckv': (512, 1, gt_kv), 'kr': (64, 1, gt_kr)}[kind]
                        nc_ = nh * hd
                        if nh == 1:
                            sq = sqr[0]
                            op('act', lambda: nc.scalar.activation(
                                out=sq[:, 0:hd], in_=pm[:, 0:hd], func=AF.Square,
                                accum_out=ssb[:, 0:1]), R=[pm], W=[sq, ssb])
                        else:
                            sq = sqr[cnt['r'] % 2]
                            op('act', lambda: nc.scalar.activation(out=sq[:, 0:nc_], in_=pm[:, 0:nc_], func=AF.Square),
                               R=[pm], W=[sq])
                            op('dve', lambda: nc.vector.tensor_reduce(
                                out=ssb[:, 0:nh], in_=sq[:, 0:nc_].rearrange("p (h d) -> p h d", h=nh),
                                axis=AX.X, op=ALU.add), R=[sq], W=[ssb])
                        op('act', lambda: nc.scalar.activation(out=ssb[:, 8:8 + nh], in_=ssb[:, 0:nh], func=AF.Sqrt,
                                                               bias=epsb[:], scale=1.0 / hd), R=[ssb, epsb], A=[ssb])
                        op('dve', lambda: nc.vector.reciprocal(out=ssb[:, 16:16 + nh], in_=ssb[:, 8:8 + nh]),
                           R=[ssb], A=[ssb])
                        rope = kind in ('qr', 'kr')
                        need_f32 = kind in ('ak', 'ckv', 'kr')
                        tgt = nf if (need_f32 or rope) else nb
                        if nh == 1:
                            op('dve', lambda: nc.vector.scalar_tensor_tensor(
                                out=tgt[:, 0:hd], in0=pm[:, 0:hd], scalar=ssb[:, 16:17], in1=gt[:, 0:hd],
                                op0=ALU.mult, op1=ALU.mult), R=[pm, ssb, gt], W=[tgt])
                        else:
                            rb = ssb[:, 16:16 + nh].unsqueeze(2).broadcast_to([128, nh, hd])
                            op('dve', lambda: nc.vector.tensor_tensor(
                                out=sq[:, 0:nc_].rearrange("p (h d) -> p h d", h=nh),
                                in0=pm[:, 0:nc_].rearrange("p (h d) -> p h d", h=nh), in1=rb, op=ALU.mult),
                               R=[pm, ssb], W=[sq])
                            op('dve', lambda: nc.vector.tensor_tensor(out=tgt[:, 0:nc_], in0=sq[:, 0:nc_], in1=gt[:, 0:nc_],
                                                                      op=ALU.mult), R=[sq, gt], W=[tgt])
                        if rope:
                            xv = nf[:, 0:nh * 64].rearrange("p (h t c) -> p h t c", h=nh, t=2)
                            x1, x2 = xv[:, :, 0, :], xv[:, :, 1, :]
                            cosb = cst[:, li, 0, :].unsqueeze(1).broadcast_to([128, nh, 32])
                            sinb = cst[:, li, 1, :].unsqueeze(1).broadcast_to([128, nh, 32])
                            tv = [rtmp[:, i, 0:nh * 32].rearrange("p (h c) -> p h c", h=nh) for i in range(4)]
                            for i, (a, b) in enumerate([(x1, cosb), (x2, sinb), (x2, cosb), (x1, sinb)]):
                                op('dve', lambda a=a, b=b, i=i: nc.vector.tensor_tensor(out=tv[i], in0=a, in1=b, op=ALU.mult),
                                   R=[nf, cst], A=[rtmp] if i else [], W=[] if i else [rtmp])
                            op('dve', lambda: nc.vector.tensor_tensor(out=x1, in0=tv[0], in1=tv[1], op=ALU.subtract),
                               R=[rtmp], A=[nf])
                            op('dve', lambda: nc.vector.tensor_tensor(out=x2, in0=tv[2], in1=tv[3], op=ALU.add),
                               R=[rtmp], A=[nf])
                        if need_f32 or rope:
                            if kind == 'kr':
                                op('act', lambda: nc.scalar.activation(out=nb[:, 0:64], in_=nf[:, 0:64], func=AF.Copy),
                                   R=[nf], W=[nb])
                                op('act', lambda: nc.scalar.activation(out=nb[:, 64:128], in_=nf[:, 0:64], func=AF.Copy),
                                   R=[nf], A=[nb])
                            else:
                                op('act', lambda: nc.scalar.activation(out=nb[:], in_=nf[:], func=AF.Copy),
                                   R=[nf], W=[nb])
                        if own and kind == 'ak' and g >= 12:
                            orow = (g - 12) * 128
                            dma('sp', oak_d[orow:orow + 128, h0 * 128:h0 * 128 + 512], nf[:], R=[nf], A=[B_out], sb=nf)
                        if own and kind == 'ckv':
                            dma('sp', ockv_d[g * 128:(g + 1) * 128, :], nf[:], R=[nf], A=[B_out], sb=nf)
                        if own and kind == 'kr':
                            dma('sp', okr_d[g * 128:(g + 1) * 128, :], nf[:, 0:64], R=[nf], A=[B_out], sb=nf)
                        nblk = 1 if kind == 'kr' else 4
                        fin = make_final(kind, src, groups, lgs, ts_, h0) if li == lgs[-1] else None
                        pend.append((nb, li, nblk, ts_, fin))
            while pend:
                do_transposes(*pend.pop(0))

            cbuf = sb("cbuf", [128, 16, 512], BF16, pa)
            kbuf = sb("kbuf", [128, 16, 128], BF16, pa)
            _kc = int(_os.environ.get('KCACHE', '7'))
            if stage >= 1:
              if _kc & 1:
                cvv = cbuf[:, 8:16, :].rearrange("p (g a) c -> p g (a c)", g=4)
                dma('pool', cvv, cav.rearrange("(g p) c -> p g c", p=128), W=[cbuf], sb=cbuf)
                dma('sp', av_s[2560:3072, :].rearrange("(g p) c -> p g c", p=128), cvv, R=[cbuf], A=[B_av], sb=cbuf)
                dma('pool', cbuf[:, 0:8, :].rearrange("p (g a) c -> p g (a c)", g=4),
                    cak.rearrange("(g p) c -> p g c", p=128), A=[cbuf], sb=cbuf)
                ckview = cbuf[:, 0:8, :].rearrange("p (g a) c -> p g (a c)", g=4)
                for half in range(2):
                    ts_ = tst[cnt['tst'] % 2]
                    cnt['tst'] += 1
                    for g in range(4):
                        pt = ptr[cnt['tr'] % 2]
                        cnt['tr'] += 1

                        def mm(pt=pt, g=g, half=half):
                            ins = None
                            for j in range(4):
                                h = half * 4 + j
                                ins = nc.tensor.transpose(out=pt[:, j * 128:(j + 1) * 128],
                                                          in_=ckview[:, g, h * 128:(h + 1) * 128], identity=identb[:])
                            return ins
                        op('pe', mm, R=[cbuf, identb], W=[pt])
                        op('act', lambda pt=pt, g=g, ts_=ts_: nc.scalar.activation(
                            out=ts_[:, 0:4, g * 128:(g + 1) * 128], in_=pt[:].rearrange("p (k t) -> p k t", k=4),
                            func=AF.Copy), R=[pt], A=[ts_])
                    dma('sp', akT_s[half * 4:half * 4 + 4, :, 2560:3072].rearrange("h p t -> p h t"),
                        ts_[:, 0:4, 0:512], R=[ts_], A=[B_akT], sb=ts_)
              if _kc & 2:
                dma('pool', cbuf[:, :, :], cckv.rearrange("(g p) c -> p g c", p=128), W=[cbuf], sb=cbuf)
                for blk in range(4):
                    ts_ = tst[cnt['tst'] % 2]
                    cnt['tst'] += 1
                    for gg in range(4):
                        g = blk * 4 + gg
                        pt = ptr[cnt['tr'] % 2]
                        cnt['tr'] += 1

                        def mm(pt=pt, g=g):
                            ins = None
                            for j in range(4):
                                ins = nc.tensor.transpose(out=pt[:, j * 128:(j + 1) * 128],
                                                          in_=cbuf[:, g, j * 128:(j + 1) * 128], identity=identb[:])
                            return ins
                        op('pe', mm, R=[cbuf, identb], W=[pt])
                        op('act', lambda pt=pt, gg=gg, ts_=ts_: nc.scalar.activation(
                            out=ts_[:, 0:4, gg * 128:(gg + 1) * 128], in_=pt[:].rearrange("p (k t) -> p k t", k=4),
                            func=AF.Copy), R=[pt], A=[ts_])
                    d0 = 4096 + blk * 512
                    dma('sp', ckvT_s[0:4, :, d0:d0 + 512].rearrange("h p t -> p h t"), ts_[:, 0:4, 0:512],
                        R=[ts_], A=[B_ckvT], sb=ts_)
              if _kc & 4:
                kst = xst[0]
                kstv = kst[:, 0:1024].rearrange("p (g c) -> p g c", c=64)
                dma('sp', kstv, ckr.rearrange("(g p) c -> p g c", p=128), W=[kst], sb=kst)
                op('act', lambda: nc.scalar.activation(out=kbuf[:, :, 0:64], in_=kstv, func=AF.Copy), R=[kst], W=[kbuf])
                op('dve', lambda: nc.vector.tensor_copy(out=kbuf[:, :, 64:128], in_=kstv), R=[kst], A=[kbuf])
                for blk in range(4):
                    ts_ = tst[cnt['tst'] % 2]
                    cnt['tst'] += 1
                    pt = ptr[cnt['tr'] % 2]
                    cnt['tr'] += 1

                    def mm(pt=pt, blk=blk):
                        ins = None
                        for j in range(4):
                            ins = nc.tensor.transpose(out=pt[:, j * 128:(j + 1) * 128],
                                                      in_=kbuf[:, blk * 4 + j, :], identity=identb[:])
                        return ins
                    op('pe', mm, R=[kbuf, identb], W=[pt])
                    op('act', lambda pt=pt, ts_=ts_: nc.scalar.activation(out=ts_[:, 0, 0:512], in_=pt[:], func=AF.Copy),
                       R=[pt], W=[ts_])
                    d0 = 4096 + blk * 512
                    dma('sp', krT_s[:, d0:d0 + 512], ts_[:, 0, 0:512], R=[ts_], A=[B_krT], sb=ts_)
            kb.barrier()

        with contextlib.ExitStack() as pb:
          if stage >= 2:
              ckvT = sb("ckvT", [128, 4, NKEY_M], BF16, pb)
              krT = sb("krT", [128, NKEY_M], BF16, pb)
              wkvb_b, knT_b, Vh_b = wring[0], wring[1], wring[2]
              wkvb = wkvb_b[:].rearrange("p k c -> p (k c)").rearrange("p (k n) -> p k n", k=4)
              hsets = [(sb(f"qn_h{i}", [128, TOWN], BF16, pb), sb(f"qr_h{i}", [128, TOWN], BF16, pb),
                        sb(f"aq_h{i}", [128, TOWN], BF16, pb), sb(f"ak_h{i}", [128, NKEY_B], BF16, pb),
                        sb(f"av_h{i}", [128, 25, 128], BF16, pb), sb(f"MB_h{i}", [128, 640], F32, pb))
                       for i in range(2)]
              qn, qr, aq, ak, av, MB = hsets[0]

              def head_loads(h):
                  qn_, qr_, aq_, ak_, av_, MB_ = hsets[h % 2]
                  dma('sp', qn_[:], qnT_s[h, :, :], R=[B_qnT], W=[qn_], sb=qn_)
                  dma('sp', qr_[:], qrT_s[h // 2, :, :], R=[B_qrT], W=[qr_], sb=qr_)
                  dma('sp', aq_[:], aqT_s[h, :, :], R=[B_aqT], W=[aq_], sb=aq_)
                  dma('sp', ak_[:], akT_s[h, :, :], R=[B_akT], W=[ak_], sb=ak_)
                  dma('sp', av_[:], av_s[:, h * 128:(h + 1) * 128].rearrange("(t p) d -> p t d", p=128),
                      R=[B_av], W=[av_], sb=av_)
                  dma('sp', MB_[:], MB_s[h, :, :], R=[B_MB], W=[MB_], sb=MB_)
              knT = knT_b[:].rearrange("p k c -> p (k c)")[:, 0:NKEY_M]
              Vh = Vh_b[:].rearrange("p k c -> p (k c)")[:, 0:NKEY_M].rearrange("p (t d) -> p t d", d=128)
              PT = [sb(f"PT{i}", [128, 512], BF16, pb) for i in range(4)]
              sbias = [sb(f"sbias{i}", [128, 512], F32, pb) for i in range(3)]
              sqfr = [sb(f"sqf{i}", [128, 512], F32, pb) for i in range(2)]
              rsfr = [sb(f"rsf{i}", [128, 512], F32, pb) for i in range(2)]
              rden = sb("rden", [128, 512], F32, pb)
              obT = sb("obT_h", [128, TOWN], BF16, pb)
              oaT = sb("oaT_h", [128, TOWN], BF16, pb)
              NPS = 4
              pS = [ps(f"pS{i}", [128, 512], F32, pb) for i in range(NPS)]
              pOr = [ps(f"pO{i}", [128, 512], F32, pb) for i in range(2)]
              pDr = [ps(f"pD{i}", [128, 512], F32, pb) for i in range(2)]
              c = dict(s=0, p=0, e=0, b=0, f=0, o=0, x=0)
              pX = pS + pOr + pDr

              for cc in range(4):
                  dma('sp', ckvT[:, cc, :], ckvT_s[cc, :, :], R=[B_ckvT], A=[ckvT], sb=ckvT)
              dma('sp', krT[:], krT_s[:, :], R=[B_krT], W=[krT], sb=krT)
              dma('pool', wkvb, w_kv_b.rearrange("(k p) n -> p k n", p=128), W=[wkvb_b], sb=wkvb_b)

              def attend(nq, tiles_, q_of, out_buf, out_c0, k_stat, v_stat, kr_stat=None, hp=0):
                  prevq = []
                  n = len(tiles_)
                  lag = 3
                  pO, pD = pOr[c['o'] % 2], pDr[c['o'] % 2]
                  c['o'] += 1

                  def pv(t, P, first, last):
                      c0, c1, np_ = t['c0'], t['c1'], t['np']
                      def mm():
                          nc.tensor.matmul(pO[:, c0:c1], lhsT=v_stat(t)[0:np_, :], rhs=P[0:np_, c0:c1],
                                           start=first, stop=last)
                          return nc.tensor.matmul(pD[:, c0:c1], lhsT=onesb[0:np_, :], rhs=P[0:np_, c0:c1],
                                                  start=first, stop=last)
                      if first:
                          op('pe', mm, R=[P, onesb, Vh_b, av], W=[pO, pD])
                      else:
                          op('pe', mm, R=[P, onesb, Vh_b, av], A=[pO, pD])
                  for i, t in enumerate(tiles_):
                      c0, c1, np_ = t['c0'], t['c1'], t['np']
                      S = pS[c['s'] % NPS]
                      c['s'] += 1
                      P = PT[c['p'] % 4]
                      c['p'] += 1

                      def mm(S=S, t=t, c0=c0, c1=c1, np_=np_):
                          ops = q_of(t, c0, c1)
                          ins = None
                          for j, (l, r_) in enumerate(ops):
                              ins = nc.tensor.matmul(S[0:np_, c0:c1], lhsT=l, rhs=r_, start=(j == 0), stop=(j == len(ops) - 1))
                          return ins
                      op('pe', mm, R=[knT_b, krT, qn, qr, aq, ak], W=[S])
                      if len(prevq) >= lag:
                          pr_ = prevq.pop(0)
                          pv(pr_[0], pr_[1], pr_[2] == 0, False)
                      bias_ap = pmask[0:np_, :] if t.get('bias') == 'pm' else zerob[0:np_, :]
                      if t.get('mb0') is not None:
                          sbt = sbias[c['b'] % 3]
                          c['b'] += 1
                          m0 = t['mb0']
                          op('dve', lambda S=S, sbt=sbt, m0=m0: nc.vector.scalar_tensor_tensor(
                              out=sbt[0:np_, c0:c1], in0=S[0:np_, c0:c1], scalar=BAND_SCALE, in1=MB[0:np_, m0:m0 + (c1 - c0)],
                              op0=ALU.mult, op1=ALU.add), R=[S, MB], W=[sbt])
                          op('act', lambda sbt=sbt, P=P: nc.scalar.activation(out=P[0:np_, c0:c1], in_=sbt[0:np_, c0:c1],
                                                                            func=AF.Exp, bias=bias_ap, scale=1.0),
                             R=[sbt, pmask, zerob], W=[P])
                      else:
                          op('act', lambda S=S, P=P: nc.scalar.activation(out=P[0:np_, c0:c1], in_=S[0:np_, c0:c1],
                                                                        func=AF.Exp, bias=bias_ap, scale=MLA_SCALE),
                             R=[S, pmask, zerob], W=[P])
                      if t.get('mask') is not None:
                          which, mc = t['mask']
                          pr = slice(0, 64) if which == 'lo' else slice(64, 128)
                          op('dve', lambda P=P, pr=pr, mc=mc: nc.vector.memset(P[pr, mc:mc + 64], 0.0), W=[P])
                      prevq.append((t, P, i))
                  while prevq:
                      pr_ = prevq.pop(0)
                      pv(pr_[0], pr_[1], pr_[2] == 0, len(prevq) == 0)
                  op('dve', lambda: nc.vector.reciprocal(out=rden[:, 0:nq], in_=pD[:, 0:nq]), R=[pD], W=[rden])
                  op('dve', lambda: nc.vector.tensor_tensor(out=out_buf[:, out_c0:out_c0 + nq], in0=pO[:, 0:nq],
                                                            in1=rden[:, 0:nq], op=ALU.mult), R=[pO, rden], A=[out_buf])

              def exp_gen(hh):
                kchunks = [(kc, min(512, NKEY_M - kc)) for kc in range(0, NKEY_M, 512)]
                vgroups = [(kt0, min(4, 49 - kt0)) for kt0 in range(0, 49, 4)]

                def exp_stage1(kc, n):
                    pe_ = pS[c['s'] % NPS]
                    c['s'] += 1
                    sq_ = sqfr[c['e'] % 2]

                    def mm():
                        ins = None
                        for cc in range(4):
                            ins = nc.tensor.matmul(pe_[:, 0:n], lhsT=wkvb[:, cc, hh * 256:hh * 256 + 128],
                                                   rhs=ckvT[:, cc, kc:kc + n], start=(cc == 0), stop=(cc == 3))
                        return ins
                    op('pe', mm, R=[wkvb_b, ckvT], W=[pe_])
                    op('act', lambda: nc.scalar.activation(out=sq_[:, 0:n], in_=pe_[:, 0:n], func=AF.Square),
                       R=[pe_], W=[sq_])
                    c['e'] += 1
                    return (kc, n, pe_, sq_)

                def exp_stage2(kc, n, pe_, sq_):
                    pN = pS[c['s'] % NPS]
                    c['s'] += 1
                    rs_ = rsfr[c['f'] % 2]
                    c['f'] += 1
                    op('pe', lambda: nc.tensor.matmul(pN[:, 0:n], lhsT=onesf[:], rhs=sq_[:, 0:n], start=True, stop=True),
                       R=[onesf, sq_], W=[pN])
                    op('act', lambda: nc.scalar.activation(out=rs_[:, 0:n], in_=pN[:, 0:n], func=AF.Sqrt,
                                                           bias=epsb[:], scale=1.0 / 128), R=[pN, epsb], W=[rs_])
                    op('dve', lambda: nc.vector.reciprocal(out=rs_[:, 0:n], in_=rs_[:, 0:n]), R=[rs_], W=[rs_])
                    op('dve', lambda: nc.vector.scalar_tensor_tensor(
                        out=knT[:, kc:kc + n], in0=pe_[:, 0:n], scalar=gkn[:, 0:1], in1=rs_[:, 0:n],
                        op0=ALU.mult, op1=ALU.mult), R=[pe_, gkn, rs_], A=[knT_b])

                def v_group(kt0, nt):
                    pe_ = pS[c['s'] % NPS]
                    c['s'] += 1

                    def mm():
                        ins = None
                        for j in range(nt):
                            kt = kt0 + j
                            for cc in range(4):
                                ins = nc.tensor.matmul(pe_[:, j * 128:(j + 1) * 128], lhsT=ckvT[:, cc, kt * 128:(kt + 1) * 128],
                                                       rhs=wkvb[:, cc, hh * 256 + 128:hh * 256 + 256],
                                                       start=(cc == 0), stop=(cc == 3))
                        return ins
                    op('pe', mm, R=[wkvb_b, ckvT], W=[pe_])
                    op('act', lambda: nc.scalar.activation(
                        out=Vh[:, kt0:kt0 + nt, :], in_=pe_[:, 0:nt * 128].rearrange("p (t d) -> p t d", t=nt),
                        func=AF.Copy), R=[pe_], A=[Vh_b])
                for ii in range(len(kchunks)):
                    st1 = exp_stage1(*kchunks[ii])
                    if ii < len(vgroups):
                        v_group(*vgroups[ii])
                    exp_stage2(*st1)
                    yield

              for _ in exp_gen(0):
                  pass
              for h in range(8):
                  hp = (h % 2) * 64
                  if h == 0:
                      head_loads(0)
                  qn, qr, aq, ak, av, MB = hsets[h % 2]
                  if h + 1 < 8:
                      head_loads(h + 1)
                  def q_mla(qbase):
                      def f(t, c0, c1):
                          kidx, np_ = t['kidx'], t['np']
                          return [(knT[:, kidx:kidx + np_], qn[:, qbase + c0:qbase + c1]),
                                  (krT[hp:hp + 64, kidx:kidx + np_], qr[hp:hp + 64, qbase + c0:qbase + c1])]
                      return f
                  vm = lambda t: Vh[:, t['kidx'] // 128, :]
                  for j in range(4):
                      tl = [dict(kidx=kt * 128, np=128, c0=0, c1=512, bias='pm') for kt in range(16)]
                      for kt in range(4 * j + 4):
                          if kt < 4 * j:
                              tl.append(dict(kidx=2048 + kt * 128, np=128, c0=0, c1=512))
                          else:
                              m = kt - 4 * j
                              tl.append(dict(kidx=2048 + kt * 128, np=128, c0=128 * m, c1=512, mask=('hi', 128 * m)))
                      attend(512, tl, q_mla(512 * j), obT, 512 * j, None, vm)
                  tl = [dict(kidx=4096 + kt * 128, np=128, c0=0, c1=64) for kt in range(16)]
                  tl.append(dict(kidx=6144, np=64, c0=0, c1=64))
                  attend(64, tl, q_mla(2048), obT, 2048, None, vm)

                  eg = exp_gen(h + 1) if h + 1 < 8 else iter(())

                  def q_band(qbase):
                      def f(t, c0, c1):
                          kidx, np_ = t['kidx'], t['np']
                          return [(ak[:, kidx:kidx + np_], aq[:, qbase + c0:qbase + c1])]
                      return f
                  va = lambda t: av[:, t['kidx'] // 128, :]
                  for j in range(4):
                      q0 = 512 * j
                      tl = []
                      for t_ in [3, 0, 1, 2, 4, 5, 6, 7]:
                          ki0 = q0 + 128 * t_
                          if t_ <= 3:
                              c0, c1 = 0, 128 * (t_ + 1)
                              mask = ('lo', c1 - 64)
                          else:
                              c0, c1 = 128 * (t_ - 4), 512
                              mask = ('hi', c0)
                          d = dict(kidx=ki0, np=128, c0=c0, c1=c1, mask=mask, mb0=c0 + 512 - 128 * t_)
                          if j == 0 and t_ <= 3:
                              d['bias'] = 'pm'
                          tl.append(d)
                      for _ in range(3):
                          next(eg, None)
                      attend(512, tl, q_band(q0), oaT, q0, None, va)
                  for _ in range(3):
                      next(eg, None)
                  tl = [dict(kidx=2560 + 128 * t_, np=128, c0=0, c1=64, mb0=512 - 128 * t_) for t_ in range(4)]
                  tl.append(dict(kidx=3072, np=64, c0=0, c1=64, mb0=0))
                  attend(64, tl, q_band(2048), oaT, 2048, None, va)
                  for _ in eg:
                      pass
                  op('dve', lambda: nc.vector.memset(obT[:, 2112:TOWN], 0.0), A=[obT])
                  op('dve', lambda: nc.vector.memset(oaT[:, 2112:TOWN], 0.0), A=[oaT])
                  dma('sp', obT_s[h, :, :], obT[:], R=[obT], A=[B_obT], sb=obT)
                  dma('sp', oaT_s[h, :, :], oaT[:], R=[oaT], A=[B_oaT], sb=oaT)
              kb.barrier()

        with contextlib.ExitStack() as pc:
          if stage >= 3:
              TC = 768
              acc = sb("acc", [128, 6, D], F32, pc)
              xnT = sb("xnTc", [128, 16, TC], BF16, pc)
              oaTt = sb("oaTt", [128, 8, TC], BF16, pc)
              obTt = sb("obTt", [128, 8, TC], BF16, pc)
              mixT = sb("mixT", [128, 16, TC], BF16, pc)
              uTb = [oaTt, obTt]
              fs = dict(i=0, j=0,
                        xnb=[sb("xnbc0", [128, D], BF16, pc)] * 2,
                        ss=[sb(f"fssc{i}", [128, 4], F32, pc) for i in range(2)],
                        pst=[ps(f"pstc{i}", [128, 1024], BF16, pc) for i in range(2)])
              sg = [sb(f"sg{i}", [128, 512], F32, pc) for i in range(4)]
              gtmp = sb("gtmp", [128, 4, TC], F32, pc)
              pq = [ps(f"pq{i}", [128, 512], F32, pc) for i in range(6)]
              cq = dict(p=0, s=0, u=0)
              ctiles = [list(range(0, 6)), list(range(6, 12)), list(range(12, 17))]
              accg = [Buf(f"accg{i}") for i in range(6)]

              def segs(T):
                  return [(s0, min(512, T - s0)) for s0 in range(0, T, 512)]

              def nextp():
                  p = pq[cq['p'] % 6]
                  cq['p'] += 1
                  return p

              def csched():
                  L = []
                  for fb in range(4):
                      L.append(('ga', fb)); L.append(('pa', fb)); L.append(('gb', fb)); L.append(('pb', fb))
                  for cb in range(4):
                      L.append(('wo', cb))
                  for f in range(16):
                      L.append(('up', f)); L.append(('dn', f))
                  return L
              sched = [(ti, k, i) for ti in range(len(ctiles)) for (k, i) in csched()]
              loaded = {}

              def issue(idx):
                  if idx >= len(sched):
                      return
                  ti, k, i = sched[idx]
                  if k == 'ga':
                      loaded[idx] = wload(wsrc(w_in, 0, 16, 5184 + 512 * i, 512), 16, 512)
                  elif k == 'gb':
                      loaded[idx] = wload(wsrc(w_in, 0, 16, 7232 + 512 * i, 512), 16, 512)
                  elif k == 'pa':
                      loaded[idx] = wload(wsrc(w_pa, 0, 8, 512 * i, 512), 8, 512)
                  elif k == 'pb':
                      loaded[idx] = wload(wsrc(w_pb, 0, 8, 512 * i, 512), 8, 512)
                  elif k == 'wo':
                      loaded[idx] = wload(wsrc(w_out, 0, 16, 512 * i, 512), 16, 512)
                  elif k == 'up':
                      loaded[idx] = wload(wsrc(w_up, 0, 16, 512 * i, 512), 16, 512)
                  else:
                      loaded[idx] = wload(wsrc(w_down, 512 * i, 4, 0, 2048), 4, 2048,
                                          view=lambda slot: slot[:].rearrange("p k c -> p (k c)").rearrange("p (k n) -> p k n", k=4))
              freeslots = list(wring)
              nxt = [0]
              sidx = [0]

              def pump():
                  while nxt[0] < len(sched) and freeslots:
                      wstate['force'] = freeslots.pop(0)
                      issue(nxt[0])
                      nxt[0] += 1
                  wstate['force'] = None

              def getw():
                  s = loaded.pop(sidx[0])
                  sidx[0] += 1
                  return s

              def release(slot):
                  freeslots.append(slot)
                  pump()
              pump()

              for ti, groups in enumerate(ctiles):
                  G = len(groups)
                  T = G * 128
                  sg_ = segs(T)
                  load_Gbig(g_mix)
                  for li, g in enumerate(groups):
                      dma('sp', acc[:, li, :], xo[g * 128:(g + 1) * 128, :], W=[accg[li]], sb=accg[li])
                  t0 = groups[0] * 128
                  dma('sp', oaTt[:, :, 0:T], oaT_s[:, :, t0:t0 + T].rearrange("h p t -> p h t"), R=[B_oaT], W=[oaTt], sb=oaTt)
                  dma('sp', obTt[:, :, 0:T], obT_s[:, :, t0:t0 + T].rearrange("h p t -> p h t"), R=[B_obT], W=[obTt], sb=obTt)
                  for li, g in enumerate(groups):
                      frontend(fs, accg[li], acc[:, li, :], xnT, li, T)
                  for fb in range(4):
                      for half_ in range(2):
                          wg_, wp_ = getw(), getw()
                          actp = oaTt if half_ == 0 else obTt
                          for cc in range(4):
                              fc = fb * 4 + cc
                              for (s0, sn) in sg_:
                                  pg_, pp_ = nextp(), nextp()
                                  for (pp, ws, act_, nk) in ((pg_, wg_, xnT, 16), (pp_, wp_, actp, 8)):
                                      def mm(pp=pp, ws=ws, act_=act_, nk=nk):
                                          ins = None
                                          for k in range(nk):
                                              ins = nc.tensor.matmul(pp[:, 0:sn], lhsT=ws[:, k, cc * 128:(cc + 1) * 128],
                                                                     rhs=act_[:, k, s0:s0 + sn], start=(k == 0), stop=(k == nk - 1))
                                          return ins
                                      op('pe', mm, R=[ws, act_], W=[pp])
                                  s1 = sg[cq['s'] % 4]
                                  cq['s'] += 1
                                  op('act', lambda: nc.scalar.activation(out=s1[:, 0:sn], in_=pg_[:, 0:sn], func=AF.Sigmoid),
                                     R=[pg_], W=[s1])
                                  if half_ == 0:
                                      op('dve', lambda: nc.vector.tensor_tensor(out=gtmp[:, cc, s0:s0 + sn], in0=s1[:, 0:sn],
                                                                                in1=pp_[:, 0:sn], op=ALU.mult),
                                         R=[pp_, s1], A=[gtmp])
                                  else:
                                      op('dve', lambda: nc.vector.tensor_tensor(out=s1[:, 0:sn], in0=s1[:, 0:sn], in1=pp_[:, 0:sn],
                                                                                op=ALU.mult), R=[pp_, s1], W=[s1])
                                      op('dve', lambda: nc.vector.tensor_tensor(out=mixT[:, fc, s0:s0 + sn], in0=s1[:, 0:sn],
                                                                                in1=gtmp[:, cc, s0:s0 + sn], op=ALU.add),
                                         R=[s1, gtmp], A=[mixT])
                          release(wg_)
                          release(wp_)
                  for cb in range(4):
                      wo = getw()
                      for li in range(G):
                          pp = nextp()

                          def mm(pp=pp, li=li, wo=wo):
                              ins = None
                              for k in range(16):
                                  ins = nc.tensor.matmul(pp[:], lhsT=mixT[:, k, li * 128:(li + 1) * 128], rhs=wo[:, k, :],
                                                         start=(k == 0), stop=(k == 15))
                              return ins
                          op('pe', mm, R=[mixT, wo], W=[pp])
                          a_ = acc[:, li, cb * 512:(cb + 1) * 512]
                          op('dve', lambda a_=a_, pp=pp: nc.vector.tensor_tensor(out=a_, in0=a_, in1=pp[:], op=ALU.add),
                             R=[pp], W=[accg[li]])
                      release(wo)
                  load_Gbig(g_ffn)
                  for li in range(G):
                      frontend(fs, accg[li], acc[:, li, :], xnT, li, T)
                  for f in range(16):
                      wu, wd = getw(), getw()
                      u_b = uTb[cq['u'] % 2]
                      u_ = u_b[:, 0:4, :]
                      cq['u'] += 1
                      for cc in range(4):
                          for (s0, sn) in sg_:
                              pp = nextp()

                              def mm(pp=pp, cc=cc, s0=s0, sn=sn):
                                  ins = None
                                  for k in range(16):
                                      ins = nc.tensor.matmul(pp[:, 0:sn], lhsT=wu[:, k, cc * 128:(cc + 1) * 128],
                                                             rhs=xnT[:, k, s0:s0 + sn], start=(k == 0), stop=(k == 15))
                                  return ins
                              op('pe', mm, R=[wu, xnT], W=[pp])
                              s1 = sg[cq['s'] % 4]
                              cq['s'] += 1
                              op('act', lambda: nc.scalar.activation(out=s1[:, 0:sn], in_=pp[:, 0:sn], func=AF.Relu),
                                 R=[pp], W=[s1])
                              op('dve', lambda: nc.vector.tensor_tensor(out=u_[:, cc, s0:s0 + sn], in0=s1[:, 0:sn],
                                                                        in1=s1[:, 0:sn], op=ALU.mult), R=[s1], A=[u_b])
                      release(wu)
                      wdv = wd[:].rearrange("p k c -> p (k c)").rearrange("p (k n) -> p k n", k=4)
                      for li in range(G):
                          for cb in range(4):
                              pp = nextp()

                              def mm(pp=pp, li=li, cb=cb):
                                  ins = None
                                  for k in range(4):
                                      ins = nc.tensor.matmul(pp[:], lhsT=u_[:, k, li * 128:(li + 1) * 128],
                                                             rhs=wdv[:, k, cb * 512:(cb + 1) * 512], start=(k == 0), stop=(k == 3))
                                  return ins
                              op('pe', mm, R=[u_b, wd], W=[pp])
                              a_ = acc[:, li, cb * 512:(cb + 1) * 512]
                              op('dve', lambda a_=a_, pp=pp: nc.vector.tensor_tensor(out=a_, in0=a_, in1=pp[:], op=ALU.add),
                                 R=[pp], W=[accg[li]])
                          if f == 15:
                              g = groups[li]
                              dma('sp', y_d[g * 128:(g + 1) * 128, :], acc[:, li, :], R=[accg[li]], A=[B_out], sb=accg[li])
                      release(wd)
              kb.barrier()
        kb.barrier(engines=['sp'])
    return nc


_CACHE = {}


def _rope_tables(pos):
    half = 32
    freqs = (np.float32(10000.0) ** (-(np.arange(half, dtype=np.float32) / np.float32(half)))).astype(np.float32)
    ang = pos.astype(np.float32)[:, None] * freqs[None, :]
    return np.cos(ang).astype(np.float32), np.sin(ang).astype(np.float32)


def kernel(x_prompt, x_sample, cache_a_k, cache_a_v, cache_mla_ckv, cache_mla_krope,
           norm_mix_g, w_in, g_aq, g_ak, rel_bias, g_kv, g_kr, g_qn, g_qr, g_kn,
           w_kv_b, w_pa, w_pb, w_out, norm_ffn_g, w_up, w_down):
    import os
    stage = float(os.environ.get("KSTAGE", "99"))
    if 'nc' not in _CACHE:
        _CACHE['nc'] = build_program(stage)
    nc = _CACHE['nc']
    in_maps = _prep(x_prompt, x_sample, cache_a_k, cache_a_v, cache_mla_ckv, cache_mla_krope,
                    norm_mix_g, w_in, g_aq, g_ak, rel_bias, g_kv, g_kr, g_qn, g_qr, g_kn,
                    w_kv_b, w_pa, w_pb, w_out, norm_ffn_g, w_up, w_down)
    ncores = int(os.environ.get("KCORES", "8"))
    res = run_bass_kernel_spmd(nc, in_maps[:ncores], core_ids=list(range(ncores)))
    R = list(res.results) + [res.results[0]] * (8 - ncores)
    return _assemble(R)


def _prep(x_prompt, x_sample, cache_a_k, cache_a_v, cache_mla_ckv, cache_mla_krope,
          norm_mix_g, w_in, g_aq, g_ak, rel_bias, g_kv, g_kr, g_qn, g_qr, g_kn,
          w_kv_b, w_pa, w_pb, w_out, norm_ffn_g, w_up, w_down):
    f = lambda a: np.ascontiguousarray(np.asarray(a, dtype=np.float32))
    x_prompt, x_sample = f(x_prompt), f(x_sample)
    idx = np.clip(np.arange(768) - 127, -63, 256) + 63
    tabext = f(np.asarray(rel_bias)[0][:, idx])
    shared = dict(
        w_in=f(w_in[0]), w_kv_b=f(w_kv_b[0]), w_pa=f(w_pa[0]), w_pb=f(w_pb[0]), w_out=f(w_out[0]),
        w_up=f(w_up[0]), w_down=f(w_down[0]),
        g_mix=f(norm_mix_g[0]).reshape(1, D), g_ffn=f(norm_ffn_g[0]).reshape(1, D),
        g_aq=f(g_aq[0]).reshape(1, 128), g_ak=f(g_ak[0]).reshape(1, 128), g_kv=f(g_kv[0]).reshape(1, 512),
        g_kr=f(g_kr[0]).reshape(1, 64), g_qn=f(g_qn[0]).reshape(1, 128), g_qr=f(g_qr[0]).reshape(1, 64),
        g_kn=f(g_kn[0]).reshape(128, 1), tabext=tabext,
        identb=np.eye(128, dtype=np.float32), identf=np.eye(128, dtype=np.float32),
    )
    cos_all, sin_all = _rope_tables(np.arange(4096))
    in_maps = []
    for c in range(8):
        b, half = c // 2, c % 2
        xo = np.zeros((TOWN, D), np.float32)
        xo[0:2048] = x_prompt[b, half * 2048:(half + 1) * 2048]
        xo[2048:2112] = x_sample[c]
        xp = np.zeros((NPRE, D), np.float32)
        if half == 1:
            xp[:] = x_prompt[b, 0:2048]
        cos_o = np.zeros((TOWN, 32), np.float32)
        sin_o = np.zeros((TOWN, 32), np.float32)
        cos_o[0:2048] = cos_all[half * 2048:(half + 1) * 2048]
        sin_o[0:2048] = sin_all[half * 2048:(half + 1) * 2048]
        cos_o[2048:2112] = cos_all[2048:2112]
        sin_o[2048:2112] = sin_all[2048:2112]
        m = dict(shared)
        m.update(
            xo=xo, xp=xp,
            cak=f(cache_a_k[0, c]).reshape(512, 1024), cav=f(cache_a_v[0, c]).reshape(512, 1024),
            cckv=f(cache_mla_ckv[0, c]), ckr=f(cache_mla_krope[0, c]),
            cos_o=cos_o, sin_o=sin_o, cos_p=cos_all[0:2048].copy(), sin_p=sin_all[0:2048].copy(),
            pmask=np.full((128, 1), 0.0 if half == 1 else NEG, np.float32),
        )
        in_maps.append(m)
    return in_maps


def _assemble(R):
    yp = np.zeros((4, 4096, D), np.float32)
    ys = np.zeros((8, 64, D), np.float32)
    akp = np.zeros((1, 4, 512, 8, 128), np.float32)
    avp = np.zeros((1, 4, 512, 8, 128), np.float32)
    ckp = np.zeros((1, 4, 4096, 512), np.float32)
    krp = np.zeros((1, 4, 4096, 64), np.float32)
    aks = np.zeros((1, 8, 64, 8, 128), np.float32)
    avs = np.zeros((1, 8, 64, 8, 128), np.float32)
    cks = np.zeros((1, 8, 64, 512), np.float32)
    krs = np.zeros((1, 8, 64, 64), np.float32)
    for c in range(8):
        b, half = c // 2, c % 2
        r = R[c]
        yp[b, half * 2048:(half + 1) * 2048] = r["y"][0:2048]
        ys[c] = r["y"][2048:2112]
        ckp[0, b, half * 2048:(half + 1) * 2048] = r["o_ckv"][0:2048]
        krp[0, b, half * 2048:(half + 1) * 2048] = r["o_kr"][0:2048]
        cks[0, c] = r["o_ckv"][2048:2112]
        krs[0, c] = r["o_kr"][2048:2112]
        if half == 1:
            akp[0, b] = r["o_ak"][0:512].reshape(512, 8, 128)
            avp[0, b] = r["o_av"][0:512].reshape(512, 8, 128)
        aks[0, c] = r["o_ak"][512:576].reshape(64, 8, 128)
        avs[0, c] = r["o_av"][512:576].reshape(64, 8, 128)
    return (yp, ys, akp, avp, ckp, krp, aks, avs, cks, krs)
```

```python
import contextlib
import numpy as np
import concourse.bass as bass
import concourse.mybir as mybir
from concourse.bass_utils import run_bass_kernel_spmd

F32 = mybir.dt.float32
BF16 = mybir.dt.bfloat16
AF = mybir.ActivationFunctionType
ALU = mybir.AluOpType
AX = mybir.AxisListType

D = 2048
DIN = 9280
NGO = 17
TOWN = NGO * 128
NPRE = 2048
EPS = 1e-6
BAND_SCALE = 128 ** -0.5
MLA_SCALE = 192 ** -0.5
NEG = -30000.0
NKEY_M = 6272
NKEY_B = 3200


class Buf:
    def __init__(self, name, t=None):
        self.name = name
        self.t = t
        self.w = {}
        self.r = {}
        self.dkey = None
        self.dcnt = 0

    def __getitem__(self, idx):
        return self.t[idx]


class KB:
    LIMIT = 20000

    def __init__(self, nc, es):
        self.nc = nc
        self.es = es
        self.eng = dict(pe=nc.tensor, act=nc.scalar, dve=nc.vector, pool=nc.gpsimd, sp=nc.sync)
        self.sems = {}
        self.final = {}
        self.ekey = {}
        self.ecnt = {}
        self.nsem = 0
        for e in self.eng:
            self._roll(e)
        self.waited = {e: {} for e in self.eng}
        self.dbufs = []
        self.pe_pending = False

    def _newsem(self, name):
        h = self.es.enter_context(self.nc.semaphore(name))
        self.sems[name] = h
        self.nsem += 1
        return name

    def _roll(self, e):
        if e in self.ekey:
            self.final[self.ekey[e]] = self.ecnt[e]
        self.ekey[e] = self._newsem(f"e_{e}_{self.nsem}")
        self.ecnt[e] = 0

    def _wait(self, e, need):
        wd = self.waited[e]
        for key, val in need.items():
            if val <= 0:
                continue
            if key == self.ekey['pe'] and val > self.ecnt['pe']:
                raise RuntimeError("wait on unmarked PE op")
            if wd.get(key, 0) < val:
                self.eng[e].wait_ge(self.sems[key], val)
                wd[key] = val

    @staticmethod
    def _need(R, W, A):
        need = {}

        def upd(d):
            for k, v in d.items():
                if need.get(k, 0) < v:
                    need[k] = v
        for b in R:
            upd(b.w)
        for b in W:
            upd(b.w)
            upd(b.r)
        for b in A:
            upd(b.r)
        return need

    def _record(self, key, val, R, W, A):
        for b in R:
            if b.r.get(key, 0) < val:
                b.r[key] = val
        for b in W:
            b.w = {key: val}
            b.r = {}
        for b in A:
            if b.w.get(key, 0) < val:
                b.w[key] = val
            b.r = {}

    def op(self, e, fn, R=(), W=(), A=(), mark=True):
        if not (e == 'pe' and self.pe_pending):
            if self.ecnt[e] >= self.LIMIT:
                self._roll(e)
        need = self._need(R, W, A)
        if e == 'pe':
            need = {k: v for k, v in need.items() if not k.startswith('e_pe_')}
        self._wait(e, need)
        ins = fn()
        key = self.ekey[e]
        if mark:
            self.ecnt[e] += 1
            ins.then_inc(self.sems[key], 1)
            val = self.ecnt[e]
            if e == 'pe':
                self.pe_pending = False
        else:
            val = self.ecnt[e] + 1
            if e == 'pe':
                self.pe_pending = True
        self._record(key, val, R, W, A)
        return ins

    def dma(self, e, out, in_, R=(), W=(), A=(), sb=None):
        need = self._need(R, W, A)
        self._wait(e, need)
        if sb.dkey is None:
            sb.dkey = {}
            sb.dcnt = {}
            self.dbufs.append(sb)
        if e not in sb.dkey:
            sb.dkey[e] = self._newsem("d_" + e + "_" + sb.name)
            sb.dcnt[e] = 0
        ins = self.eng[e].dma_start(out=out, in_=in_)
        sb.dcnt[e] += 16
        ins.then_inc(self.sems[sb.dkey[e]], 16)
        self._record(sb.dkey[e], sb.dcnt[e], R, W, A)
        return ins

    def barrier(self, engines=None):
        assert not self.pe_pending
        need = dict(self.final)
        for e in self.eng:
            need[self.ekey[e]] = self.ecnt[e]
        for b in self.dbufs:
            for e_ in b.dkey:
                need[b.dkey[e_]] = b.dcnt[e_]
        for e in (engines or self.eng):
            self._wait(e, dict(need))


def _runs(groups, mapfn):
    runs = []
    for li, g in enumerate(groups):
        d = mapfn(g)
        if d is None:
            continue
        if runs and runs[-1][0] + runs[-1][1] == li and runs[-1][2] + runs[-1][1] * 128 == d:
            runs[-1][1] += 1
        else:
            runs.append([li, 1, d])
    return runs


def build_program(stage=99):
    nc = bass.Bass("TRN2", target_bir_lowering=False)

    def din(name, shape, dt=F32):
        return nc.dram_tensor(name, list(shape), dt, kind="ExternalInput").ap()

    def dout(name, shape):
        return nc.dram_tensor(name, list(shape), F32, kind="ExternalOutput").ap()

    def dscr(name, shape, dt=BF16):
        return nc.dram_tensor(name, list(shape), dt, kind="Internal").ap()

    xo = din("xo", [TOWN, D])
    xp = din("xp", [NPRE, D])
    cak = din("cak", [512, 1024])
    cav = din("cav", [512, 1024])
    cckv = din("cckv", [2048, 512])
    ckr = din("ckr", [2048, 64])
    w_in = din("w_in", [D, DIN])
    w_kv_b = din("w_kv_b", [512, 2048])
    w_pa = din("w_pa", [1024, D])
    w_pb = din("w_pb", [1024, D])
    w_out = din("w_out", [D, D])
    w_up = din("w_up", [D, 4 * D])
    w_down = din("w_down", [4 * D, D])
    g_mix = din("g_mix", [1, D])
    g_ffn = din("g_ffn", [1, D])
    g_aq = din("g_aq", [1, 128])
    g_ak = din("g_ak", [1, 128])
    g_kv = din("g_kv", [1, 512])
    g_kr = din("g_kr", [1, 64])
    g_qn = din("g_qn", [1, 128])
    g_qr = din("g_qr", [1, 64])
    g_kn = din("g_kn", [128, 1])
    tabext = din("tabext", [8, 768])
    cos_o = din("cos_o", [TOWN, 32])
    sin_o = din("sin_o", [TOWN, 32])
    cos_p = din("cos_p", [NPRE, 32])
    sin_p = din("sin_p", [NPRE, 32])
    pmask_d = din("pmask", [128, 1])
    identb_d = din("identb", [128, 128])
    identf_d = din("identf", [128, 128])

    y_d = dout("y", [TOWN, D])
    oak_d = dout("o_ak", [640, 1024])
    oav_d = dout("o_av", [640, 1024])
    ockv_d = dout("o_ckv", [TOWN, 512])
    okr_d = dout("o_kr", [TOWN, 64])

    aqT_s = dscr("aqT_s", [8, 128, TOWN])
    qnT_s = dscr("qnT_s", [8, 128, TOWN])
    qrT_s = dscr("qrT_s", [4, 128, TOWN])
    akT_s = dscr("akT_s", [8, 128, NKEY_B])
    av_s = dscr("av_s", [NKEY_B, 1024])
    ckvT_s = dscr("ckvT_s", [4, 128, NKEY_M])
    krT_s = dscr("krT_s", [128, NKEY_M])
    oaT_s = dscr("oaT_s", [8, 128, TOWN])
    MB_s = dscr("MB_s", [8, 128, 640], F32)
    obT_s = dscr("obT_s", [8, 128, TOWN])

    with contextlib.ExitStack() as es:
        kb = KB(nc, es)
        op, dma = kb.op, kb.dma

        def sb(name, shape, dt, stack=None):
            t = (stack or es).enter_context(nc.sbuf_tensor("sb_" + name, list(shape), dt))
            return Buf(name, t)

        def ps(name, shape, dt, stack=None):
            t = (stack or es).enter_context(nc.psum_tensor("ps_" + name, list(shape), dt))
            return Buf(name, t)

        B_aqT, B_qnT, B_qrT, B_akT, B_av = Buf("s_aqT"), Buf("s_qnT"), Buf("s_qrT"), Buf("s_akT"), Buf("s_av")
        B_ckvT, B_krT, B_oaT, B_obT = Buf("s_ckvT"), Buf("s_krT"), Buf("s_oaT"), Buf("s_obT")
        B_out = Buf("outs")

        identb = sb("identb", [128, 128], BF16)
        identf = sb("identf", [128, 128], F32)
        onesb = sb("onesb", [128, 128], BF16)
        onesf = sb("onesf", [128, 128], F32)
        epsb = sb("epsb", [128, 1], F32)
        zerob = sb("zerob", [128, 1], F32)
        pmask = sb("pmaskb", [128, 1], F32)
        Gbig = sb("Gbig", [128, D], F32)
        gkn = sb("gknb", [128, 1], F32)
        wring = [sb(f"wring{i}", [128, 16, 512], BF16) for i in range(3)]

        dma('pool', identb[:], identb_d[:, :], W=[identb], sb=identb)
        dma('sp', identf[:], identf_d[:, :], W=[identf], sb=identf)
        dma('sp', pmask[:], pmask_d[:, :], W=[pmask], sb=pmask)
        dma('sp', gkn[:], g_kn[:, :], W=[gkn], sb=gkn)
        op('dve', lambda: nc.vector.memset(onesb[:], 1.0), W=[onesb])
        op('dve', lambda: nc.vector.memset(onesf[:], 1.0), W=[onesf])
        op('dve', lambda: nc.vector.memset(epsb[:], EPS), W=[epsb])
        op('dve', lambda: nc.vector.memset(zerob[:], 0.0), W=[zerob])

        B_MB = Buf("s_MB")
        mbsem = Buf("mbsem")
        for kk in range(128):
            dma('sp', MB_s[:, kk, :], tabext[:, 127 - kk:127 - kk + 640], A=[B_MB], sb=mbsem)

        def load_gain(dst, g, n, rep):
            src = bass.AP(g.tensor, 0, [[0, 128], [0, rep], [1, n]])
            dma('sp', dst[:].rearrange("p (r n) -> p r n", r=rep), src, W=[dst], sb=dst)

        def load_Gbig(g):
            dma('sp', Gbig[:], g.partition_broadcast(128)[:, 0, :], W=[Gbig], sb=Gbig)

        wstate = {'n': 0}

        def wload(src_ap, nk, ncols, view=None):
            if wstate.get('force') is not None:
                slot = wstate['force']
            else:
                slot = wring[wstate['n'] % 3]
                wstate['n'] += 1
            if isinstance(src_ap, list):
                for i, (c0, n, sp_) in enumerate(src_ap):
                    if i == 0:
                        dma('pool', slot[:, 0:nk, c0:c0 + n], sp_, W=[slot], sb=slot)
                    else:
                        dma('pool', slot[:, 0:nk, c0:c0 + n], sp_, A=[slot], sb=slot)
                return slot
            dst = slot[:, 0:nk, 0:ncols] if view is None else view(slot)
            dma('pool', dst, src_ap, W=[slot], sb=slot)
            return slot

        def wsrc(w, r0, nk, c0, ncols):
            return w[r0:r0 + nk * 128, c0:c0 + ncols].rearrange("(k p) n -> p k n", p=128)

        def frontend(fs, src_buf, src_ap, xnT, loc, T):
            xnb = fs['xnb'][fs['i'] % 2]
            ssb = fs['ss'][fs['i'] % 2]
            fs['i'] += 1
            op('act', lambda: nc.scalar.activation(out=xnb[:], in_=src_ap, func=AF.Square,
                                                   accum_out=ssb[:, 0:1]), R=[src_buf], W=[xnb, ssb])
            op('act', lambda: nc.scalar.activation(out=ssb[:, 1:2], in_=ssb[:, 0:1], func=AF.Sqrt,
                                                   bias=epsb[:], scale=1.0 / D), R=[ssb, epsb], A=[ssb])
            op('dve', lambda: nc.vector.reciprocal(out=ssb[:, 2:3], in_=ssb[:, 1:2]), R=[ssb], A=[ssb])
            op('dve', lambda: nc.vector.scalar_tensor_tensor(out=xnb[:], in0=src_ap, scalar=ssb[:, 2:3],
                                                             in1=Gbig[:], op0=ALU.mult, op1=ALU.mult),
               R=[src_buf, ssb, Gbig], W=[xnb])
            for half in range(2):
                pt = fs['pst'][fs['j'] % 2]
                fs['j'] += 1

                def mm(pt=pt, half=half):
                    ins = None
                    for j in range(8):
                        k = half * 8 + j
                        ins = nc.tensor.transpose(out=pt[:, j * 128:(j + 1) * 128],
                                                  in_=xnb[:, k * 128:(k + 1) * 128], identity=identb[:])
                    return ins
                op('pe', mm, R=[xnb, identb], W=[pt])
                dst = xnT[:, half * 8:half * 8 + 8, loc * 128:(loc + 1) * 128]
                src = pt[:].rearrange("p (k t) -> p k t", k=8)
                if half == 0:
                    op('act', lambda: nc.scalar.activation(out=dst, in_=src, func=AF.Copy), R=[pt], A=[xnT])
                else:
                    op('dve', lambda: nc.vector.tensor_copy(out=dst, in_=src), R=[pt], A=[xnT])

        with contextlib.ExitStack() as pa:
            TA = 768
            gt_aq = sb("gt_aq", [128, 512], F32, pa)
            gt_ak = sb("gt_ak", [128, 512], F32, pa)
            gt_qn = sb("gt_qn", [128, 512], F32, pa)
            gt_qr = sb("gt_qr", [128, 512], F32, pa)
            gt_kv = sb("gt_kv", [128, 512], F32, pa)
            gt_kr = sb("gt_kr", [128, 64], F32, pa)
            load_gain(gt_aq, g_aq, 128, 4)
            load_gain(gt_ak, g_ak, 128, 4)
            load_gain(gt_qn, g_qn, 128, 4)
            load_gain(gt_qr, g_qr, 64, 8)
            load_gain(gt_kv, g_kv, 512, 1)
            load_gain(gt_kr, g_kr, 64, 1)
            xst = [sb(f"xst{i}", [128, D], F32, pa) for i in range(2)]
            fs = dict(i=0, j=0,
                      xnb=[sb(f"xnb{i}", [128, D], BF16, pa) for i in range(2)],
                      ss=[sb(f"fss{i}", [128, 4], F32, pa) for i in range(2)],
                      pst=[ps(f"pst{i}", [128, 1024], BF16, pa) for i in range(2)])
            xnTs = [sb(f"xnT{i}", [128, 16, TA], BF16, pa) for i in range(2)]
            psm = [ps(f"psm{i}", [128, 512], F32, pa) for i in range(4)]
            ptr = [ps(f"ptr{i}", [128, 512], BF16, pa) for i in range(2)]
            sqr = [sb(f"sqj{i}", [128, 512], F32, pa) for i in range(2)]
            ssr = [sb(f"ssr{i}", [128, 24], F32, pa) for i in range(4)]
            nfr = [sb(f"nf{i}", [128, 512], F32, pa) for i in range(4)]
            nbr = [sb(f"nb{i}", [128, 512], BF16, pa) for i in range(4)]
            rtmp = sb("rtmp", [128, 4, 256], F32, pa)
            tst = [sb(f"tst{i}", [128, 4, TA], BF16, pa) for i in range(2)]
            csts = [sb(f"cst{i}", [128, 8, 2, 32], F32, pa) for i in range(2)]
            cnt = dict(ps=0, tr=0, r=0, tst=0)

            load_Gbig(g_mix)

            def u_plain(c0, n):
                return lambda: (wsrc(w_in, 0, 16, c0, n), 16, n, None)

            def u_qn(h0):
                def f():
                    src = [(j * 128, 128, bass.AP(w_in.tensor, 3072 + 192 * (h0 + j), [[DIN, 128], [128 * DIN, 16], [1, 128]]))
                           for j in range(4)]
                    return (src, 16, 512, None)
                return f

            def u_qr():
                def f():
                    src = [(j * 64, 64, bass.AP(w_in.tensor, 3072 + 128 + 192 * j, [[DIN, 128], [128 * DIN, 16], [1, 64]]))
                           for j in range(8)]
                    return (src, 16, 512, None)
                return f

            U = {
                'aq0': ('aq', u_plain(0, 512), 0), 'aq1': ('aq', u_plain(512, 512), 4),
                'ak0': ('ak', u_plain(1024, 512), 0), 'ak1': ('ak', u_plain(1536, 512), 4),
                'av0': ('av', u_plain(2048, 512), 0), 'av1': ('av', u_plain(2560, 512), 4),
                'qn0': ('qn', u_qn(0), 0), 'qn1': ('qn', u_qn(4), 4),
                'qr': ('qr', u_qr(), 0),
                'ckv': ('ckv', u_plain(4608, 512), 0),
                'kr': ('kr', u_plain(5120, 64), 0),
            }
            own_units = ['aq0', 'aq1', 'ak0', 'ak1', 'av0', 'av1', 'qn0', 'qn1', 'qr', 'ckv', 'kr']
            tiles = [
                ('p', list(range(0, 6)), ['ckv', 'kr']),
                ('p', list(range(6, 12)), ['ckv', 'kr']),
                ('p', list(range(12, 16)), ['ckv', 'kr', 'ak0', 'ak1', 'av0', 'av1']),
                ('o', list(range(0, 6)), own_units),
                ('o', list(range(6, 12)), own_units),
                ('o', list(range(12, 17)), own_units),
            ]

            def map_band(src, g):
                if src == 'p':
                    return (g - 12) * 128 if g >= 12 else None
                return 512 + g * 128 if g < 16 else 3072

            def map_mla(src, g):
                if src == 'p':
                    return g * 128
                return 2048 + g * 128 if g < 16 else 6144

            def map_own(src, g):
                return g * 128

            sched = [(ti, un) for ti, t in enumerate(tiles) for un in t[2]]
            loaded = {}

            def issue(idx):
                if idx < len(sched):
                    ti, un = sched[idx]
                    src, nk, ncols, view = U[un][1]()
                    loaded[idx] = wload(src, nk, ncols, view)
            sidx = 0
            if stage < 0.1:
                tiles = []
            elif stage < 0.4:
                tiles = tiles[:1]
            elif stage < 0.45:
                tiles = tiles[:2]
            elif stage < 0.55:
                tiles = tiles[:3]
                tiles[2] = (tiles[2][0], tiles[2][1], ['ckv', 'kr'])
            elif stage < 0.61:
                tiles = tiles[:3]
                tiles[2] = (tiles[2][0], tiles[2][1], ['ckv', 'kr', 'ak0'])
            elif stage < 0.63:
                tiles = tiles[:3]
                tiles[2] = (tiles[2][0], tiles[2][1], ['ckv', 'kr', 'ak0', 'ak1'])
            elif stage < 0.65:
                tiles = tiles[:3]
                tiles[2] = (tiles[2][0], tiles[2][1], ['ckv', 'kr', 'av0'])
            elif stage < 0.7:
                tiles = tiles[:3]
            import os as _os
            if _os.environ.get("KTILES"):
                tiles = tiles[:int(_os.environ["KTILES"])]
            if _os.environ.get("KOWN"):
                ou = _os.environ["KOWN"].split(",")
                tiles = [(a, b, (ou if a == 'o' else c)) for (a, b, c) in tiles]
            sched = [(ti, un) for ti, t in enumerate(tiles) for un in t[2]]
            issue(0)
            issue(1)
            def tile_front(ti):
                src, groups, units = tiles[ti]
                xsrc = xp if src == 'p' else xo
                G = len(groups)
                T = G * 128
                cst_ = csts[ti % 2]
                xnT_ = xnTs[ti % 2]
                csrc_c = (cos_p if src == 'p' else cos_o)
                csrc_s = (sin_p if src == 'p' else sin_o)
                g0 = groups[0]
                dma('sp', cst_[:, 0:G, 0, :], csrc_c[g0 * 128:(g0 + G) * 128, :].rearrange("(g p) c -> p g c", p=128),
                    W=[cst_], sb=cst_)
                dma('sp', cst_[:, 0:G, 1, :], csrc_s[g0 * 128:(g0 + G) * 128, :].rearrange("(g p) c -> p g c", p=128),
                    A=[cst_], sb=cst_)
                for li, g in enumerate(groups):
                    xs = xst[(fs['i']) % 2]
                    dma('sp', xs[:], xsrc[g * 128:(g + 1) * 128, :], W=[xs], sb=xs)
                    frontend(fs, xs, xs[:], xnT_, li, T)

            pend = []

            def do_transposes(nb, li, nblk, ts_, final_cb):
                pt = ptr[cnt['tr'] % 2]
                cnt['tr'] += 1

                def mm():
                    ins = None
                    for j in range(nblk):
                        ins = nc.tensor.transpose(out=pt[:, j * 128:(j + 1) * 128],
                                                  in_=nb[:, j * 128:(j + 1) * 128], identity=identb[:])
                    return ins
                op('pe', mm, R=[nb, identb], W=[pt])
                dst = ts_[:, 0:nblk, li * 128:(li + 1) * 128]
                s_ = pt[:, 0:nblk * 128].rearrange("p (k t) -> p k t", k=nblk)
                op('act', lambda: nc.scalar.activation(out=dst, in_=s_, func=AF.Copy), R=[pt], A=[ts_])
                if final_cb is not None:
                    final_cb()

            def make_final(kind, src, groups, lgs, ts_, h0):
                def fin():
                    if kind in ('aq', 'qn', 'qr'):
                        dstT, Bd, mp = {'aq': (aqT_s, B_aqT, map_own), 'qn': (qnT_s, B_qnT, map_own),
                                        'qr': (qrT_s, B_qrT, map_own)}[kind]
                    elif kind == 'ak':
                        dstT, Bd, mp = akT_s, B_akT, map_band
                    elif kind == 'ckv':
                        dstT, Bd, mp = ckvT_s, B_ckvT, map_mla
                    else:
                        dstT, Bd, mp = krT_s, B_krT, map_mla
                    sub = [groups[li] for li in lgs]
                    for (ls, n, d0) in _runs(sub, lambda g: mp(src, g)):
                        l0 = lgs[ls]
                        if kind == 'kr':
                            dma('sp', dstT[:, d0:d0 + n * 128], ts_[:, 0, l0 * 128:(l0 + n) * 128],
                                R=[ts_], A=[Bd], sb=ts_)
                        else:
                            hh = h0 if kind in ('aq', 'ak', 'qn') else 0
                            dma('sp', dstT[hh:hh + 4, :, d0:d0 + n * 128].rearrange("h p t -> p h t"),
                                ts_[:, 0:4, l0 * 128:(l0 + n) * 128], R=[ts_], A=[Bd], sb=ts_)
                return fin

            if tiles:
                tile_front(0)
            for ti, (src, groups, units) in enumerate(tiles):
                G = len(groups)
                T = G * 128
                cst = csts[ti % 2]
                xnT = xnTs[ti % 2]
                for ui, un in enumerate(units):
                    if ui == min(1, len(units) - 1) and ti + 1 < len(tiles):
                        tile_front(ti + 1)
                    kind, _, h0 = U[un]
                    slot = loaded.pop(sidx)
                    issue(sidx + 2)
                    sidx += 1
                    ncols = 64 if kind == 'kr' else 512
                    if kind in ('ak', 'av') and src == 'p':
                        lgs = [li for li, g in enumerate(groups) if g >= 12]
                    else:
                        lgs = list(range(G))
                    transposed = kind != 'av'
                    if not transposed:
                        while pend:
                            do_transposes(*pend.pop(0))
                    if transposed:
                        ts_ = tst[cnt['tst'] % 2]
                        cnt['tst'] += 1
                    for li in lgs:
                        g = groups[li]
                        pm = psm[cnt['ps'] % 4]
                        cnt['ps'] += 1

                        def mm(pm=pm, li=li):
                            ins = None
                            for k in range(16):
                                ins = nc.tensor.matmul(pm[:, 0:ncols], lhsT=xnT[:, k, li * 128:(li + 1) * 128],
                                                       rhs=slot[:, k, 0:ncols], start=(k == 0), stop=(k == 15))
                            return ins
                        op('pe', mm, R=[xnT, slot], W=[pm])
                        if len(pend) >= 3:
                            do_transposes(*pend.pop(0))
                        ri = cnt['r'] % 4
                        cnt['r'] += 1
                        nf, nb, ssb = nfr[ri], nbr[ri], ssr[ri]
                        own = (src == 'o')
                        if kind == 'av':
                            need_out = own and g >= 12
                            if need_out:
                                op('dve', lambda: nc.vector.tensor_copy(out=nf[:], in_=pm[:]), R=[pm], W=[nf])
                                orow = (g - 12) * 128
                                dma('sp', oav_d[orow:orow + 128, h0 * 128:h0 * 128 + 512], nf[:], R=[nf], A=[B_out], sb=nf)
                                op('act', lambda: nc.scalar.activation(out=nb[:], in_=nf[:], func=AF.Copy), R=[nf], W=[nb])
                            else:
                                op('dve', lambda: nc.vector.tensor_copy(out=nb[:], in_=pm[:]), R=[pm], W=[nb])
                            krow = map_band(src, g)
                            dma('sp', av_s[krow:krow + 128, h0 * 128:h0 * 128 + 512], nb[:], R=[nb], A=[B_av], sb=nb)
                            continue
                        hd, nh, gt = {'aq': (128, 4, gt_aq), 'ak': (128, 4, gt_ak), 'qn': (128, 4, gt_qn),
                                      'qr': (64, 8, gt_qr), 'ckv': (512, 1, gt_kv), 'kr': (64, 1, gt_kr)}[kind]
                        nc_ = nh * hd
                        if nh == 1:
                            sq = sqr[0]
                            op('act', lambda: nc.scalar.activation(
                                out=sq[:, 0:hd], in_=pm[:, 0:hd], func=AF.Square,
                                accum_out=ssb[:, 0:1]), R=[pm], W=[sq, ssb])
                        else:
                            sq = sqr[cnt['r'] % 2]
                            op('act', lambda: nc.scalar.activation(out=sq[:, 0:nc_], in_=pm[:, 0:nc_], func=AF.Square),
                               R=[pm], W=[sq])
                            op('dve', lambda: nc.vector.tensor_reduce(
                                out=ssb[:, 0:nh], in_=sq[:, 0:nc_].rearrange("p (h d) -> p h d", h=nh),
                                axis=AX.X, op=ALU.add), R=[sq], W=[ssb])
                        op('act', lambda: nc.scalar.activation(out=ssb[:, 8:8 + nh], in_=ssb[:, 0:nh], func=AF.Sqrt,
                                                               bias=epsb[:], scale=1.0 / hd), R=[ssb, epsb], A=[ssb])
                        op('dve', lambda: nc.vector.reciprocal(out=ssb[:, 16:16 + nh], in_=ssb[:, 8:8 + nh]),
                           R=[ssb], A=[ssb])
                        rope = kind in ('qr', 'kr')
                        need_f32 = kind in ('ak', 'ckv', 'kr')
                        tgt = nf if (need_f32 or rope) else nb
                        if nh == 1:
                            op('dve', lambda: nc.vector.scalar_tensor_tensor(
                                out=tgt[:, 0:hd], in0=pm[:, 0:hd], scalar=ssb[:, 16:17], in1=gt[:, 0:hd],
                                op0=ALU.mult, op1=ALU.mult), R=[pm, ssb, gt], W=[tgt])
                        else:
                            rb = ssb[:, 16:16 + nh].unsqueeze(2).broadcast_to([128, nh, hd])
                            op('dve', lambda: nc.vector.tensor_tensor(
                                out=sq[:, 0:nc_].rearrange("p (h d) -> p h d", h=nh),
                                in0=pm[:, 0:nc_].rearrange("p (h d) -> p h d", h=nh), in1=rb, op=ALU.mult),
                               R=[pm, ssb], W=[sq])
                            op('dve', lambda: nc.vector.tensor_tensor(out=tgt[:, 0:nc_], in0=sq[:, 0:nc_], in1=gt[:, 0:nc_],
                                                                      op=ALU.mult), R=[sq, gt], W=[tgt])
                        if rope:
                            xv = nf[:, 0:nh * 64].rearrange("p (h t c) -> p h t c", h=nh, t=2)
                            x1, x2 = xv[:, :, 0, :], xv[:, :, 1, :]
                            cosb = cst[:, li, 0, :].unsqueeze(1).broadcast_to([128, nh, 32])
                            sinb = cst[:, li, 1, :].unsqueeze(1).broadcast_to([128, nh, 32])
                            tv = [rtmp[:, i, 0:nh * 32].rearrange("p (h c) -> p h c", h=nh) for i in range(4)]
                            for i, (a, b) in enumerate([(x1, cosb), (x2, sinb), (x2, cosb), (x1, sinb)]):
                                op('dve', lambda a=a, b=b, i=i: nc.vector.tensor_tensor(out=tv[i], in0=a, in1=b, op=ALU.mult),
                                   R=[nf, cst], A=[rtmp] if i else [], W=[] if i else [rtmp])
                            op('dve', lambda: nc.vector.tensor_tensor(out=x1, in0=tv[0], in1=tv[1], op=ALU.subtract),
                               R=[rtmp], A=[nf])
                            op('dve', lambda: nc.vector.tensor_tensor(out=x2, in0=tv[2], in1=tv[3], op=ALU.add),
                               R=[rtmp], A=[nf])
                        if need_f32 or rope:
                            if kind == 'kr':
                                op('act', lambda: nc.scalar.activation(out=nb[:, 0:64], in_=nf[:, 0:64], func=AF.Copy),
                                   R=[nf], W=[nb])
                                op('act', lambda: nc.scalar.activation(out=nb[:, 64:128], in_=nf[:, 0:64], func=AF.Copy),
                                   R=[nf], A=[nb])
                            else:
                                op('act', lambda: nc.scalar.activation(out=nb[:], in_=nf[:], func=AF.Copy),
                                   R=[nf], W=[nb])
                        if own and kind == 'ak' and g >= 12:
                            orow = (g - 12) * 128
                            dma('sp', oak_d[orow:orow + 128, h0 * 128:h0 * 128 + 512], nf[:], R=[nf], A=[B_out], sb=nf)
                        if own and kind == 'ckv':
                            dma('sp', ockv_d[g * 128:(g + 1) * 128, :], nf[:], R=[nf], A=[B_out], sb=nf)
                        if own and kind == 'kr':
                            dma('sp', okr_d[g * 128:(g + 1) * 128, :], nf[:, 0:64], R=[nf], A=[B_out], sb=nf)
                        nblk = 1 if kind == 'kr' else 4
                        fin = make_final(kind, src, groups, lgs, ts_, h0) if li == lgs[-1] else None
                        pend.append((nb, li, nblk, ts_, fin))
            while pend:
                do_transposes(*pend.pop(0))

            cbuf = sb("cbuf", [128, 16, 512], BF16, pa)
            kbuf = sb("kbuf", [128, 16, 128], BF16, pa)
            _kc = int(_os.environ.get('KCACHE', '7'))
            if stage >= 1:
              if _kc & 1:
                cvv = cbuf[:, 8:16, :].rearrange("p (g a) c -> p g (a c)", g=4)
                dma('pool', cvv, cav.rearrange("(g p) c -> p g c", p=128), W=[cbuf], sb=cbuf)
                dma('sp', av_s[2560:3072, :].rearrange("(g p) c -> p g c", p=128), cvv, R=[cbuf], A=[B_av], sb=cbuf)
                dma('pool', cbuf[:, 0:8, :].rearrange("p (g a) c -> p g (a c)", g=4),
                    cak.rearrange("(g p) c -> p g c", p=128), A=[cbuf], sb=cbuf)
                ckview = cbuf[:, 0:8, :].rearrange("p (g a) c -> p g (a c)", g=4)
                for half in range(2):
                    ts_ = tst[cnt['tst'] % 2]
                    cnt['tst'] += 1
                    for g in range(4):
                        pt = ptr[cnt['tr'] % 2]
                        cnt['tr'] += 1

                        def mm(pt=pt, g=g, half=half):
                            ins = None
                            for j in range(4):
                                h = half * 4 + j
                                ins = nc.tensor.transpose(out=pt[:, j * 128:(j + 1) * 128],
                                                          in_=ckview[:, g, h * 128:(h + 1) * 128], identity=identb[:])
                            return ins
                        op('pe', mm, R=[cbuf, identb], W=[pt])
                        op('act', lambda pt=pt, g=g, ts_=ts_: nc.scalar.activation(
                            out=ts_[:, 0:4, g * 128:(g + 1) * 128], in_=pt[:].rearrange("p (k t) -> p k t", k=4),
                            func=AF.Copy), R=[pt], A=[ts_])
                    dma('sp', akT_s[half * 4:half * 4 + 4, :, 2560:3072].rearrange("h p t -> p h t"),
                        ts_[:, 0:4, 0:512], R=[ts_], A=[B_akT], sb=ts_)
              if _kc & 2:
                dma('pool', cbuf[:, :, :], cckv.rearrange("(g p) c -> p g c", p=128), W=[cbuf], sb=cbuf)
                for blk in range(4):
                    ts_ = tst[cnt['tst'] % 2]
                    cnt['tst'] += 1
                    for gg in range(4):
                        g = blk * 4 + gg
                        pt = ptr[cnt['tr'] % 2]
                        cnt['tr'] += 1

                        def mm(pt=pt, g=g):
                            ins = None
                            for j in range(4):
                                ins = nc.tensor.transpose(out=pt[:, j * 128:(j + 1) * 128],
                                                          in_=cbuf[:, g, j * 128:(j + 1) * 128], identity=identb[:])
                            return ins
                        op('pe', mm, R=[cbuf, identb], W=[pt])
                        op('act', lambda pt=pt, gg=gg, ts_=ts_: nc.scalar.activation(
                            out=ts_[:, 0:4, gg * 128:(gg + 1) * 128], in_=pt[:].rearrange("p (k t) -> p k t", k=4),
                            func=AF.Copy), R=[pt], A=[ts_])
                    d0 = 4096 + blk * 512
                    dma('sp', ckvT_s[0:4, :, d0:d0 + 512].rearrange("h p t -> p h t"), ts_[:, 0:4, 0:512],
                        R=[ts_], A=[B_ckvT], sb=ts_)
              if _kc & 4:
                kst = xst[0]
                kstv = kst[:, 0:1024].rearrange("p (g c) -> p g c", c=64)
                dma('sp', kstv, ckr.rearrange("(g p) c -> p g c", p=128), W=[kst], sb=kst)
                op('act', lambda: nc.scalar.activation(out=kbuf[:, :, 0:64], in_=kstv, func=AF.Copy), R=[kst], W=[kbuf])
                op('dve', lambda: nc.vector.tensor_copy(out=kbuf[:, :, 64:128], in_=kstv), R=[kst], A=[kbuf])
                for blk in range(4):
                    ts_ = tst[cnt['tst'] % 2]
                    cnt['tst'] += 1
                    pt = ptr[cnt['tr'] % 2]
                    cnt['tr'] += 1

                    def mm(pt=pt, blk=blk):
                        ins = None
                        for j in range(4):
                            ins = nc.tensor.transpose(out=pt[:, j * 128:(j + 1) * 128],
                                                      in_=kbuf[:, blk * 4 + j, :], identity=identb[:])
                        return ins
                    op('pe', mm, R=[kbuf, identb], W=[pt])
                    op('act', lambda pt=pt, ts_=ts_: nc.scalar.activation(out=ts_[:, 0, 0:512], in_=pt[:], func=AF.Copy),
                       R=[pt], W=[ts_])
                    d0 = 4096 + blk * 512
                    dma('sp', krT_s[:, d0:d0 + 512], ts_[:, 0, 0:512], R=[ts_], A=[B_krT], sb=ts_)
            kb.barrier()

        with contextlib.ExitStack() as pb:
          if stage >= 2:
              ckvT = sb("ckvT", [128, 4, NKEY_M], BF16, pb)
              krT = sb("krT", [128, NKEY_M], BF16, pb)
              wkvb_b, knT_b, Vh_b = wring[0], wring[1], wring[2]
              wkvb = wkvb_b[:].rearrange("p k c -> p (k c)").rearrange("p (k n) -> p k n", k=4)
              hsets = [(sb(f"qn_h{i}", [128, TOWN], BF16, pb), sb(f"qr_h{i}", [128, TOWN], BF16, pb),
                        sb(f"aq_h{i}", [128, TOWN], BF16, pb), sb(f"ak_h{i}", [128, NKEY_B], BF16, pb),
                        sb(f"av_h{i}", [128, 25, 128], BF16, pb), sb(f"MB_h{i}", [128, 640], F32, pb))
                       for i in range(2)]
              qn, qr, aq, ak, av, MB = hsets[0]

              def head_loads(h):
                  qn_, qr_, aq_, ak_, av_, MB_ = hsets[h % 2]
                  dma('sp', qn_[:], qnT_s[h, :, :], R=[B_qnT], W=[qn_], sb=qn_)
                  dma('sp', qr_[:], qrT_s[h // 2, :, :], R=[B_qrT], W=[qr_], sb=qr_)
                  dma('sp', aq_[:], aqT_s[h, :, :], R=[B_aqT], W=[aq_], sb=aq_)
                  dma('sp', ak_[:], akT_s[h, :, :], R=[B_akT], W=[ak_], sb=ak_)
                  dma('sp', av_[:], av_s[:, h * 128:(h + 1) * 128].rearrange("(t p) d -> p t d", p=128),
                      R=[B_av], W=[av_], sb=av_)
                  dma('sp', MB_[:], MB_s[h, :, :], R=[B_MB], W=[MB_], sb=MB_)
              knT = knT_b[:].rearrange("p k c -> p (k c)")[:, 0:NKEY_M]
              Vh = Vh_b[:].rearrange("p k c -> p (k c)")[:, 0:NKEY_M].rearrange("p (t d) -> p t d", d=128)
              PT = [sb(f"PT{i}", [128, 512], BF16, pb) for i in range(4)]
              sbias = [sb(f"sbias{i}", [128, 512], F32, pb) for i in range(3)]
              sqfr = [sb(f"sqf{i}", [128, 512], F32, pb) for i in range(2)]
              rsfr = [sb(f"rsf{i}", [128, 512], F32, pb) for i in range(2)]
              rden = sb("rden", [128, 512], F32, pb)
              obT = sb("obT_h", [128, TOWN], BF16, pb)
              oaT = sb("oaT_h", [128, TOWN], BF16, pb)
              NPS = 4
              pS = [ps(f"pS{i}", [128, 512], F32, pb) for i in range(NPS)]
              pOr = [ps(f"pO{i}", [128, 512], F32, pb) for i in range(2)]
              pDr = [ps(f"pD{i}", [128, 512], F32, pb) for i in range(2)]
              c = dict(s=0, p=0, e=0, b=0, f=0, o=0, x=0)
              pX = pS + pOr + pDr

              for cc in range(4):
                  dma('sp', ckvT[:, cc, :], ckvT_s[cc, :, :], R=[B_ckvT], A=[ckvT], sb=ckvT)
              dma('sp', krT[:], krT_s[:, :], R=[B_krT], W=[krT], sb=krT)
              dma('pool', wkvb, w_kv_b.rearrange("(k p) n -> p k n", p=128), W=[wkvb_b], sb=wkvb_b)

              def attend(nq, tiles_, q_of, out_buf, out_c0, k_stat, v_stat, kr_stat=None, hp=0):
                  prevq = []
                  n = len(tiles_)
                  lag = 3
                  pO, pD = pOr[c['o'] % 2], pDr[c['o'] % 2]
                  c['o'] += 1

                  def pv(t, P, first, last):
                      c0, c1, np_ = t['c0'], t['c1'], t['np']
                      def mm():
                          nc.tensor.matmul(pO[:, c0:c1], lhsT=v_stat(t)[0:np_, :], rhs=P[0:np_, c0:c1],
                                           start=first, stop=last)
                          return nc.tensor.matmul(pD[:, c0:c1], lhsT=onesb[0:np_, :], rhs=P[0:np_, c0:c1],
                                                  start=first, stop=last)
                      if first:
                          op('pe', mm, R=[P, onesb, Vh_b, av], W=[pO, pD])
                      else:
                          op('pe', mm, R=[P, onesb, Vh_b, av], A=[pO, pD])
                  for i, t in enumerate(tiles_):
                      c0, c1, np_ = t['c0'], t['c1'], t['np']
                      S = pS[c['s'] % NPS]
                      c['s'] += 1
                      P = PT[c['p'] % 4]
                      c['p'] += 1

                      def mm(S=S, t=t, c0=c0, c1=c1, np_=np_):
                          ops = q_of(t, c0, c1)
                          ins = None
                          for j, (l, r_) in enumerate(ops):
                              ins = nc.tensor.matmul(S[0:np_, c0:c1], lhsT=l, rhs=r_, start=(j == 0), stop=(j == len(ops) - 1))
                          return ins
                      op('pe', mm, R=[knT_b, krT, qn, qr, aq, ak], W=[S])
                      if len(prevq) >= lag:
                          pr_ = prevq.pop(0)
                          pv(pr_[0], pr_[1], pr_[2] == 0, False)
                      bias_ap = pmask[0:np_, :] if t.get('bias') == 'pm' else zerob[0:np_, :]
                      if t.get('mb0') is not None:
                          sbt = sbias[c['b'] % 3]
                          c['b'] += 1
                          m0 = t['mb0']
                          op('dve', lambda S=S, sbt=sbt, m0=m0: nc.vector.scalar_tensor_tensor(
                              out=sbt[0:np_, c0:c1], in0=S[0:np_, c0:c1], scalar=BAND_SCALE, in1=MB[0:np_, m0:m0 + (c1 - c0)],
                              op0=ALU.mult, op1=ALU.add), R=[S, MB], W=[sbt])
                          op('act', lambda sbt=sbt, P=P: nc.scalar.activation(out=P[0:np_, c0:c1], in_=sbt[0:np_, c0:c1],
                                                                            func=AF.Exp, bias=bias_ap, scale=1.0),
                             R=[sbt, pmask, zerob], W=[P])
                      else:
                          op('act', lambda S=S, P=P: nc.scalar.activation(out=P[0:np_, c0:c1], in_=S[0:np_, c0:c1],
                                                                        func=AF.Exp, bias=bias_ap, scale=MLA_SCALE),
                             R=[S, pmask, zerob], W=[P])
                      if t.get('mask') is not None:
                          which, mc = t['mask']
                          pr = slice(0, 64) if which == 'lo' else slice(64, 128)
                          op('dve', lambda P=P, pr=pr, mc=mc: nc.vector.memset(P[pr, mc:mc + 64], 0.0), W=[P])
                      prevq.append((t, P, i))
                  while prevq:
                      pr_ = prevq.pop(0)
                      pv(pr_[0], pr_[1], pr_[2] == 0, len(prevq) == 0)
                  op('dve', lambda: nc.vector.reciprocal(out=rden[:, 0:nq], in_=pD[:, 0:nq]), R=[pD], W=[rden])
                  op('dve', lambda: nc.vector.tensor_tensor(out=out_buf[:, out_c0:out_c0 + nq], in0=pO[:, 0:nq],
                                                            in1=rden[:, 0:nq], op=ALU.mult), R=[pO, rden], A=[out_buf])

              def exp_gen(hh):
                kchunks = [(kc, min(512, NKEY_M - kc)) for kc in range(0, NKEY_M, 512)]
                vgroups = [(kt0, min(4, 49 - kt0)) for kt0 in range(0, 49, 4)]

                def exp_stage1(kc, n):
                    pe_ = pS[c['s'] % NPS]
                    c['s'] += 1
                    sq_ = sqfr[c['e'] % 2]

                    def mm():
                        ins = None
                        for cc in range(4):
                            ins = nc.tensor.matmul(pe_[:, 0:n], lhsT=wkvb[:, cc, hh * 256:hh * 256 + 128],
                                                   rhs=ckvT[:, cc, kc:kc + n], start=(cc == 0), stop=(cc == 3))
                        return ins
                    op('pe', mm, R=[wkvb_b, ckvT], W=[pe_])
                    op('act', lambda: nc.scalar.activation(out=sq_[:, 0:n], in_=pe_[:, 0:n], func=AF.Square),
                       R=[pe_], W=[sq_])
                    c['e'] += 1
                    return (kc, n, pe_, sq_)

                def exp_stage2(kc, n, pe_, sq_):
                    pN = pS[c['s'] % NPS]
                    c['s'] += 1
                    rs_ = rsfr[c['f'] % 2]
                    c['f'] += 1
                    op('pe', lambda: nc.tensor.matmul(pN[:, 0:n], lhsT=onesf[:], rhs=sq_[:, 0:n], start=True, stop=True),
                       R=[onesf, sq_], W=[pN])
                    op('act', lambda: nc.scalar.activation(out=rs_[:, 0:n], in_=pN[:, 0:n], func=AF.Sqrt,
                                                           bias=epsb[:], scale=1.0 / 128), R=[pN, epsb], W=[rs_])
                    op('dve', lambda: nc.vector.reciprocal(out=rs_[:, 0:n], in_=rs_[:, 0:n]), R=[rs_], W=[rs_])
                    op('dve', lambda: nc.vector.scalar_tensor_tensor(
                        out=knT[:, kc:kc + n], in0=pe_[:, 0:n], scalar=gkn[:, 0:1], in1=rs_[:, 0:n],
                        op0=ALU.mult, op1=ALU.mult), R=[pe_, gkn, rs_], A=[knT_b])

                def v_group(kt0, nt):
                    pe_ = pS[c['s'] % NPS]
                    c['s'] += 1

                    def mm():
                        ins = None
                        for j in range(nt):
                            kt = kt0 + j
                            for cc in range(4):
                                ins = nc.tensor.matmul(pe_[:, j * 128:(j + 1) * 128], lhsT=ckvT[:, cc, kt * 128:(kt + 1) * 128],
                                                       rhs=wkvb[:, cc, hh * 256 + 128:hh * 256 + 256],
                                                       start=(cc == 0), stop=(cc == 3))
                        return ins
                    op('pe', mm, R=[wkvb_b, ckvT], W=[pe_])
                    op('act', lambda: nc.scalar.activation(
                        out=Vh[:, kt0:kt0 + nt, :], in_=pe_[:, 0:nt * 128].rearrange("p (t d) -> p t d", t=nt),
                        func=AF.Copy), R=[pe_], A=[Vh_b])
                for ii in range(len(kchunks)):
                    st1 = exp_stage1(*kchunks[ii])
                    if ii < len(vgroups):
                        v_group(*vgroups[ii])
                    exp_stage2(*st1)
                    yield

              for _ in exp_gen(0):
                  pass
              for h in range(8):
                  hp = (h % 2) * 64
                  if h == 0:
                      head_loads(0)
                  qn, qr, aq, ak, av, MB = hsets[h % 2]
                  if h + 1 < 8:
                      head_loads(h + 1)
                  def q_mla(qbase):
                      def f(t, c0, c1):
                          kidx, np_ = t['kidx'], t['np']
                          return [(knT[:, kidx:kidx + np_], qn[:, qbase + c0:qbase + c1]),
                                  (krT[hp:hp + 64, kidx:kidx + np_], qr[hp:hp + 64, qbase + c0:qbase + c1])]
                      return f
                  vm = lambda t: Vh[:, t['kidx'] // 128, :]
                  for j in range(4):
                      tl = [dict(kidx=kt * 128, np=128, c0=0, c1=512, bias='pm') for kt in range(16)]
                      for kt in range(4 * j + 4):
                          if kt < 4 * j:
                              tl.append(dict(kidx=2048 + kt * 128, np=128, c0=0, c1=512))
                          else:
                              m = kt - 4 * j
                              tl.append(dict(kidx=2048 + kt * 128, np=128, c0=128 * m, c1=512, mask=('hi', 128 * m)))
                      attend(512, tl, q_mla(512 * j), obT, 512 * j, None, vm)
                  tl = [dict(kidx=4096 + kt * 128, np=128, c0=0, c1=64) for kt in range(16)]
                  tl.append(dict(kidx=6144, np=64, c0=0, c1=64))
                  attend(64, tl, q_mla(2048), obT, 2048, None, vm)

                  eg = exp_gen(h + 1) if h + 1 < 8 else iter(())

                  def q_band(qbase):
                      def f(t, c0, c1):
                          kidx, np_ = t['kidx'], t['np']
                          return [(ak[:, kidx:kidx + np_], aq[:, qbase + c0:qbase + c1])]
                      return f
                  va = lambda t: av[:, t['kidx'] // 128, :]
                  for j in range(4):
                      q0 = 512 * j
                      tl = []
                      for t_ in [3, 0, 1, 2, 4, 5, 6, 7]:
                          ki0 = q0 + 128 * t_
                          if t_ <= 3:
                              c0, c1 = 0, 128 * (t_ + 1)
                              mask = ('lo', c1 - 64)
                          else:
                              c0, c1 = 128 * (t_ - 4), 512
                              mask = ('hi', c0)
                          d = dict(kidx=ki0, np=128, c0=c0, c1=c1, mask=mask, mb0=c0 + 512 - 128 * t_)
                          if j == 0 and t_ <= 3:
                              d['bias'] = 'pm'
                          tl.append(d)
                      for _ in range(3):
                          next(eg, None)
                      attend(512, tl, q_band(q0), oaT, q0, None, va)
                  for _ in range(3):
                      next(eg, None)
                  tl = [dict(kidx=2560 + 128 * t_, np=128, c0=0, c1=64, mb0=512 - 128 * t_) for t_ in range(4)]
                  tl.append(dict(kidx=3072, np=64, c0=0, c1=64, mb0=0))
                  attend(64, tl, q_band(2048), oaT, 2048, None, va)
                  for _ in eg:
                      pass
                  op('dve', lambda: nc.vector.memset(obT[:, 2112:TOWN], 0.0), A=[obT])
                  op('dve', lambda: nc.vector.memset(oaT[:, 2112:TOWN], 0.0), A=[oaT])
                  dma('sp', obT_s[h, :, :], obT[:], R=[obT], A=[B_obT], sb=obT)
                  dma('sp', oaT_s[h, :, :], oaT[:], R=[oaT], A=[B_oaT], sb=oaT)
              kb.barrier()

        with contextlib.ExitStack() as pc:
          if stage >= 3:
              TC = 768
              acc = sb("acc", [128, 6, D], F32, pc)
              xnT = sb("xnTc", [128, 16, TC], BF16, pc)
              oaTt = sb("oaTt", [128, 8, TC], BF16, pc)
              obTt = sb("obTt", [128, 8, TC], BF16, pc)
              mixT = sb("mixT", [128, 16, TC], BF16, pc)
              uTb = [oaTt, obTt]
              fs = dict(i=0, j=0,
                        xnb=[sb("xnbc0", [128, D], BF16, pc)] * 2,
                        ss=[sb(f"fssc{i}", [128, 4], F32, pc) for i in range(2)],
                        pst=[ps(f"pstc{i}", [128, 1024], BF16, pc) for i in range(2)])
              sg = [sb(f"sg{i}", [128, 512], F32, pc) for i in range(4)]
              gtmp = sb("gtmp", [128, 4, TC], F32, pc)
              pq = [ps(f"pq{i}", [128, 512], F32, pc) for i in range(6)]
              cq = dict(p=0, s=0, u=0)
              ctiles = [list(range(0, 6)), list(range(6, 12)), list(range(12, 17))]
              accg = [Buf(f"accg{i}") for i in range(6)]

              def segs(T):
                  return [(s0, min(512, T - s0)) for s0 in range(0, T, 512)]

              def nextp():
                  p = pq[cq['p'] % 6]
                  cq['p'] += 1
                  return p

              def csched():
                  L = []
                  for fb in range(4):
                      L.append(('ga', fb)); L.append(('pa', fb)); L.append(('gb', fb)); L.append(('pb', fb))
                  for cb in range(4):
                      L.append(('wo', cb))
                  for f in range(16):
                      L.append(('up', f)); L.append(('dn', f))
                  return L
              sched = [(ti, k, i) for ti in range(len(ctiles)) for (k, i) in csched()]
              loaded = {}

              def issue(idx):
                  if idx >= len(sched):
                      return
                  ti, k, i = sched[idx]
                  if k == 'ga':
                      loaded[idx] = wload(wsrc(w_in, 0, 16, 5184 + 512 * i, 512), 16, 512)
                  elif k == 'gb':
                      loaded[idx] = wload(wsrc(w_in, 0, 16, 7232 + 512 * i, 512), 16, 512)
                  elif k == 'pa':
                      loaded[idx] = wload(wsrc(w_pa, 0, 8, 512 * i, 512), 8, 512)
                  elif k == 'pb':
                      loaded[idx] = wload(wsrc(w_pb, 0, 8, 512 * i, 512), 8, 512)
                  elif k == 'wo':
                      loaded[idx] = wload(wsrc(w_out, 0, 16, 512 * i, 512), 16, 512)
                  elif k == 'up':
                      loaded[idx] = wload(wsrc(w_up, 0, 16, 512 * i, 512), 16, 512)
                  else:
                      loaded[idx] = wload(wsrc(w_down, 512 * i, 4, 0, 2048), 4, 2048,
                                          view=lambda slot: slot[:].rearrange("p k c -> p (k c)").rearrange("p (k n) -> p k n", k=4))
              freeslots = list(wring)
              nxt = [0]
              sidx = [0]

              def pump():
                  while nxt[0] < len(sched) and freeslots:
                      wstate['force'] = freeslots.pop(0)
                      issue(nxt[0])
                      nxt[0] += 1
                  wstate['force'] = None

              def getw():
                  s = loaded.pop(sidx[0])
                  sidx[0] += 1
                  return s

              def release(slot):
                  freeslots.append(slot)
                  pump()
              pump()

              for ti, groups in enumerate(ctiles):
                  G = len(groups)
                  T = G * 128
                  sg_ = segs(T)
                  load_Gbig(g_mix)
                  for li, g in enumerate(groups):
                      dma('sp', acc[:, li, :], xo[g * 128:(g + 1) * 128, :], W=[accg[li]], sb=accg[li])
                  t0 = groups[0] * 128
                  dma('sp', oaTt[:, :, 0:T], oaT_s[:, :, t0:t0 + T].rearrange("h p t -> p h t"), R=[B_oaT], W=[oaTt], sb=oaTt)
                  dma('sp', obTt[:, :, 0:T], obT_s[:, :, t0:t0 + T].rearrange("h p t -> p h t"), R=[B_obT], W=[obTt], sb=obTt)
                  for li, g in enumerate(groups):
                      frontend(fs, accg[li], acc[:, li, :], xnT, li, T)
                  for fb in range(4):
                      for half_ in range(2):
                          wg_, wp_ = getw(), getw()
                          actp = oaTt if half_ == 0 else obTt
                          for cc in range(4):
                              fc = fb * 4 + cc
                              for (s0, sn) in sg_:
                                  pg_, pp_ = nextp(), nextp()
                                  for (pp, ws, act_, nk) in ((pg_, wg_, xnT, 16), (pp_, wp_, actp, 8)):
                                      def mm(pp=pp, ws=ws, act_=act_, nk=nk):
                                          ins = None
                                          for k in range(nk):
                                              ins = nc.tensor.matmul(pp[:, 0:sn], lhsT=ws[:, k, cc * 128:(cc + 1) * 128],
                                                                     rhs=act_[:, k, s0:s0 + sn], start=(k == 0), stop=(k == nk - 1))
                                          return ins
                                      op('pe', mm, R=[ws, act_], W=[pp])
                                  s1 = sg[cq['s'] % 4]
                                  cq['s'] += 1
                                  op('act', lambda: nc.scalar.activation(out=s1[:, 0:sn], in_=pg_[:, 0:sn], func=AF.Sigmoid),
                                     R=[pg_], W=[s1])
                                  if half_ == 0:
                                      op('dve', lambda: nc.vector.tensor_tensor(out=gtmp[:, cc, s0:s0 + sn], in0=s1[:, 0:sn],
                                                                                in1=pp_[:, 0:sn], op=ALU.mult),
                                         R=[pp_, s1], A=[gtmp])
                                  else:
                                      op('dve', lambda: nc.vector.tensor_tensor(out=s1[:, 0:sn], in0=s1[:, 0:sn], in1=pp_[:, 0:sn],
                                                                                op=ALU.mult), R=[pp_, s1], W=[s1])
                                      op('dve', lambda: nc.vector.tensor_tensor(out=mixT[:, fc, s0:s0 + sn], in0=s1[:, 0:sn],
                                                                                in1=gtmp[:, cc, s0:s0 + sn], op=ALU.add),
                                         R=[s1, gtmp], A=[mixT])
                          release(wg_)
                          release(wp_)
                  for cb in range(4):
                      wo = getw()
                      for li in range(G):
                          pp = nextp()

                          def mm(pp=pp, li=li, wo=wo):
                              ins = None
                              for k in range(16):
                                  ins = nc.tensor.matmul(pp[:], lhsT=mixT[:, k, li * 128:(li + 1) * 128], rhs=wo[:, k, :],
                                                         start=(k == 0), stop=(k == 15))
                              return ins
                          op('pe', mm, R=[mixT, wo], W=[pp])
                          a_ = acc[:, li, cb * 512:(cb + 1) * 512]
                          op('dve', lambda a_=a_, pp=pp: nc.vector.tensor_tensor(out=a_, in0=a_, in1=pp[:], op=ALU.add),
                             R=[pp], W=[accg[li]])
                      release(wo)
                  load_Gbig(g_ffn)
                  for li in range(G):
                      frontend(fs, accg[li], acc[:, li, :], xnT, li, T)
                  for f in range(16):
                      wu, wd = getw(), getw()
                      u_b = uTb[cq['u'] % 2]
                      u_ = u_b[:, 0:4, :]
                      cq['u'] += 1
                      for cc in range(4):
                          for (s0, sn) in sg_:
                              pp = nextp()

                              def mm(pp=pp, cc=cc, s0=s0, sn=sn):
                                  ins = None
                                  for k in range(16):
                                      ins = nc.tensor.matmul(pp[:, 0:sn], lhsT=wu[:, k, cc * 128:(cc + 1) * 128],
                                                             rhs=xnT[:, k, s0:s0 + sn], start=(k == 0), stop=(k == 15))
                                  return ins
                              op('pe', mm, R=[wu, xnT], W=[pp])
                              s1 = sg[cq['s'] % 4]
                              cq['s'] += 1
                              op('act', lambda: nc.scalar.activation(out=s1[:, 0:sn], in_=pp[:, 0:sn], func=AF.Relu),
                                 R=[pp], W=[s1])
                              op('act', lambda: nc.scalar.activation(out=u_[:, cc, s0:s0 + sn], in_=s1[:, 0:sn],
                                                                     func=AF.Square), R=[s1], A=[u_b])
                      release(wu)
                      wdv = wd[:].rearrange("p k c -> p (k c)").rearrange("p (k n) -> p k n", k=4)
                      for li in range(G):
                          for cb in range(4):
                              pp = nextp()

                              def mm(pp=pp, li=li, cb=cb):
                                  ins = None
                                  for k in range(4):
                                      ins = nc.tensor.matmul(pp[:], lhsT=u_[:, k, li * 128:(li + 1) * 128],
                                                             rhs=wdv[:, k, cb * 512:(cb + 1) * 512], start=(k == 0), stop=(k == 3))
                                  return ins
                              op('pe', mm, R=[u_b, wd], W=[pp])
                              a_ = acc[:, li, cb * 512:(cb + 1) * 512]
                              op('dve', lambda a_=a_, pp=pp: nc.vector.tensor_tensor(out=a_, in0=a_, in1=pp[:], op=ALU.add),
                                 R=[pp], W=[accg[li]])
                          if f == 15:
                              g = groups[li]
                              dma('sp', y_d[g * 128:(g + 1) * 128, :], acc[:, li, :], R=[accg[li]], A=[B_out], sb=accg[li])
                      release(wd)
              kb.barrier()
        kb.barrier(engines=['sp'])
    return nc


_CACHE = {}


def _rope_tables(pos):
    half = 32
    freqs = (np.float32(10000.0) ** (-(np.arange(half, dtype=np.float32) / np.float32(half)))).astype(np.float32)
    ang = pos.astype(np.float32)[:, None] * freqs[None, :]
    return np.cos(ang).astype(np.float32), np.sin(ang).astype(np.float32)


def kernel(x_prompt, x_sample, cache_a_k, cache_a_v, cache_mla_ckv, cache_mla_krope,
           norm_mix_g, w_in, g_aq, g_ak, rel_bias, g_kv, g_kr, g_qn, g_qr, g_kn,
           w_kv_b, w_pa, w_pb, w_out, norm_ffn_g, w_up, w_down):
    import os
    stage = float(os.environ.get("KSTAGE", "99"))
    if 'nc' not in _CACHE:
        _CACHE['nc'] = build_program(stage)
    nc = _CACHE['nc']
    in_maps = _prep(x_prompt, x_sample, cache_a_k, cache_a_v, cache_mla_ckv, cache_mla_krope,
                    norm_mix_g, w_in, g_aq, g_ak, rel_bias, g_kv, g_kr, g_qn, g_qr, g_kn,
                    w_kv_b, w_pa, w_pb, w_out, norm_ffn_g, w_up, w_down)
    ncores = int(os.environ.get("KCORES", "8"))
    res = run_bass_kernel_spmd(nc, in_maps[:ncores], core_ids=list(range(ncores)))
    R = list(res.results) + [res.results[0]] * (8 - ncores)
    return _assemble(R)


def _prep(x_prompt, x_sample, cache_a_k, cache_a_v, cache_mla_ckv, cache_mla_krope,
          norm_mix_g, w_in, g_aq, g_ak, rel_bias, g_kv, g_kr, g_qn, g_qr, g_kn,
          w_kv_b, w_pa, w_pb, w_out, norm_ffn_g, w_up, w_down):
    f = lambda a: np.ascontiguousarray(np.asarray(a, dtype=np.float32))
    x_prompt, x_sample = f(x_prompt), f(x_sample)
    idx = np.clip(np.arange(768) - 127, -63, 256) + 63
    tabext = f(np.asarray(rel_bias)[0][:, idx])
    shared = dict(
        w_in=f(w_in[0]), w_kv_b=f(w_kv_b[0]), w_pa=f(w_pa[0]), w_pb=f(w_pb[0]), w_out=f(w_out[0]),
        w_up=f(w_up[0]), w_down=f(w_down[0]),
        g_mix=f(norm_mix_g[0]).reshape(1, D), g_ffn=f(norm_ffn_g[0]).reshape(1, D),
        g_aq=f(g_aq[0]).reshape(1, 128), g_ak=f(g_ak[0]).reshape(1, 128), g_kv=f(g_kv[0]).reshape(1, 512),
        g_kr=f(g_kr[0]).reshape(1, 64), g_qn=f(g_qn[0]).reshape(1, 128), g_qr=f(g_qr[0]).reshape(1, 64),
        g_kn=f(g_kn[0]).reshape(128, 1), tabext=tabext,
        identb=np.eye(128, dtype=np.float32), identf=np.eye(128, dtype=np.float32),
    )
    cos_all, sin_all = _rope_tables(np.arange(4096))
    in_maps = []
    for c in range(8):
        b, half = c // 2, c % 2
        xo = np.zeros((TOWN, D), np.float32)
        xo[0:2048] = x_prompt[b, half * 2048:(half + 1) * 2048]
        xo[2048:2112] = x_sample[c]
        xp = np.zeros((NPRE, D), np.float32)
        if half == 1:
            xp[:] = x_prompt[b, 0:2048]
        cos_o = np.zeros((TOWN, 32), np.float32)
        sin_o = np.zeros((TOWN, 32), np.float32)
        cos_o[0:2048] = cos_all[half * 2048:(half + 1) * 2048]
        sin_o[0:2048] = sin_all[half * 2048:(half + 1) * 2048]
        cos_o[2048:2112] = cos_all[2048:2112]
        sin_o[2048:2112] = sin_all[2048:2112]
        m = dict(shared)
        m.update(
            xo=xo, xp=xp,
            cak=f(cache_a_k[0, c]).reshape(512, 1024), cav=f(cache_a_v[0, c]).reshape(512, 1024),
            cckv=f(cache_mla_ckv[0, c]), ckr=f(cache_mla_krope[0, c]),
            cos_o=cos_o, sin_o=sin_o, cos_p=cos_all[0:2048].copy(), sin_p=sin_all[0:2048].copy(),
            pmask=np.full((128, 1), 0.0 if half == 1 else NEG, np.float32),
        )
        in_maps.append(m)
    return in_maps


def _assemble(R):
    yp = np.zeros((4, 4096, D), np.float32)
    ys = np.zeros((8, 64, D), np.float32)
    akp = np.zeros((1, 4, 512, 8, 128), np.float32)
    avp = np.zeros((1, 4, 512, 8, 128), np.float32)
    ckp = np.zeros((1, 4, 4096, 512), np.float32)
    krp = np.zeros((1, 4, 4096, 64), np.float32)
    aks = np.zeros((1, 8, 64, 8, 128), np.float32)
    avs = np.zeros((1, 8, 64, 8, 128), np.float32)
    cks = np.zeros((1, 8, 64, 512), np.float32)
    krs = np.zeros((1, 8, 64, 64), np.float32)
    for c in range(8):
        b, half = c // 2, c % 2
        r = R[c]
        yp[b, half * 2048:(half + 1) * 2048] = r["y"][0:2048]
        ys[c] = r["y"][2048:2112]
        ckp[0, b, half * 2048:(half + 1) * 2048] = r["o_ckv"][0:2048]
        krp[0, b, half * 2048:(half + 1) * 2048] = r["o_kr"][0:2048]
        cks[0, c] = r["o_ckv"][2048:2112]
        krs[0, c] = r["o_kr"][2048:2112]
        if half == 1:
            akp[0, b] = r["o_ak"][0:512].reshape(512, 8, 128)
            avp[0, b] = r["o_av"][0:512].reshape(512, 8, 128)
        aks[0, c] = r["o_ak"][512:576].reshape(64, 8, 128)
        avs[0, c] = r["o_av"][512:576].reshape(64, 8, 128)
    return (yp, ys, akp, avp, ckp, krp, aks, avs, cks, krs)
```

```python
import contextlib
import numpy as np
import concourse.bass as bass
import concourse.mybir as mybir
from concourse.bass_utils import run_bass_kernel_spmd

F32 = mybir.dt.float32
BF16 = mybir.dt.bfloat16
AF = mybir.ActivationFunctionType
ALU = mybir.AluOpType
AX = mybir.AxisListType

D = 2048
DIN = 9280
NGO = 17
TOWN = NGO * 128
NPRE = 2048
EPS = 1e-6
BAND_SCALE = 128 ** -0.5
MLA_SCALE = 192 ** -0.5
NEG = -30000.0
NKEY_M = 6272
NKEY_B = 3200


class Buf:
    def __init__(self, name, t=None):
        self.name = name
        self.t = t
        self.w = {}
        self.r = {}
        self.dkey = None
        self.dcnt = 0

    def __getitem__(self, idx):
        return self.t[idx]


class KB:
    LIMIT = 20000

    def __init__(self, nc, es):
        self.nc = nc
        self.es = es
        self.eng = dict(pe=nc.tensor, act=nc.scalar, dve=nc.vector, pool=nc.gpsimd, sp=nc.sync)
        self.sems = {}
        self.final = {}
        self.ekey = {}
        self.ecnt = {}
        self.nsem = 0
        for e in self.eng:
            self._roll(e)
        self.waited = {e: {} for e in self.eng}
        self.dbufs = []
        self.pe_pending = False

    def _newsem(self, name):
        h = self.es.enter_context(self.nc.semaphore(name))
        self.sems[name] = h
        self.nsem += 1
        return name

    def _roll(self, e):
        if e in self.ekey:
            self.final[self.ekey[e]] = self.ecnt[e]
        self.ekey[e] = self._newsem(f"e_{e}_{self.nsem}")
        self.ecnt[e] = 0

    def _wait(self, e, need):
        wd = self.waited[e]
        for key, val in need.items():
            if val <= 0:
                continue
            if key == self.ekey['pe'] and val > self.ecnt['pe']:
                raise RuntimeError("wait on unmarked PE op")
            if wd.get(key, 0) < val:
                self.eng[e].wait_ge(self.sems[key], val)
                wd[key] = val

    @staticmethod
    def _need(R, W, A):
        need = {}

        def upd(d):
            for k, v in d.items():
                if need.get(k, 0) < v:
                    need[k] = v
        for b in R:
            upd(b.w)
        for b in W:
            upd(b.w)
            upd(b.r)
        for b in A:
            upd(b.r)
        return need

    def _record(self, key, val, R, W, A):
        for b in R:
            if b.r.get(key, 0) < val:
                b.r[key] = val
        for b in W:
            b.w = {key: val}
            b.r = {}
        for b in A:
            if b.w.get(key, 0) < val:
                b.w[key] = val
            b.r = {}

    def op(self, e, fn, R=(), W=(), A=(), mark=True):
        if not (e == 'pe' and self.pe_pending):
            if self.ecnt[e] >= self.LIMIT:
                self._roll(e)
        need = self._need(R, W, A)
        if e == 'pe':
            need = {k: v for k, v in need.items() if not k.startswith('e_pe_')}
        self._wait(e, need)
        ins = fn()
        key = self.ekey[e]
        if mark:
            self.ecnt[e] += 1
            ins.then_inc(self.sems[key], 1)
            val = self.ecnt[e]
            if e == 'pe':
                self.pe_pending = False
        else:
            val = self.ecnt[e] + 1
            if e == 'pe':
                self.pe_pending = True
        self._record(key, val, R, W, A)
        return ins

    def dma(self, e, out, in_, R=(), W=(), A=(), sb=None):
        need = self._need(R, W, A)
        self._wait(e, need)
        if sb.dkey is None:
            sb.dkey = {}
            sb.dcnt = {}
            self.dbufs.append(sb)
        if e not in sb.dkey:
            sb.dkey[e] = self._newsem("d_" + e + "_" + sb.name)
            sb.dcnt[e] = 0
        ins = self.eng[e].dma_start(out=out, in_=in_)
        sb.dcnt[e] += 16
        ins.then_inc(self.sems[sb.dkey[e]], 16)
        self._record(sb.dkey[e], sb.dcnt[e], R, W, A)
        return ins

    def barrier(self, engines=None):
        assert not self.pe_pending
        need = dict(self.final)
        for e in self.eng:
            need[self.ekey[e]] = self.ecnt[e]
        for b in self.dbufs:
            for e_ in b.dkey:
                need[b.dkey[e_]] = b.dcnt[e_]
        for e in (engines or self.eng):
            self._wait(e, dict(need))


def _runs(groups, mapfn):
    runs = []
    for li, g in enumerate(groups):
        d = mapfn(g)
        if d is None:
            continue
        if runs and runs[-1][0] + runs[-1][1] == li and runs[-1][2] + runs[-1][1] * 128 == d:
            runs[-1][1] += 1
        else:
            runs.append([li, 1, d])
    return runs


def build_program(stage=99):
    nc = bass.Bass("TRN2", target_bir_lowering=False)

    def din(name, shape, dt=F32):
        return nc.dram_tensor(name, list(shape), dt, kind="ExternalInput").ap()

    def dout(name, shape):
        return nc.dram_tensor(name, list(shape), F32, kind="ExternalOutput").ap()

    def dscr(name, shape, dt=BF16):
        return nc.dram_tensor(name, list(shape), dt, kind="Internal").ap()

    xo = din("xo", [TOWN, D])
    xp = din("xp", [NPRE, D])
    cak = din("cak", [512, 1024])
    cav = din("cav", [512, 1024])
    cckv = din("cckv", [2048, 512])
    ckr = din("ckr", [2048, 64])
    w_in = din("w_in", [D, DIN])
    w_kv_b = din("w_kv_b", [512, 2048])
    w_pa = din("w_pa", [1024, D])
    w_pb = din("w_pb", [1024, D])
    w_out = din("w_out", [D, D])
    w_up = din("w_up", [D, 4 * D])
    w_down = din("w_down", [4 * D, D])
    g_mix = din("g_mix", [1, D])
    g_ffn = din("g_ffn", [1, D])
    g_aq = din("g_aq", [1, 128])
    g_ak = din("g_ak", [1, 128])
    g_kv = din("g_kv", [1, 512])
    g_kr = din("g_kr", [1, 64])
    g_qn = din("g_qn", [1, 128])
    g_qr = din("g_qr", [1, 64])
    g_kn = din("g_kn", [128, 1])
    tabext = din("tabext", [8, 768])
    cos_o = din("cos_o", [TOWN, 32])
    sin_o = din("sin_o", [TOWN, 32])
    cos_p = din("cos_p", [NPRE, 32])
    sin_p = din("sin_p", [NPRE, 32])
    pmask_d = din("pmask", [128, 1])
    identb_d = din("identb", [128, 128])
    identf_d = din("identf", [128, 128])

    y_d = dout("y", [TOWN, D])
    oak_d = dout("o_ak", [640, 1024])
    oav_d = dout("o_av", [640, 1024])
    ockv_d = dout("o_ckv", [TOWN, 512])
    okr_d = dout("o_kr", [TOWN, 64])

    aqT_s = dscr("aqT_s", [8, 128, TOWN])
    qnT_s = dscr("qnT_s", [8, 128, TOWN])
    qrT_s = dscr("qrT_s", [4, 128, TOWN])
    akT_s = dscr("akT_s", [8, 128, NKEY_B])
    av_s = dscr("av_s", [NKEY_B, 1024])
    ckvT_s = dscr("ckvT_s", [4, 128, NKEY_M])
    krT_s = dscr("krT_s", [128, NKEY_M])
    oaT_s = dscr("oaT_s", [8, 128, TOWN])
    MB_s = dscr("MB_s", [8, 128, 640], F32)
    obT_s = dscr("obT_s", [8, 128, TOWN])

    with contextlib.ExitStack() as es:
        kb = KB(nc, es)
        op, dma = kb.op, kb.dma

        def sb(name, shape, dt, stack=None):
            t = (stack or es).enter_context(nc.sbuf_tensor("sb_" + name, list(shape), dt))
            return Buf(name, t)

        def ps(name, shape, dt, stack=None):
            t = (stack or es).enter_context(nc.psum_tensor("ps_" + name, list(shape), dt))
            return Buf(name, t)

        B_aqT, B_qnT, B_qrT, B_akT, B_av = Buf("s_aqT"), Buf("s_qnT"), Buf("s_qrT"), Buf("s_akT"), Buf("s_av")
        B_ckvT, B_krT, B_oaT, B_obT = Buf("s_ckvT"), Buf("s_krT"), Buf("s_oaT"), Buf("s_obT")
        B_out = Buf("outs")

        identb = sb("identb", [128, 128], BF16)
        identf = sb("identf", [128, 128], F32)
        onesb = sb("onesb", [128, 128], BF16)
        onesf = sb("onesf", [128, 128], F32)
        epsb = sb("epsb", [128, 1], F32)
        zerob = sb("zerob", [128, 1], F32)
        pmask = sb("pmaskb", [128, 1], F32)
        Gbig = sb("Gbig", [128, D], F32)
        gkn = sb("gknb", [128, 1], F32)
        wring = [sb(f"wring{i}", [128, 16, 512], BF16) for i in range(3)]

        dma('pool', identb[:], identb_d[:, :], W=[identb], sb=identb)
        dma('sp', identf[:], identf_d[:, :], W=[identf], sb=identf)
        dma('sp', pmask[:], pmask_d[:, :], W=[pmask], sb=pmask)
        dma('sp', gkn[:], g_kn[:, :], W=[gkn], sb=gkn)
        op('dve', lambda: nc.vector.memset(onesb[:], 1.0), W=[onesb])
        op('dve', lambda: nc.vector.memset(onesf[:], 1.0), W=[onesf])
        op('dve', lambda: nc.vector.memset(epsb[:], EPS), W=[epsb])
        op('dve', lambda: nc.vector.memset(zerob[:], 0.0), W=[zerob])

        B_MB = Buf("s_MB")
        mbsem = Buf("mbsem")
        for kk in range(128):
            dma('sp', MB_s[:, kk, :], tabext[:, 127 - kk:127 - kk + 640], A=[B_MB], sb=mbsem)

        def load_gain(dst, g, n, rep):
            src = bass.AP(g.tensor, 0, [[0, 128], [0, rep], [1, n]])
            dma('sp', dst[:].rearrange("p (r n) -> p r n", r=rep), src, W=[dst], sb=dst)

        def load_Gbig(g):
            dma('sp', Gbig[:], g.partition_broadcast(128)[:, 0, :], W=[Gbig], sb=Gbig)

        wstate = {'n': 0}

        def wload(src_ap, nk, ncols, view=None):
            if wstate.get('force') is not None:
                slot = wstate['force']
            else:
                slot = wring[wstate['n'] % 3]
                wstate['n'] += 1
            if isinstance(src_ap, list):
                for i, (c0, n, sp_) in enumerate(src_ap):
                    if i == 0:
                        dma('pool', slot[:, 0:nk, c0:c0 + n], sp_, W=[slot], sb=slot)
                    else:
                        dma('pool', slot[:, 0:nk, c0:c0 + n], sp_, A=[slot], sb=slot)
                return slot
            dst = slot[:, 0:nk, 0:ncols] if view is None else view(slot)
            dma('pool', dst, src_ap, W=[slot], sb=slot)
            return slot

        def wsrc(w, r0, nk, c0, ncols):
            return w[r0:r0 + nk * 128, c0:c0 + ncols].rearrange("(k p) n -> p k n", p=128)

        def frontend(fs, src_buf, src_ap, xnT, loc, T):
            xnb = fs['xnb'][fs['i'] % 2]
            ssb = fs['ss'][fs['i'] % 2]
            fs['i'] += 1
            op('act', lambda: nc.scalar.activation(out=xnb[:], in_=src_ap, func=AF.Square,
                                                   accum_out=ssb[:, 0:1]), R=[src_buf], W=[xnb, ssb])
            op('act', lambda: nc.scalar.activation(out=ssb[:, 1:2], in_=ssb[:, 0:1], func=AF.Sqrt,
                                                   bias=epsb[:], scale=1.0 / D), R=[ssb, epsb], A=[ssb])
            op('dve', lambda: nc.vector.reciprocal(out=ssb[:, 2:3], in_=ssb[:, 1:2]), R=[ssb], A=[ssb])
            op('dve', lambda: nc.vector.scalar_tensor_tensor(out=xnb[:], in0=src_ap, scalar=ssb[:, 2:3],
                                                             in1=Gbig[:], op0=ALU.mult, op1=ALU.mult),
               R=[src_buf, ssb, Gbig], W=[xnb])
            for half in range(2):
                pt = fs['pst'][fs['j'] % 2]
                fs['j'] += 1

                def mm(pt=pt, half=half):
                    ins = None
                    for j in range(8):
                        k = half * 8 + j
                        ins = nc.tensor.transpose(out=pt[:, j * 128:(j + 1) * 128],
                                                  in_=xnb[:, k * 128:(k + 1) * 128], identity=identb[:])
                    return ins
                op('pe', mm, R=[xnb, identb], W=[pt])
                dst = xnT[:, half * 8:half * 8 + 8, loc * 128:(loc + 1) * 128]
                src = pt[:].rearrange("p (k t) -> p k t", k=8)
                if half == 0:
                    op('act', lambda: nc.scalar.activation(out=dst, in_=src, func=AF.Copy), R=[pt], A=[xnT])
                else:
                    op('dve', lambda: nc.vector.tensor_copy(out=dst, in_=src), R=[pt], A=[xnT])

        with contextlib.ExitStack() as pa:
            TA = 768
            gt_aq = sb("gt_aq", [128, 512], F32, pa)
            gt_ak = sb("gt_ak", [128, 512], F32, pa)
            gt_qn = sb("gt_qn", [128, 512], F32, pa)
            gt_qr = sb("gt_qr", [128, 512], F32, pa)
            gt_kv = sb("gt_kv", [128, 512], F32, pa)
            gt_kr = sb("gt_kr", [128, 64], F32, pa)
            load_gain(gt_aq, g_aq, 128, 4)
            load_gain(gt_ak, g_ak, 128, 4)
            load_gain(gt_qn, g_qn, 128, 4)
            load_gain(gt_qr, g_qr, 64, 8)
            load_gain(gt_kv, g_kv, 512, 1)
            load_gain(gt_kr, g_kr, 64, 1)
            xst = [sb(f"xst{i}", [128, D], F32, pa) for i in range(2)]
            fs = dict(i=0, j=0,
                      xnb=[sb(f"xnb{i}", [128, D], BF16, pa) for i in range(2)],
                      ss=[sb(f"fss{i}", [128, 4], F32, pa) for i in range(2)],
                      pst=[ps(f"pst{i}", [128, 1024], BF16, pa) for i in range(2)])
            xnTs = [sb(f"xnT{i}", [128, 16, TA], BF16, pa) for i in range(2)]
            psm = [ps(f"psm{i}", [128, 512], F32, pa) for i in range(4)]
            ptr = [ps(f"ptr{i}", [128, 512], BF16, pa) for i in range(2)]
            sqr = [sb(f"sqj{i}", [128, 512], F32, pa) for i in range(2)]
            ssr = [sb(f"ssr{i}", [128, 24], F32, pa) for i in range(4)]
            nfr = [sb(f"nf{i}", [128, 512], F32, pa) for i in range(4)]
            nbr = [sb(f"nb{i}", [128, 512], BF16, pa) for i in range(4)]
            rtmp = sb("rtmp", [128, 4, 256], F32, pa)
            tst = [sb(f"tst{i}", [128, 4, TA], BF16, pa) for i in range(2)]
            csts = [sb(f"cst{i}", [128, 8, 2, 32], F32, pa) for i in range(2)]
            cnt = dict(ps=0, tr=0, r=0, tst=0)

            load_Gbig(g_mix)

            def u_plain(c0, n):
                return lambda: (wsrc(w_in, 0, 16, c0, n), 16, n, None)

            def u_qn(h0):
                def f():
                    src = [(j * 128, 128, bass.AP(w_in.tensor, 3072 + 192 * (h0 + j), [[DIN, 128], [128 * DIN, 16], [1, 128]]))
                           for j in range(4)]
                    return (src, 16, 512, None)
                return f

            def u_qr():
                def f():
                    src = [(j * 64, 64, bass.AP(w_in.tensor, 3072 + 128 + 192 * j, [[DIN, 128], [128 * DIN, 16], [1, 64]]))
                           for j in range(8)]
                    return (src, 16, 512, None)
                return f

            U = {
                'aq0': ('aq', u_plain(0, 512), 0), 'aq1': ('aq', u_plain(512, 512), 4),
                'ak0': ('ak', u_plain(1024, 512), 0), 'ak1': ('ak', u_plain(1536, 512), 4),
                'av0': ('av', u_plain(2048, 512), 0), 'av1': ('av', u_plain(2560, 512), 4),
                'qn0': ('qn', u_qn(0), 0), 'qn1': ('qn', u_qn(4), 4),
                'qr': ('qr', u_qr(), 0),
                'ckv': ('ckv', u_plain(4608, 512), 0),
                'kr': ('kr', u_plain(5120, 64), 0),
            }
            own_units = ['aq0', 'aq1', 'ak0', 'ak1', 'av0', 'av1', 'qn0', 'qn1', 'qr', 'ckv', 'kr']
            tiles = [
                ('p', list(range(0, 6)), ['ckv', 'kr']),
                ('p', list(range(6, 12)), ['ckv', 'kr']),
                ('p', list(range(12, 16)), ['ckv', 'kr', 'ak0', 'ak1', 'av0', 'av1']),
                ('o', list(range(0, 6)), own_units),
                ('o', list(range(6, 12)), own_units),
                ('o', list(range(12, 17)), own_units),
            ]

            def map_band(src, g):
                if src == 'p':
                    return (g - 12) * 128 if g >= 12 else None
                return 512 + g * 128 if g < 16 else 3072

            def map_mla(src, g):
                if src == 'p':
                    return g * 128
                return 2048 + g * 128 if g < 16 else 6144

            def map_own(src, g):
                return g * 128

            sched = [(ti, un) for ti, t in enumerate(tiles) for un in t[2]]
            loaded = {}

            def issue(idx):
                if idx < len(sched):
                    ti, un = sched[idx]
                    src, nk, ncols, view = U[un][1]()
                    loaded[idx] = wload(src, nk, ncols, view)
            sidx = 0
            if stage < 0.1:
                tiles = []
            elif stage < 0.4:
                tiles = tiles[:1]
            elif stage < 0.45:
                tiles = tiles[:2]
            elif stage < 0.55:
                tiles = tiles[:3]
                tiles[2] = (tiles[2][0], tiles[2][1], ['ckv', 'kr'])
            elif stage < 0.61:
                tiles = tiles[:3]
                tiles[2] = (tiles[2][0], tiles[2][1], ['ckv', 'kr', 'ak0'])
            elif stage < 0.63:
                tiles = tiles[:3]
                tiles[2] = (tiles[2][0], tiles[2][1], ['ckv', 'kr', 'ak0', 'ak1'])
            elif stage < 0.65:
                tiles = tiles[:3]
                tiles[2] = (tiles[2][0], tiles[2][1], ['ckv', 'kr', 'av0'])
            elif stage < 0.7:
                tiles = tiles[:3]
            import os as _os
            if _os.environ.get("KTILES"):
                tiles = tiles[:int(_os.environ["KTILES"])]
            if _os.environ.get("KOWN"):
                ou = _os.environ["KOWN"].split(",")
                tiles = [(a, b, (ou if a == 'o' else c)) for (a, b, c) in tiles]
            sched = [(ti, un) for ti, t in enumerate(tiles) for un in t[2]]
            issue(0)
            issue(1)
            def tile_front(ti):
                src, groups, units = tiles[ti]
                xsrc = xp if src == 'p' else xo
                G = len(groups)
                T = G * 128
                cst_ = csts[ti % 2]
                xnT_ = xnTs[ti % 2]
                csrc_c = (cos_p if src == 'p' else cos_o)
                csrc_s = (sin_p if src == 'p' else sin_o)
                g0 = groups[0]
                dma('sp', cst_[:, 0:G, 0, :], csrc_c[g0 * 128:(g0 + G) * 128, :].rearrange("(g p) c -> p g c", p=128),
                    W=[cst_], sb=cst_)
                dma('sp', cst_[:, 0:G, 1, :], csrc_s[g0 * 128:(g0 + G) * 128, :].rearrange("(g p) c -> p g c", p=128),
                    A=[cst_], sb=cst_)
                for li, g in enumerate(groups):
                    xs = xst[(fs['i']) % 2]
                    dma('sp', xs[:], xsrc[g * 128:(g + 1) * 128, :], W=[xs], sb=xs)
                    frontend(fs, xs, xs[:], xnT_, li, T)

            pend = []

            def do_transposes(nb, li, nblk, ts_, final_cb):
                pt = ptr[cnt['tr'] % 2]
                cnt['tr'] += 1

                def mm():
                    ins = None
                    for j in range(nblk):
                        ins = nc.tensor.transpose(out=pt[:, j * 128:(j + 1) * 128],
                                                  in_=nb[:, j * 128:(j + 1) * 128], identity=identb[:])
                    return ins
                op('pe', mm, R=[nb, identb], W=[pt])
                dst = ts_[:, 0:nblk, li * 128:(li + 1) * 128]
                s_ = pt[:, 0:nblk * 128].rearrange("p (k t) -> p k t", k=nblk)
                op('act', lambda: nc.scalar.activation(out=dst, in_=s_, func=AF.Copy), R=[pt], A=[ts_])
                if final_cb is not None:
                    final_cb()

            def make_final(kind, src, groups, lgs, ts_, h0):
                def fin():
                    if kind in ('aq', 'qn', 'qr'):
                        dstT, Bd, mp = {'aq': (aqT_s, B_aqT, map_own), 'qn': (qnT_s, B_qnT, map_own),
                                        'qr': (qrT_s, B_qrT, map_own)}[kind]
                    elif kind == 'ak':
                        dstT, Bd, mp = akT_s, B_akT, map_band
                    elif kind == 'ckv':
                        dstT, Bd, mp = ckvT_s, B_ckvT, map_mla
                    else:
                        dstT, Bd, mp = krT_s, B_krT, map_mla
                    sub = [groups[li] for li in lgs]
                    for (ls, n, d0) in _runs(sub, lambda g: mp(src, g)):
                        l0 = lgs[ls]
                        if kind == 'kr':
                            dma('sp', dstT[:, d0:d0 + n * 128], ts_[:, 0, l0 * 128:(l0 + n) * 128],
                                R=[ts_], A=[Bd], sb=ts_)
                        else:
                            hh = h0 if kind in ('aq', 'ak', 'qn') else 0
                            dma('sp', dstT[hh:hh + 4, :, d0:d0 + n * 128].rearrange("h p t -> p h t"),
                                ts_[:, 0:4, l0 * 128:(l0 + n) * 128], R=[ts_], A=[Bd], sb=ts_)
                return fin

            if tiles:
                tile_front(0)
            for ti, (src, groups, units) in enumerate(tiles):
                G = len(groups)
                T = G * 128
                cst = csts[ti % 2]
                xnT = xnTs[ti % 2]
                for ui, un in enumerate(units):
                    if ui == min(1, len(units) - 1) and ti + 1 < len(tiles):
                        tile_front(ti + 1)
                    kind, _, h0 = U[un]
                    slot = loaded.pop(sidx)
                    issue(sidx + 2)
                    sidx += 1
                    ncols = 64 if kind == 'kr' else 512
                    if kind in ('ak', 'av') and src == 'p':
                        lgs = [li for li, g in enumerate(groups) if g >= 12]
                    else:
                        lgs = list(range(G))
                    transposed = kind != 'av'
                    if not transposed:
                        while pend:
                            do_transposes(*pend.pop(0))
                    if transposed:
                        ts_ = tst[cnt['tst'] % 2]
                        cnt['tst'] += 1
                    for li in lgs:
                        g = groups[li]
                        pm = psm[cnt['ps'] % 4]
                        cnt['ps'] += 1

                        def mm(pm=pm, li=li):
                            ins = None
                            for k in range(16):
                                ins = nc.tensor.matmul(pm[:, 0:ncols], lhsT=xnT[:, k, li * 128:(li + 1) * 128],
                                                       rhs=slot[:, k, 0:ncols], start=(k == 0), stop=(k == 15))
                            return ins
                        op('pe', mm, R=[xnT, slot], W=[pm])
                        if len(pend) >= 3:
                            do_transposes(*pend.pop(0))
                        ri = cnt['r'] % 4
                        cnt['r'] += 1
                        nf, nb, ssb = nfr[ri], nbr[ri], ssr[ri]
                        own = (src == 'o')
                        if kind == 'av':
                            need_out = own and g >= 12
                            if need_out:
                                op('dve', lambda: nc.vector.tensor_copy(out=nf[:], in_=pm[:]), R=[pm], W=[nf])
                                orow = (g - 12) * 128
                                dma('sp', oav_d[orow:orow + 128, h0 * 128:h0 * 128 + 512], nf[:], R=[nf], A=[B_out], sb=nf)
                                op('act', lambda: nc.scalar.activation(out=nb[:], in_=nf[:], func=AF.Copy), R=[nf], W=[nb])
                            else:
                                op('dve', lambda: nc.vector.tensor_copy(out=nb[:], in_=pm[:]), R=[pm], W=[nb])
                            krow = map_band(src, g)
                            dma('sp', av_s[krow:krow + 128, h0 * 128:h0 * 128 + 512], nb[:], R=[nb], A=[B_av], sb=nb)
                            continue
                        hd, nh, gt = {'aq': (128, 4, gt_aq), 'ak': (128, 4, gt_ak), 'qn': (128, 4, gt_qn),
                                      'qr': (64, 8, gt_qr), 'ckv': (512, 1, gt_kv), 'kr': (64, 1, gt_kr)}[kind]
                        nc_ = nh * hd
                        if nh == 1:
                            sq = sqr[0]
                            op('act', lambda: nc.scalar.activation(
                                out=sq[:, 0:hd], in_=pm[:, 0:hd], func=AF.Square,
                                accum_out=ssb[:, 0:1]), R=[pm], W=[sq, ssb])
                        else:
                            sq = sqr[cnt['r'] % 2]
                            op('act', lambda: nc.scalar.activation(out=sq[:, 0:nc_], in_=pm[:, 0:nc_], func=AF.Square),
                               R=[pm], W=[sq])
                            op('dve', lambda: nc.vector.tensor_reduce(
                                out=ssb[:, 0:nh], in_=sq[:, 0:nc_].rearrange("p (h d) -> p h d", h=nh),
                                axis=AX.X, op=ALU.add), R=[sq], W=[ssb])
                        op('act', lambda: nc.scalar.activation(out=ssb[:, 8:8 + nh], in_=ssb[:, 0:nh], func=AF.Sqrt,
                                                               bias=epsb[:], scale=1.0 / hd), R=[ssb, epsb], A=[ssb])
                        op('dve', lambda: nc.vector.reciprocal(out=ssb[:, 16:16 + nh], in_=ssb[:, 8:8 + nh]),
                           R=[ssb], A=[ssb])
                        rope = kind in ('qr', 'kr')
                        need_f32 = kind in ('ak', 'ckv', 'kr')
                        tgt = nf if (need_f32 or rope) else nb
                        if nh == 1:
                            op('dve', lambda: nc.vector.scalar_tensor_tensor(
                                out=tgt[:, 0:hd], in0=pm[:, 0:hd], scalar=ssb[:, 16:17], in1=gt[:, 0:hd],
                                op0=ALU.mult, op1=ALU.mult), R=[pm, ssb, gt], W=[tgt])
                        else:
                            rb = ssb[:, 16:16 + nh].unsqueeze(2).broadcast_to([128, nh, hd])
                            op('dve', lambda: nc.vector.tensor_tensor(
                                out=sq[:, 0:nc_].rearrange("p (h d) -> p h d", h=nh),
                                in0=pm[:, 0:nc_].rearrange("p (h d) -> p h d", h=nh), in1=rb, op=ALU.mult),
                               R=[pm, ssb], W=[sq])
                            op('dve', lambda: nc.vector.tensor_tensor(out=tgt[:, 0:nc_], in0=sq[:, 0:nc_], in1=gt[:, 0:nc_],
                                                                      op=ALU.mult), R=[sq, gt], W=[tgt])
                        if rope:
                            xv = nf[:, 0:nh * 64].rearrange("p (h t c) -> p h t c", h=nh, t=2)
                            x1, x2 = xv[:, :, 0, :], xv[:, :, 1, :]
                            cosb = cst[:, li, 0, :].unsqueeze(1).broadcast_to([128, nh, 32])
                            sinb = cst[:, li, 1, :].unsqueeze(1).broadcast_to([128, nh, 32])
                            tv = [rtmp[:, i, 0:nh * 32].rearrange("p (h c) -> p h c", h=nh) for i in range(4)]
                            for i, (a, b) in enumerate([(x1, cosb), (x2, sinb), (x2, cosb), (x1, sinb)]):
                                op('dve', lambda a=a, b=b, i=i: nc.vector.tensor_tensor(out=tv[i], in0=a, in1=b, op=ALU.mult),
                                   R=[nf, cst], A=[rtmp] if i else [], W=[] if i else [rtmp])
                            op('dve', lambda: nc.vector.tensor_tensor(out=x1, in0=tv[0], in1=tv[1], op=ALU.subtract),
                               R=[rtmp], A=[nf])
                            op('dve', lambda: nc.vector.tensor_tensor(out=x2, in0=tv[2], in1=tv[3], op=ALU.add),
                               R=[rtmp], A=[nf])
                        if need_f32 or rope:
                            if kind == 'kr':
                                op('act', lambda: nc.scalar.activation(out=nb[:, 0:64], in_=nf[:, 0:64], func=AF.Copy),
                                   R=[nf], W=[nb])
                                op('act', lambda: nc.scalar.activation(out=nb[:, 64:128], in_=nf[:, 0:64], func=AF.Copy),
                                   R=[nf], A=[nb])
                            else:
                                op('act', lambda: nc.scalar.activation(out=nb[:], in_=nf[:], func=AF.Copy),
                                   R=[nf], W=[nb])
                        if own and kind == 'ak' and g >= 12:
                            orow = (g - 12) * 128
                            dma('sp', oak_d[orow:orow + 128, h0 * 128:h0 * 128 + 512], nf[:], R=[nf], A=[B_out], sb=nf)
                        if own and kind == 'ckv':
                            dma('sp', ockv_d[g * 128:(g + 1) * 128, :], nf[:], R=[nf], A=[B_out], sb=nf)
                        if own and kind == 'kr':
                            dma('sp', okr_d[g * 128:(g + 1) * 128, :], nf[:, 0:64], R=[nf], A=[B_out], sb=nf)
                        nblk = 1 if kind == 'kr' else 4
                        fin = make_final(kind, src, groups, lgs, ts_, h0) if li == lgs[-1] else None
                        pend.append((nb, li, nblk, ts_, fin))
            while pend:
                do_transposes(*pend.pop(0))

            cbuf = sb("cbuf", [128, 16, 512], BF16, pa)
            kbuf = sb("kbuf", [128, 16, 128], BF16, pa)
            _kc = int(_os.environ.get('KCACHE', '7'))
            if stage >= 1:
              if _kc & 1:
                cvv = cbuf[:, 8:16, :].rearrange("p (g a) c -> p g (a c)", g=4)
                dma('pool', cvv, cav.rearrange("(g p) c -> p g c", p=128), W=[cbuf], sb=cbuf)
                dma('sp', av_s[2560:3072, :].rearrange("(g p) c -> p g c", p=128), cvv, R=[cbuf], A=[B_av], sb=cbuf)
                dma('pool', cbuf[:, 0:8, :].rearrange("p (g a) c -> p g (a c)", g=4),
                    cak.rearrange("(g p) c -> p g c", p=128), A=[cbuf], sb=cbuf)
                ckview = cbuf[:, 0:8, :].rearrange("p (g a) c -> p g (a c)", g=4)
                for half in range(2):
                    ts_ = tst[cnt['tst'] % 2]
                    cnt['tst'] += 1
                    for g in range(4):
                        pt = ptr[cnt['tr'] % 2]
                        cnt['tr'] += 1

                        def mm(pt=pt, g=g, half=half):
                            ins = None
                            for j in range(4):
                                h = half * 4 + j
                                ins = nc.tensor.transpose(out=pt[:, j * 128:(j + 1) * 128],
                                                          in_=ckview[:, g, h * 128:(h + 1) * 128], identity=identb[:])
                            return ins
                        op('pe', mm, R=[cbuf, identb], W=[pt])
                        op('act', lambda pt=pt, g=g, ts_=ts_: nc.scalar.activation(
                            out=ts_[:, 0:4, g * 128:(g + 1) * 128], in_=pt[:].rearrange("p (k t) -> p k t", k=4),
                            func=AF.Copy), R=[pt], A=[ts_])
                    dma('sp', akT_s[half * 4:half * 4 + 4, :, 2560:3072].rearrange("h p t -> p h t"),
                        ts_[:, 0:4, 0:512], R=[ts_], A=[B_akT], sb=ts_)
              if _kc & 2:
                dma('pool', cbuf[:, :, :], cckv.rearrange("(g p) c -> p g c", p=128), W=[cbuf], sb=cbuf)
                for blk in range(4):
                    ts_ = tst[cnt['tst'] % 2]
                    cnt['tst'] += 1
                    for gg in range(4):
                        g = blk * 4 + gg
                        pt = ptr[cnt['tr'] % 2]
                        cnt['tr'] += 1

                        def mm(pt=pt, g=g):
                            ins = None
                            for j in range(4):
                                ins = nc.tensor.transpose(out=pt[:, j * 128:(j + 1) * 128],
                                                          in_=cbuf[:, g, j * 128:(j + 1) * 128], identity=identb[:])
                            return ins
                        op('pe', mm, R=[cbuf, identb], W=[pt])
                        op('act', lambda pt=pt, gg=gg, ts_=ts_: nc.scalar.activation(
                            out=ts_[:, 0:4, gg * 128:(gg + 1) * 128], in_=pt[:].rearrange("p (k t) -> p k t", k=4),
                            func=AF.Copy), R=[pt], A=[ts_])
                    d0 = 4096 + blk * 512
                    dma('sp', ckvT_s[0:4, :, d0:d0 + 512].rearrange("h p t -> p h t"), ts_[:, 0:4, 0:512],
                        R=[ts_], A=[B_ckvT], sb=ts_)
              if _kc & 4:
                kst = xst[0]
                kstv = kst[:, 0:1024].rearrange("p (g c) -> p g c", c=64)
                dma('sp', kstv, ckr.rearrange("(g p) c -> p g c", p=128), W=[kst], sb=kst)
                op('act', lambda: nc.scalar.activation(out=kbuf[:, :, 0:64], in_=kstv, func=AF.Copy), R=[kst], W=[kbuf])
                op('dve', lambda: nc.vector.tensor_copy(out=kbuf[:, :, 64:128], in_=kstv), R=[kst], A=[kbuf])
                for blk in range(4):
                    ts_ = tst[cnt['tst'] % 2]
                    cnt['tst'] += 1
                    pt = ptr[cnt['tr'] % 2]
                    cnt['tr'] += 1

                    def mm(pt=pt, blk=blk):
                        ins = None
                        for j in range(4):
                            ins = nc.tensor.transpose(out=pt[:, j * 128:(j + 1) * 128],
                                                      in_=kbuf[:, blk * 4 + j, :], identity=identb[:])
                        return ins
                    op('pe', mm, R=[kbuf, identb], W=[pt])
                    op('act', lambda pt=pt, ts_=ts_: nc.scalar.activation(out=ts_[:, 0, 0:512], in_=pt[:], func=AF.Copy),
                       R=[pt], W=[ts_])
                    d0 = 4096 + blk * 512
                    dma('sp', krT_s[:, d0:d0 + 512], ts_[:, 0, 0:512], R=[ts_], A=[B_krT], sb=ts_)
            kb.barrier()

        with contextlib.ExitStack() as pb:
          if stage >= 2:
              ckvT = sb("ckvT", [128, 4, NKEY_M], BF16, pb)
              krT = sb("krT", [128, NKEY_M], BF16, pb)
              wkvb_b, knT_b, Vh_b = wring[0], wring[1], wring[2]
              wkvb = wkvb_b[:].rearrange("p k c -> p (k c)").rearrange("p (k n) -> p k n", k=4)
              hsets = [(sb(f"qn_h{i}", [128, TOWN], BF16, pb), sb(f"qr_h{i}", [128, TOWN], BF16, pb),
                        sb(f"aq_h{i}", [128, TOWN], BF16, pb), sb(f"ak_h{i}", [128, NKEY_B], BF16, pb),
                        sb(f"av_h{i}", [128, 25, 128], BF16, pb), sb(f"MB_h{i}", [128, 640], F32, pb))
                       for i in range(2)]
              qn, qr, aq, ak, av, MB = hsets[0]

              def head_loads(h):
                  qn_, qr_, aq_, ak_, av_, MB_ = hsets[h % 2]
                  dma('sp', qn_[:], qnT_s[h, :, :], R=[B_qnT], W=[qn_], sb=qn_)
                  dma('sp', qr_[:], qrT_s[h // 2, :, :], R=[B_qrT], W=[qr_], sb=qr_)
                  dma('sp', aq_[:], aqT_s[h, :, :], R=[B_aqT], W=[aq_], sb=aq_)
                  dma('sp', ak_[:], akT_s[h, :, :], R=[B_akT], W=[ak_], sb=ak_)
                  dma('sp', av_[:], av_s[:, h * 128:(h + 1) * 128].rearrange("(t p) d -> p t d", p=128),
                      R=[B_av], W=[av_], sb=av_)
                  dma('sp', MB_[:], MB_s[h, :, :], R=[B_MB], W=[MB_], sb=MB_)
              knT = knT_b[:].rearrange("p k c -> p (k c)")[:, 0:NKEY_M]
              Vh = Vh_b[:].rearrange("p k c -> p (k c)")[:, 0:NKEY_M].rearrange("p (t d) -> p t d", d=128)
              PT = [sb(f"PT{i}", [128, 512], BF16, pb) for i in range(4)]
              sbias = [sb(f"sbias{i}", [128, 512], F32, pb) for i in range(3)]
              sqfr = [sb(f"sqf{i}", [128, 512], F32, pb) for i in range(2)]
              rsfr = [sb(f"rsf{i}", [128, 512], F32, pb) for i in range(2)]
              rden = sb("rden", [128, 512], F32, pb)
              obT = sb("obT_h", [128, TOWN], BF16, pb)
              oaT = sb("oaT_h", [128, TOWN], BF16, pb)
              NPS = 4
              pS = [ps(f"pS{i}", [128, 512], F32, pb) for i in range(NPS)]
              pOr = [ps(f"pO{i}", [128, 512], F32, pb) for i in range(2)]
              pDr = [ps(f"pD{i}", [128, 512], F32, pb) for i in range(2)]
              c = dict(s=0, p=0, e=0, b=0, f=0, o=0, x=0)
              pX = pS + pOr + pDr

              for cc in range(4):
                  dma('sp', ckvT[:, cc, :], ckvT_s[cc, :, :], R=[B_ckvT], A=[ckvT], sb=ckvT)
              dma('sp', krT[:], krT_s[:, :], R=[B_krT], W=[krT], sb=krT)
              dma('pool', wkvb, w_kv_b.rearrange("(k p) n -> p k n", p=128), W=[wkvb_b], sb=wkvb_b)

              def attend(nq, tiles_, q_of, out_buf, out_c0, k_stat, v_stat, kr_stat=None, hp=0):
                  prevq = []
                  n = len(tiles_)
                  lag = 3
                  pO, pD = pOr[c['o'] % 2], pDr[c['o'] % 2]
                  c['o'] += 1

                  def pv(t, P, first, last):
                      c0, c1, np_ = t['c0'], t['c1'], t['np']
                      def mm():
                          nc.tensor.matmul(pO[:, c0:c1], lhsT=v_stat(t)[0:np_, :], rhs=P[0:np_, c0:c1],
                                           start=first, stop=last)
                          return nc.tensor.matmul(pD[:, c0:c1], lhsT=onesb[0:np_, :], rhs=P[0:np_, c0:c1],
                                                  start=first, stop=last)
                      if first:
                          op('pe', mm, R=[P, onesb, Vh_b, av], W=[pO, pD])
                      else:
                          op('pe', mm, R=[P, onesb, Vh_b, av], A=[pO, pD])
                  for i, t in enumerate(tiles_):
                      c0, c1, np_ = t['c0'], t['c1'], t['np']
                      S = pS[c['s'] % NPS]
                      c['s'] += 1
                      P = PT[c['p'] % 4]
                      c['p'] += 1

                      def mm(S=S, t=t, c0=c0, c1=c1, np_=np_):
                          ops = q_of(t, c0, c1)
                          ins = None
                          for j, (l, r_) in enumerate(ops):
                              ins = nc.tensor.matmul(S[0:np_, c0:c1], lhsT=l, rhs=r_, start=(j == 0), stop=(j == len(ops) - 1))
                          return ins
                      op('pe', mm, R=[knT_b, krT, qn, qr, aq, ak], W=[S])
                      if len(prevq) >= lag:
                          pr_ = prevq.pop(0)
                          pv(pr_[0], pr_[1], pr_[2] == 0, False)
                      bias_ap = pmask[0:np_, :] if t.get('bias') == 'pm' else zerob[0:np_, :]
                      if t.get('mb0') is not None:
                          sbt = sbias[c['b'] % 3]
                          c['b'] += 1
                          m0 = t['mb0']
                          op('dve', lambda S=S, sbt=sbt, m0=m0: nc.vector.scalar_tensor_tensor(
                              out=sbt[0:np_, c0:c1], in0=S[0:np_, c0:c1], scalar=BAND_SCALE, in1=MB[0:np_, m0:m0 + (c1 - c0)],
                              op0=ALU.mult, op1=ALU.add), R=[S, MB], W=[sbt])
                          op('act', lambda sbt=sbt, P=P: nc.scalar.activation(out=P[0:np_, c0:c1], in_=sbt[0:np_, c0:c1],
                                                                            func=AF.Exp, bias=bias_ap, scale=1.0),
                             R=[sbt, pmask, zerob], W=[P])
                      else:
                          op('act', lambda S=S, P=P: nc.scalar.activation(out=P[0:np_, c0:c1], in_=S[0:np_, c0:c1],
                                                                        func=AF.Exp, bias=bias_ap, scale=MLA_SCALE),
                             R=[S, pmask, zerob], W=[P])
                      if t.get('mask') is not None:
                          which, mc = t['mask']
                          pr = slice(0, 64) if which == 'lo' else slice(64, 128)
                          op('dve', lambda P=P, pr=pr, mc=mc: nc.vector.memset(P[pr, mc:mc + 64], 0.0), W=[P])
                      prevq.append((t, P, i))
                  while prevq:
                      pr_ = prevq.pop(0)
                      pv(pr_[0], pr_[1], pr_[2] == 0, len(prevq) == 0)
                  op('dve', lambda: nc.vector.reciprocal(out=rden[:, 0:nq], in_=pD[:, 0:nq]), R=[pD], W=[rden])
                  op('dve', lambda: nc.vector.tensor_tensor(out=out_buf[:, out_c0:out_c0 + nq], in0=pO[:, 0:nq],
                                                            in1=rden[:, 0:nq], op=ALU.mult), R=[pO, rden], A=[out_buf])

              def exp_gen(hh):
                kchunks = [(kc, min(512, NKEY_M - kc)) for kc in range(0, NKEY_M, 512)]
                vgroups = [(kt0, min(4, 49 - kt0)) for kt0 in range(0, 49, 4)]

                def exp_stage1(kc, n):
                    pe_ = pS[c['s'] % NPS]
                    c['s'] += 1
                    sq_ = sqfr[c['e'] % 2]

                    def mm():
                        ins = None
                        for cc in range(4):
                            ins = nc.tensor.matmul(pe_[:, 0:n], lhsT=wkvb[:, cc, hh * 256:hh * 256 + 128],
                                                   rhs=ckvT[:, cc, kc:kc + n], start=(cc == 0), stop=(cc == 3))
                        return ins
                    op('pe', mm, R=[wkvb_b, ckvT], W=[pe_])
                    op('act', lambda: nc.scalar.activation(out=sq_[:, 0:n], in_=pe_[:, 0:n], func=AF.Square),
                       R=[pe_], W=[sq_])
                    c['e'] += 1
                    return (kc, n, pe_, sq_)

                def exp_stage2(kc, n, pe_, sq_):
                    pN = pS[c['s'] % NPS]
                    c['s'] += 1
                    rs_ = rsfr[c['f'] % 2]
                    c['f'] += 1
                    op('pe', lambda: nc.tensor.matmul(pN[:, 0:n], lhsT=onesf[:], rhs=sq_[:, 0:n], start=True, stop=True),
                       R=[onesf, sq_], W=[pN])
                    op('act', lambda: nc.scalar.activation(out=rs_[:, 0:n], in_=pN[:, 0:n], func=AF.Sqrt,
                                                           bias=epsb[:], scale=1.0 / 128), R=[pN, epsb], W=[rs_])
                    op('dve', lambda: nc.vector.reciprocal(out=rs_[:, 0:n], in_=rs_[:, 0:n]), R=[rs_], W=[rs_])
                    op('dve', lambda: nc.vector.scalar_tensor_tensor(
                        out=knT[:, kc:kc + n], in0=pe_[:, 0:n], scalar=gkn[:, 0:1], in1=rs_[:, 0:n],
                        op0=ALU.mult, op1=ALU.mult), R=[pe_, gkn, rs_], A=[knT_b])

                def v_group(kt0, nt):
                    pe_ = pS[c['s'] % NPS]
                    c['s'] += 1

                    def mm():
                        ins = None
                        for j in range(nt):
                            kt = kt0 + j
                            for cc in range(4):
                                ins = nc.tensor.matmul(pe_[:, j * 128:(j + 1) * 128], lhsT=ckvT[:, cc, kt * 128:(kt + 1) * 128],
                                                       rhs=wkvb[:, cc, hh * 256 + 128:hh * 256 + 256],
                                                       start=(cc == 0), stop=(cc == 3))
                        return ins
                    op('pe', mm, R=[wkvb_b, ckvT], W=[pe_])
                    op('act', lambda: nc.scalar.activation(
                        out=Vh[:, kt0:kt0 + nt, :], in_=pe_[:, 0:nt * 128].rearrange("p (t d) -> p t d", t=nt),
                        func=AF.Copy), R=[pe_], A=[Vh_b])
                for ii in range(len(kchunks)):
                    st1 = exp_stage1(*kchunks[ii])
                    if ii < len(vgroups):
                        v_group(*vgroups[ii])
                    exp_stage2(*st1)
                    yield

              for _ in exp_gen(0):
                  pass
              for h in range(8):
                  hp = (h % 2) * 64
                  if h == 0:
                      head_loads(0)
                  qn, qr, aq, ak, av, MB = hsets[h % 2]
                  if h + 1 < 8:
                      head_loads(h + 1)
                  def q_mla(qbase):
                      def f(t, c0, c1):
                          kidx, np_ = t['kidx'], t['np']
                          return [(knT[:, kidx:kidx + np_], qn[:, qbase + c0:qbase + c1]),
                                  (krT[hp:hp + 64, kidx:kidx + np_], qr[hp:hp + 64, qbase + c0:qbase + c1])]
                      return f
                  vm = lambda t: Vh[:, t['kidx'] // 128, :]
                  for j in range(4):
                      tl = [dict(kidx=kt * 128, np=128, c0=0, c1=512, bias='pm') for kt in range(16)]
                      for kt in range(4 * j + 4):
                          if kt < 4 * j:
                              tl.append(dict(kidx=2048 + kt * 128, np=128, c0=0, c1=512))
                          else:
                              m = kt - 4 * j
                              tl.append(dict(kidx=2048 + kt * 128, np=128, c0=128 * m, c1=512, mask=('hi', 128 * m)))
                      attend(512, tl, q_mla(512 * j), obT, 512 * j, None, vm)
                  tl = [dict(kidx=4096 + kt * 128, np=128, c0=0, c1=64) for kt in range(16)]
                  tl.append(dict(kidx=6144, np=64, c0=0, c1=64))
                  attend(64, tl, q_mla(2048), obT, 2048, None, vm)

                  eg = exp_gen(h + 1) if h + 1 < 8 else iter(())

                  def q_band(qbase):
                      def f(t, c0, c1):
                          kidx, np_ = t['kidx'], t['np']
                          return [(ak[:, kidx:kidx + np_], aq[:, qbase + c0:qbase + c1])]
                      return f
                  va = lambda t: av[:, t['kidx'] // 128, :]
                  for j in range(4):
                      q0 = 512 * j
                      tl = []
                      for t_ in [3, 0, 1, 2, 4, 5, 6, 7]:
                          ki0 = q0 + 128 * t_
                          if t_ <= 3:
                              c0, c1 = 0, 128 * (t_ + 1)
                              mask = ('lo', c1 - 64)
                          else:
                              c0, c1 = 128 * (t_ - 4), 512
                              mask = ('hi', c0)
                          d = dict(kidx=ki0, np=128, c0=c0, c1=c1, mask=mask, mb0=c0 + 512 - 128 * t_)
                          if j == 0 and t_ <= 3:
                              d['bias'] = 'pm'
                          tl.append(d)
                      for _ in range(3):
                          next(eg, None)
                      attend(512, tl, q_band(q0), oaT, q0, None, va)
                  for _ in range(3):
                      next(eg, None)
                  tl = [dict(kidx=2560 + 128 * t_, np=128, c0=0, c1=64, mb0=512 - 128 * t_) for t_ in range(4)]
                  tl.append(dict(kidx=3072, np=64, c0=0, c1=64, mb0=0))
                  attend(64, tl, q_band(2048), oaT, 2048, None, va)
                  for _ in eg:
                      pass
                  op('dve', lambda: nc.vector.memset(obT[:, 2112:TOWN], 0.0), A=[obT])
                  op('dve', lambda: nc.vector.memset(oaT[:, 2112:TOWN], 0.0), A=[oaT])
                  dma('sp', obT_s[h, :, :], obT[:], R=[obT], A=[B_obT], sb=obT)
                  dma('sp', oaT_s[h, :, :], oaT[:], R=[oaT], A=[B_oaT], sb=oaT)
              kb.barrier()

        with contextlib.ExitStack() as pc:
          if stage >= 3:
              TC = 768
              acc = sb("acc", [128, 6, D], F32, pc)
              xnT = sb("xnTc", [128, 16, TC], BF16, pc)
              oaTt = sb("oaTt", [128, 8, TC], BF16, pc)
              obTt = sb("obTt", [128, 8, TC], BF16, pc)
              mixT = sb("mixT", [128, 16, TC], BF16, pc)
              uTb = [oaTt, obTt]
              fs = dict(i=0, j=0,
                        xnb=[sb("xnbc0", [128, D], BF16, pc)] * 2,
                        ss=[sb(f"fssc{i}", [128, 4], F32, pc) for i in range(2)],
                        pst=[ps(f"pstc{i}", [128, 1024], BF16, pc) for i in range(2)])
              sg = [sb(f"sg{i}", [128, 512], F32, pc) for i in range(4)]
              gtmp = sb("gtmp", [128, 4, TC], F32, pc)
              pq = [ps(f"pq{i}", [128, 512], F32, pc) for i in range(6)]
              cq = dict(p=0, s=0, u=0)
              ctiles = [list(range(0, 6)), list(range(6, 12)), list(range(12, 17))]
              accg = [Buf(f"accg{i}") for i in range(6)]

              def segs(T):
                  return [(s0, min(512, T - s0)) for s0 in range(0, T, 512)]

              def nextp():
                  p = pq[cq['p'] % 6]
                  cq['p'] += 1
                  return p

              def csched():
                  L = []
                  for fb in range(4):
                      L.append(('ga', fb)); L.append(('pa', fb)); L.append(('gb', fb)); L.append(('pb', fb))
                  for cb in range(4):
                      L.append(('wo', cb))
                  for f in range(16):
                      L.append(('up', f)); L.append(('dn', f))
                  return L
              sched = [(ti, k, i) for ti in range(len(ctiles)) for (k, i) in csched()]
              loaded = {}

              def issue(idx):
                  if idx >= len(sched):
                      return
                  ti, k, i = sched[idx]
                  if k == 'ga':
                      loaded[idx] = wload(wsrc(w_in, 0, 16, 5184 + 512 * i, 512), 16, 512)
                  elif k == 'gb':
                      loaded[idx] = wload(wsrc(w_in, 0, 16, 7232 + 512 * i, 512), 16, 512)
                  elif k == 'pa':
                      loaded[idx] = wload(wsrc(w_pa, 0, 8, 512 * i, 512), 8, 512)
                  elif k == 'pb':
                      loaded[idx] = wload(wsrc(w_pb, 0, 8, 512 * i, 512), 8, 512)
                  elif k == 'wo':
                      loaded[idx] = wload(wsrc(w_out, 0, 16, 512 * i, 512), 16, 512)
                  elif k == 'up':
                      loaded[idx] = wload(wsrc(w_up, 0, 16, 512 * i, 512), 16, 512)
                  else:
                      loaded[idx] = wload(wsrc(w_down, 512 * i, 4, 0, 2048), 4, 2048,
                                          view=lambda slot: slot[:].rearrange("p k c -> p (k c)").rearrange("p (k n) -> p k n", k=4))
              freeslots = list(wring)
              nxt = [0]
              sidx = [0]

              def pump():
                  while nxt[0] < len(sched) and freeslots:
                      wstate['force'] = freeslots.pop(0)
                      issue(nxt[0])
                      nxt[0] += 1
                  wstate['force'] = None

              def getw():
                  s = loaded.pop(sidx[0])
                  sidx[0] += 1
                  return s

              def release(slot):
                  freeslots.append(slot)
                  pump()
              pump()

              for ti, groups in enumerate(ctiles):
                  G = len(groups)
                  T = G * 128
                  sg_ = segs(T)
                  load_Gbig(g_mix)
                  for li, g in enumerate(groups):
                      dma('sp', acc[:, li, :], xo[g * 128:(g + 1) * 128, :], W=[accg[li]], sb=accg[li])
                  t0 = groups[0] * 128
                  dma('sp', oaTt[:, :, 0:T], oaT_s[:, :, t0:t0 + T].rearrange("h p t -> p h t"), R=[B_oaT], W=[oaTt], sb=oaTt)
                  dma('sp', obTt[:, :, 0:T], obT_s[:, :, t0:t0 + T].rearrange("h p t -> p h t"), R=[B_obT], W=[obTt], sb=obTt)
                  for li, g in enumerate(groups):
                      frontend(fs, accg[li], acc[:, li, :], xnT, li, T)
                  for fb in range(4):
                      for half_ in range(2):
                          wg_, wp_ = getw(), getw()
                          actp = oaTt if half_ == 0 else obTt
                          for cc in range(4):
                              fc = fb * 4 + cc
                              for (s0, sn) in sg_:
                                  pg_, pp_ = nextp(), nextp()
                                  for (pp, ws, act_, nk) in ((pg_, wg_, xnT, 16), (pp_, wp_, actp, 8)):
                                      def mm(pp=pp, ws=ws, act_=act_, nk=nk):
                                          ins = None
                                          for k in range(nk):
                                              ins = nc.tensor.matmul(pp[:, 0:sn], lhsT=ws[:, k, cc * 128:(cc + 1) * 128],
                                                                     rhs=act_[:, k, s0:s0 + sn], start=(k == 0), stop=(k == nk - 1))
                                          return ins
                                      op('pe', mm, R=[ws, act_], W=[pp])
                                  s1 = sg[cq['s'] % 4]
                                  cq['s'] += 1
                                  op('act', lambda: nc.scalar.activation(out=s1[:, 0:sn], in_=pg_[:, 0:sn], func=AF.Sigmoid),
                                     R=[pg_], W=[s1])
                                  if half_ == 0:
                                      op('dve', lambda: nc.vector.tensor_tensor(out=gtmp[:, cc, s0:s0 + sn], in0=s1[:, 0:sn],
                                                                                in1=pp_[:, 0:sn], op=ALU.mult),
                                         R=[pp_, s1], A=[gtmp])
                                  else:
                                      op('dve', lambda: nc.vector.tensor_tensor(out=s1[:, 0:sn], in0=s1[:, 0:sn], in1=pp_[:, 0:sn],
                                                                                op=ALU.mult), R=[pp_, s1], W=[s1])
                                      op('dve', lambda: nc.vector.tensor_tensor(out=mixT[:, fc, s0:s0 + sn], in0=s1[:, 0:sn],
                                                                                in1=gtmp[:, cc, s0:s0 + sn], op=ALU.add),
                                         R=[s1, gtmp], A=[mixT])
                          release(wg_)
                          release(wp_)
                  for cb in range(4):
                      wo = getw()
                      for li in range(G):
                          pp = nextp()

                          def mm(pp=pp, li=li, wo=wo):
                              ins = None
                              for k in range(16):
                                  ins = nc.tensor.matmul(pp[:], lhsT=mixT[:, k, li * 128:(li + 1) * 128], rhs=wo[:, k, :],
                                                         start=(k == 0), stop=(k == 15))
                              return ins
                          op('pe', mm, R=[mixT, wo], W=[pp])
                          a_ = acc[:, li, cb * 512:(cb + 1) * 512]
                          op('dve', lambda a_=a_, pp=pp: nc.vector.tensor_tensor(out=a_, in0=a_, in1=pp[:], op=ALU.add),
                             R=[pp], W=[accg[li]])
                      release(wo)
                  load_Gbig(g_ffn)
                  for li in range(G):
                      frontend(fs, accg[li], acc[:, li, :], xnT, li, T)
                  def ffn_up(wu, u_b):
                      u_ = u_b[:, 0:4, :]
                      for cc in range(4):
                          for (s0, sn) in sg_:
                              pp = nextp()

                              def mm():
                                  ins = None
                                  for k in range(16):
                                      ins = nc.tensor.matmul(pp[:, 0:sn], lhsT=wu[:, k, cc * 128:(cc + 1) * 128],
                                                             rhs=xnT[:, k, s0:s0 + sn], start=(k == 0), stop=(k == 15))
                                  return ins
                              op('pe', mm, R=[wu, xnT], W=[pp])
                              s1 = sg[cq['s'] % 4]
                              cq['s'] += 1
                              op('act', lambda: nc.scalar.activation(out=s1[:, 0:sn], in_=pp[:, 0:sn], func=AF.Relu),
                                 R=[pp], W=[s1])
                              op('dve', lambda: nc.vector.tensor_tensor(out=u_[:, cc, s0:s0 + sn], in0=s1[:, 0:sn],
                                                                        in1=s1[:, 0:sn], op=ALU.mult), R=[s1], A=[u_b])

                  def ffn_down(wd, u_b, last):
                      u_ = u_b[:, 0:4, :]
                      wdv = wd[:].rearrange("p k c -> p (k c)").rearrange("p (k n) -> p k n", k=4)
                      for li in range(G):
                          for cb in range(4):
                              pp = nextp()

                              def mm():
                                  ins = None
                                  for k in range(4):
                                      ins = nc.tensor.matmul(pp[:], lhsT=u_[:, k, li * 128:(li + 1) * 128],
                                                             rhs=wdv[:, k, cb * 512:(cb + 1) * 512], start=(k == 0), stop=(k == 3))
                                  return ins
                              op('pe', mm, R=[u_b, wd], W=[pp])
                              a_ = acc[:, li, cb * 512:(cb + 1) * 512]
                              op('dve', lambda: nc.vector.tensor_tensor(out=a_, in0=a_, in1=pp[:], op=ALU.add),
                                 R=[pp], W=[accg[li]])
                          if last:
                              g = groups[li]
                              dma('sp', y_d[g * 128:(g + 1) * 128, :], acc[:, li, :], R=[accg[li]], A=[B_out], sb=accg[li])

                  wu0 = getw()
                  ffn_up(wu0, uTb[0])
                  release(wu0)
                  for f in range(16):
                      wd = getw()
                      if f + 1 < 16:
                          wun = getw()
                          ffn_up(wun, uTb[(f + 1) % 2])
                          release(wun)
                      ffn_down(wd, uTb[f % 2], f == 15)
                      release(wd)
              kb.barrier()
        kb.barrier(engines=['sp'])
    return nc


_CACHE = {}


def _rope_tables(pos):
    half = 32
    freqs = (np.float32(10000.0) ** (-(np.arange(half, dtype=np.float32) / np.float32(half)))).astype(np.float32)
    ang = pos.astype(np.float32)[:, None] * freqs[None, :]
    return np.cos(ang).astype(np.float32), np.sin(ang).astype(np.float32)


def kernel(x_prompt, x_sample, cache_a_k, cache_a_v, cache_mla_ckv, cache_mla_krope,
           norm_mix_g, w_in, g_aq, g_ak, rel_bias, g_kv, g_kr, g_qn, g_qr, g_kn,
           w_kv_b, w_pa, w_pb, w_out, norm_ffn_g, w_up, w_down):
    import os
    stage = float(os.environ.get("KSTAGE", "99"))
    if 'nc' not in _CACHE:
        _CACHE['nc'] = build_program(stage)
    nc = _CACHE['nc']
    in_maps = _prep(x_prompt, x_sample, cache_a_k, cache_a_v, cache_mla_ckv, cache_mla_krope,
                    norm_mix_g, w_in, g_aq, g_ak, rel_bias, g_kv, g_kr, g_qn, g_qr, g_kn,
                    w_kv_b, w_pa, w_pb, w_out, norm_ffn_g, w_up, w_down)
    ncores = int(os.environ.get("KCORES", "8"))
    res = run_bass_kernel_spmd(nc, in_maps[:ncores], core_ids=list(range(ncores)))
    R = list(res.results) + [res.results[0]] * (8 - ncores)
    return _assemble(R)


def _prep(x_prompt, x_sample, cache_a_k, cache_a_v, cache_mla_ckv, cache_mla_krope,
          norm_mix_g, w_in, g_aq, g_ak, rel_bias, g_kv, g_kr, g_qn, g_qr, g_kn,
          w_kv_b, w_pa, w_pb, w_out, norm_ffn_g, w_up, w_down):
    f = lambda a: np.ascontiguousarray(np.asarray(a, dtype=np.float32))
    x_prompt, x_sample = f(x_prompt), f(x_sample)
    idx = np.clip(np.arange(768) - 127, -63, 256) + 63
    tabext = f(np.asarray(rel_bias)[0][:, idx])
    shared = dict(
        w_in=f(w_in[0]), w_kv_b=f(w_kv_b[0]), w_pa=f(w_pa[0]), w_pb=f(w_pb[0]), w_out=f(w_out[0]),
        w_up=f(w_up[0]), w_down=f(w_down[0]),
        g_mix=f(norm_mix_g[0]).reshape(1, D), g_ffn=f(norm_ffn_g[0]).reshape(1, D),
        g_aq=f(g_aq[0]).reshape(1, 128), g_ak=f(g_ak[0]).reshape(1, 128), g_kv=f(g_kv[0]).reshape(1, 512),
        g_kr=f(g_kr[0]).reshape(1, 64), g_qn=f(g_qn[0]).reshape(1, 128), g_qr=f(g_qr[0]).reshape(1, 64),
        g_kn=f(g_kn[0]).reshape(128, 1), tabext=tabext,
        identb=np.eye(128, dtype=np.float32), identf=np.eye(128, dtype=np.float32),
    )
    cos_all, sin_all = _rope_tables(np.arange(4096))
    in_maps = []
    for c in range(8):
        b, half = c // 2, c % 2
        xo = np.zeros((TOWN, D), np.float32)
        xo[0:2048] = x_prompt[b, half * 2048:(half + 1) * 2048]
        xo[2048:2112] = x_sample[c]
        xp = np.zeros((NPRE, D), np.float32)
        if half == 1:
            xp[:] = x_prompt[b, 0:2048]
        cos_o = np.zeros((TOWN, 32), np.float32)
        sin_o = np.zeros((TOWN, 32), np.float32)
        cos_o[0:2048] = cos_all[half * 2048:(half + 1) * 2048]
        sin_o[0:2048] = sin_all[half * 2048:(half + 1) * 2048]
        cos_o[2048:2112] = cos_all[2048:2112]
        sin_o[2048:2112] = sin_all[2048:2112]
        m = dict(shared)
        m.update(
            xo=xo, xp=xp,
            cak=f(cache_a_k[0, c]).reshape(512, 1024), cav=f(cache_a_v[0, c]).reshape(512, 1024),
            cckv=f(cache_mla_ckv[0, c]), ckr=f(cache_mla_krope[0, c]),
            cos_o=cos_o, sin_o=sin_o, cos_p=cos_all[0:2048].copy(), sin_p=sin_all[0:2048].copy(),
            pmask=np.full((128, 1), 0.0 if half == 1 else NEG, np.float32),
        )
        in_maps.append(m)
    return in_maps


def _assemble(R):
    yp = np.zeros((4, 4096, D), np.float32)
    ys = np.zeros((8, 64, D), np.float32)
    akp = np.zeros((1, 4, 512, 8, 128), np.float32)
    avp = np.zeros((1, 4, 512, 8, 128), np.float32)
    ckp = np.zeros((1, 4, 4096, 512), np.float32)
    krp = np.zeros((1, 4, 4096, 64), np.float32)
    aks = np.zeros((1, 8, 64, 8, 128), np.float32)
    avs = np.zeros((1, 8, 64, 8, 128), np.float32)
    cks = np.zeros((1, 8, 64, 512), np.float32)
    krs = np.zeros((1, 8, 64, 64), np.float32)
    for c in range(8):
        b, half = c // 2, c % 2
        r = R[c]
        yp[b, half * 2048:(half + 1) * 2048] = r["y"][0:2048]
        ys[c] = r["y"][2048:2112]
        ckp[0, b, half * 2048:(half + 1) * 2048] = r["o_ckv"][0:2048]
        krp[0, b, half * 2048:(half + 1) * 2048] = r["o_kr"][0:2048]
        cks[0, c] = r["o_ckv"][2048:2112]
        krs[0, c] = r["o_kr"][2048:2112]
        if half == 1:
            akp[0, b] = r["o_ak"][0:512].reshape(512, 8, 128)
            avp[0, b] = r["o_av"][0:512].reshape(512, 8, 128)
        aks[0, c] = r["o_ak"][512:576].reshape(64, 8, 128)
        avs[0, c] = r["o_av"][512:576].reshape(64, 8, 128)
    return (yp, ys, akp, avp, ckp, krp, aks, avs, cks, krs)
```
